# Optimizing a Trainium2 kernel written in Bass

```python
import jax, jax.numpy as jnp
from jax import lax
import numpy as np

D_MODEL = 1024
BATCH = 32
SEQ = 2048
DEPTH = 1

HEAD_DIM = 64
N_ATTN_HEADS = 12
D_ATTN = N_ATTN_HEADS * HEAD_DIM
DILATED_BRANCHES = ((128, 1), (512, 4), (2048, 16))
ATTN_BLOCK = 128
ROPE_THETA = 500000.0
ROPE_DIM = HEAD_DIM // 4
N_SSD_HEADS = 12
SSD_HEAD_DIM = 64
D_SSD = N_SSD_HEADS * SSD_HEAD_DIM
SSD_GROUPS = 4
SSD_HEADS_PER_GROUP = N_SSD_HEADS // SSD_GROUPS
SSD_STATE = 128
CONV_WIDTH = 4
SSD_CHUNK = 128
D_CONV = D_SSD + 2 * SSD_GROUPS * SSD_STATE
D_MIX = D_ATTN + D_SSD
D_IN_PROJ = 3 * D_ATTN + D_SSD + D_CONV + N_SSD_HEADS
D_FF = ((8 * D_MODEL // 3 + 255) // 256) * 256
ALPHA = (2.0 * DEPTH) ** 0.25
BETA = (8.0 * DEPTH) ** -0.25
LN_EPS = 1e-5
RMS_EPS = 1e-6

kernel_name = 'hybrid_ssd_dilated_attn_macaron_deepnorm'


def layer_norm(t, g, b):
    tf = t.astype(jnp.float32)
    mu = jnp.mean(tf, axis=-1, keepdims=True)
    var = jnp.mean(jnp.square(tf - mu), axis=-1, keepdims=True)
    return ((tf - mu) * lax.rsqrt(var + LN_EPS) * g + b).astype(t.dtype)


def rms_norm(t, w):
    tf = t.astype(jnp.float32)
    return (tf * lax.rsqrt(jnp.mean(tf * tf, axis=-1, keepdims=True) + RMS_EPS) * w).astype(t.dtype)


def swiglu(t, w_gate, w_up, w_down):
    return (jax.nn.silu(t @ w_gate) * (t @ w_up)) @ w_down


def rotary_tables(positions):
    inv_freq = ROPE_THETA ** (-jnp.arange(0, ROPE_DIM, 2, dtype=jnp.float32) / ROPE_DIM)
    ang = positions.astype(jnp.float32)[..., None] * inv_freq
    return jnp.cos(ang)[:, :, None, :], jnp.sin(ang)[:, :, None, :]


def partial_rope(t, cos, sin):
    half = ROPE_DIM // 2
    cos = cos.astype(t.dtype)
    sin = sin.astype(t.dtype)
    t1 = t[..., :half]
    t2 = t[..., half:ROPE_DIM]
    return jnp.concatenate([t1 * cos - t2 * sin, t2 * cos + t1 * sin, t[..., ROPE_DIM:]], axis=-1)


def banded_causal_attention(q, k, v, wr):
    n, L, h, dh = q.shape
    nblk = -(-L // ATTN_BLOCK)
    lp = nblk * ATTN_BLOCK
    qp = jnp.pad(q, ((0, 0), (0, lp - L), (0, 0), (0, 0)))
    kp = jnp.pad(k, ((0, 0), (wr, lp - L), (0, 0), (0, 0)))
    vp = jnp.pad(v, ((0, 0), (wr, lp - L), (0, 0), (0, 0)))
    scale = dh ** -0.5
    q_off = jnp.arange(ATTN_BLOCK)
    k_off = jnp.arange(ATTN_BLOCK + wr)
    rel = q_off[:, None] + wr - k_off[None, :]
    band = (rel >= 0) & (rel <= wr)

    def one_block(i):
        start = i * ATTN_BLOCK
        qb = lax.dynamic_slice_in_dim(qp, start, ATTN_BLOCK, axis=1)
        kb = lax.dynamic_slice_in_dim(kp, start, ATTN_BLOCK + wr, axis=1)
        vb = lax.dynamic_slice_in_dim(vp, start, ATTN_BLOCK + wr, axis=1)
        m_k = start - wr + k_off
        mask = band & (m_k >= 0)[None, :]
        s = jnp.einsum('nqhd,nkhd->nhqk', qb, kb).astype(jnp.float32) * scale
        s = jnp.where(mask, s, -jnp.inf)
        lse = jax.nn.logsumexp(s, axis=-1)
        p = jnp.exp(s - lse[..., None])
        o = jnp.einsum('nhqk,nkhd->nqhd', p, vb.astype(jnp.float32))
        return o, jnp.transpose(lse, (0, 2, 1))

    o, lse = lax.map(one_block, jnp.arange(nblk))
    o = jnp.transpose(o, (1, 0, 2, 3, 4)).reshape(n, lp, h, dh)[:, :L]
    lse = jnp.transpose(lse, (1, 0, 2, 3)).reshape(n, lp, h)[:, :L]
    return o, lse


def dilated_branch(q, k, v, window, dilation):
    b, s, h, dh = q.shape
    L = s // dilation

    def to_residue(t):
        return jnp.transpose(t.reshape(b, L, dilation, h, dh), (0, 2, 1, 3, 4)).reshape(b * dilation, L, h, dh)

    o, lse = banded_causal_attention(to_residue(q), to_residue(k), to_residue(v), window // dilation)
    o = jnp.transpose(o.reshape(b, dilation, L, h, dh), (0, 2, 1, 3, 4)).reshape(b, s, h, dh)
    lse = jnp.transpose(lse.reshape(b, dilation, L, h), (0, 2, 1, 3)).reshape(b, s, h)
    return o, lse


def dilated_attention_mixture(q, k, v):
    outs, lses = [], []
    for window, dilation in DILATED_BRANCHES:
        o, lse = dilated_branch(q, k, v, window, dilation)
        outs.append(o)
        lses.append(lse)
    w = jax.nn.softmax(jnp.stack(lses, axis=0), axis=0)
    return jnp.einsum('gbsh,gbshd->bshd', w, jnp.stack(outs, axis=0))


def causal_depthwise_conv(u, w, bias):
    c = u.shape[-1]
    out = lax.conv_general_dilated(u, w[:, None, :].astype(u.dtype), window_strides=(1,),
                                   padding=((CONV_WIDTH - 1, 0),),
                                   dimension_numbers=('NWC', 'WIO', 'NWC'),
                                   feature_group_count=c)
    return out + bias


def ssd_chunked(xdt, dA, Bm, Cm):
    b, s, g, j, p = xdt.shape
    n = Bm.shape[-1]
    nc = s // SSD_CHUNK
    cl = SSD_CHUNK
    xdt = xdt.astype(jnp.float32).reshape(b, nc, cl, g, j, p)
    Bc = Bm.astype(jnp.float32).reshape(b, nc, cl, g, n)
    Cc = Cm.astype(jnp.float32).reshape(b, nc, cl, g, n)
    a = jnp.transpose(dA.reshape(b, nc, cl, g, j), (0, 3, 4, 1, 2))
    a_cum = jnp.cumsum(a, axis=-1)
    tri = jnp.tril(jnp.ones((cl, cl), dtype=bool))
    seg = a_cum[..., :, None] - a_cum[..., None, :]
    Lmat = jnp.exp(jnp.where(tri, seg, -jnp.inf))
    CB = jnp.einsum('bclgn,bcsgn->bgcls', Cc, Bc)
    y_diag = jnp.einsum('bgcls,bgjcls,bcsgjp->bclgjp', CB, Lmat, xdt)
    decay_states = jnp.exp(a_cum[..., -1:] - a_cum)
    states = jnp.einsum('bclgn,bgjcl,bclgjp->bcgjpn', Bc, decay_states, xdt)
    chunk_decay = jnp.exp(a_cum[..., -1])

    def step(h, inp):
        st, dec = inp
        return dec[..., None, None] * h + st, h

    h0 = jnp.zeros((b, g, j, p, n), jnp.float32)
    _, prev = lax.scan(step, h0, (jnp.moveaxis(states, 1, 0), jnp.moveaxis(chunk_decay, -1, 0)))
    prev = jnp.moveaxis(prev, 0, 1)
    y_off = jnp.einsum('bclgn,bcgjpn,bgjcl->bclgjp', Cc, prev, jnp.exp(a_cum))
    return (y_diag + y_off).reshape(b, s, g, j, p)


def hybrid_mixer(h, cos, sin, w_in, conv_w, conv_b, dt_bias, a_log, d_skip, attn_norm_w, ssd_norm_w, w_out):
    b, s, _ = h.shape
    proj = h @ w_in
    cuts = [D_ATTN, 2 * D_ATTN, 3 * D_ATTN, 3 * D_ATTN + D_SSD, 3 * D_ATTN + D_SSD + D_CONV]
    q, k, v, z, xbc, dt = jnp.split(proj, cuts, axis=-1)
    q = partial_rope(q.reshape(b, s, N_ATTN_HEADS, HEAD_DIM), cos, sin)
    k = partial_rope(k.reshape(b, s, N_ATTN_HEADS, HEAD_DIM), cos, sin)
    v = v.reshape(b, s, N_ATTN_HEADS, HEAD_DIM)
    attn = dilated_attention_mixture(q, k, v).astype(h.dtype).reshape(b, s, D_ATTN)
    attn = rms_norm(attn, attn_norm_w)
    xbc = jax.nn.silu(causal_depthwise_conv(xbc, conv_w, conv_b))
    xs, Bm, Cm = jnp.split(xbc, [D_SSD, D_SSD + SSD_GROUPS * SSD_STATE], axis=-1)
    xs = xs.reshape(b, s, SSD_GROUPS, SSD_HEADS_PER_GROUP, SSD_HEAD_DIM)
    Bm = Bm.reshape(b, s, SSD_GROUPS, SSD_STATE)
    Cm = Cm.reshape(b, s, SSD_GROUPS, SSD_STATE)
    dt = jax.nn.softplus(dt.astype(jnp.float32) + dt_bias).reshape(b, s, SSD_GROUPS, SSD_HEADS_PER_GROUP)
    A = -jnp.exp(a_log.astype(jnp.float32)).reshape(SSD_GROUPS, SSD_HEADS_PER_GROUP)
    y = ssd_chunked(xs.astype(jnp.float32) * dt[..., None], dt * A, Bm, Cm)
    y = y + d_skip.reshape(SSD_GROUPS, SSD_HEADS_PER_GROUP)[..., None] * xs
    y = y.astype(h.dtype).reshape(b, s, D_SSD)
    y = rms_norm(y * jax.nn.silu(z), ssd_norm_w)
    return jnp.concatenate([attn, y], axis=-1) @ w_out


def setup_inputs(seed: int = 0) -> dict:
    key = jax.random.key(seed)
    ks = jax.random.split(key, 32)
    f32 = jnp.float32
    nrm = lambda k, shape, std: jax.random.normal(k, shape, f32) * std
    x = jax.random.normal(ks[0], (BATCH, SEQ, D_MODEL), f32)
    positions = (jnp.arange(SEQ, dtype=jnp.int32)[None, :]
                 + jax.random.randint(ks[1], (BATCH, 1), 0, 4096, dtype=jnp.int32))
    col_scale = jnp.concatenate([jnp.ones((2 * D_ATTN,), f32), jnp.full((D_ATTN,), BETA, f32),
                                 jnp.ones((D_SSD + D_CONV,), f32), jnp.full((N_SSD_HEADS,), 0.1, f32)])
    w_in = nrm(ks[2], (DEPTH, D_MODEL, D_IN_PROJ), D_MODEL ** -0.5) * col_scale
    conv_w = nrm(ks[3], (DEPTH, CONV_WIDTH, D_CONV), CONV_WIDTH ** -0.5)
    conv_b = nrm(ks[4], (DEPTH, D_CONV), 0.01)
    dt0 = jnp.exp(jax.random.uniform(ks[5], (DEPTH, N_SSD_HEADS), f32, np.log(1e-3), np.log(1e-1)))
    dt_bias = dt0 + jnp.log(-jnp.expm1(-dt0))
    a_log = jnp.log(jax.random.uniform(ks[6], (DEPTH, N_SSD_HEADS), f32, 1.0, 16.0))
    d_skip = 1.0 + nrm(ks[7], (DEPTH, N_SSD_HEADS), 0.01)
    attn_norm_w = 1.0 + nrm(ks[8], (DEPTH, D_ATTN), 0.01)
    ssd_norm_w = 1.0 + nrm(ks[9], (DEPTH, D_SSD), 0.01)
    w_out = nrm(ks[10], (DEPTH, D_MIX, D_MODEL), BETA * D_MIX ** -0.5)

    def ffn(k0, k1, k2):
        return (nrm(k0, (DEPTH, D_MODEL, D_FF), D_MODEL ** -0.5),
                nrm(k1, (DEPTH, D_MODEL, D_FF), BETA * D_MODEL ** -0.5),
                nrm(k2, (DEPTH, D_FF, D_MODEL), BETA * D_FF ** -0.5))

    ffn1_gate, ffn1_up, ffn1_down = ffn(ks[11], ks[12], ks[13])
    ffn2_gate, ffn2_up, ffn2_down = ffn(ks[14], ks[15], ks[16])
    gain = lambda k: 1.0 + nrm(k, (DEPTH, D_MODEL), 0.01)
    bias = lambda k: nrm(k, (DEPTH, D_MODEL), 0.01)
    return {'x': x, 'positions': positions,
            'ln1_g': gain(ks[17]), 'ln1_b': bias(ks[18]),
            'ffn1_gate': ffn1_gate, 'ffn1_up': ffn1_up, 'ffn1_down': ffn1_down,
            'w_in': w_in, 'conv_w': conv_w, 'conv_b': conv_b, 'dt_bias': dt_bias, 'a_log': a_log,
            'd_skip': d_skip, 'attn_norm_w': attn_norm_w, 'ssd_norm_w': ssd_norm_w, 'w_out': w_out,
            'ln2_g': gain(ks[19]), 'ln2_b': bias(ks[20]),
            'ffn2_gate': ffn2_gate, 'ffn2_up': ffn2_up, 'ffn2_down': ffn2_down,
            'ln3_g': gain(ks[21]), 'ln3_b': bias(ks[22])}


def reference(x, positions, ln1_g, ln1_b, ffn1_gate, ffn1_up, ffn1_down, w_in, conv_w, conv_b,
              dt_bias, a_log, d_skip, attn_norm_w, ssd_norm_w, w_out, ln2_g, ln2_b,
              ffn2_gate, ffn2_up, ffn2_down, ln3_g, ln3_b):
    cos, sin = rotary_tables(positions)
    h = x
    for l in range(DEPTH):
        h = layer_norm(ALPHA * h + 0.5 * swiglu(h, ffn1_gate[l], ffn1_up[l], ffn1_down[l]), ln1_g[l], ln1_b[l])
        mix = hybrid_mixer(h, cos, sin, w_in[l], conv_w[l], conv_b[l], dt_bias[l], a_log[l], d_skip[l],
                           attn_norm_w[l], ssd_norm_w[l], w_out[l])
        h = layer_norm(ALPHA * h + mix, ln2_g[l], ln2_b[l])
        h = layer_norm(ALPHA * h + 0.5 * swiglu(h, ffn2_gate[l], ffn2_up[l], ffn2_down[l]), ln3_g[l], ln3_b[l])
    return h
```

```python
import contextlib
import numpy as np
import ml_dtypes
import concourse.bass as bass
import concourse.mybir as mybir
from concourse.bass_utils import run_bass_kernel_spmd

F32 = mybir.dt.float32
BF16 = mybir.dt.bfloat16
I32 = mybir.dt.int32
AF = mybir.ActivationFunctionType
ALU = mybir.AluOpType

D = 1024
DFF = 2816
NFC = 22
SEQ = 2048
H = 12
HD = 64
NG = 4
DIN = 4876
DMIX = 1536
TU = 512
NT = 4
UPS = SEQ // TU
NCORE = 8
ALPHA = float(2.0 ** 0.25)
LN_EPS = 1e-5
RMS_EPS = 1e-6
NEG = -30000.0
MSTART = {0: 5, 1: 1, 2: 0}

SEG_QKV = [(0, 512), (512, 512), (1024, 512), (1536, 512), (2048, 256)]
SEG_Z = [(2304, 512), (2816, 256)]
SEG_X = [(3072, 512), (3584, 512), (4096, 512), (4608, 256)]
COL_DT = 4864

ARENA = 110592
BLK = 256


class _StopNow(Exception):
    pass


class Trk:
    __slots__ = ("w", "r", "excl")

    def __init__(self, excl=False):
        self.w = {}
        self.r = {}
        self.excl = excl


class Chan:
    __slots__ = ("sem", "n")

    def __init__(self, sem):
        self.sem = sem
        self.n = 0


class Buf:
    __slots__ = ("ap", "trks", "chan", "chan2")

    def __init__(self, ap, trks, chan=None, chan2=None):
        self.ap = ap
        self.trks = trks
        self.chan = chan
        self.chan2 = chan2


def _trks(lst):
    out = []
    for b in lst:
        if isinstance(b, Trk):
            out.append(b)
        else:
            out.extend(b.trks)
    return out


class Ctx:
    def __init__(self, nc, es):
        self.nc = nc
        self.es = es
        self.eng = {}
        for name, h in (("pe", nc.tensor), ("act", nc.scalar), ("dve", nc.vector),
                        ("pool", nc.gpsimd), ("sp", nc.sync)):
            sem = es.enter_context(nc.semaphore("sem_" + name))
            self.eng[name] = {"h": h, "sem": sem, "cnt": 0, "known": {}}
        self.nchan = 0
        self.chans = []

    def chan(self):
        self.nchan += 1
        c = Chan(self.es.enter_context(self.nc.semaphore("dch%d" % self.nchan)))
        self.chans.append(c)
        return c

    def _need(self, e, reads, writes):
        need = {}

        def add(d, skip_self):
            for k, (sem, thr) in d.items():
                if skip_self and sem is e["sem"]:
                    continue
                if need.get(k, (None, 0))[1] < thr:
                    need[k] = (sem, thr)

        for t in reads:
            add(t.w, False)
            if t.excl:
                add(t.r, True)
        for t in writes:
            add(t.r, True)
            add(t.w, True)
        return need

    def _emit_waits(self, e, need):
        for k, (sem, thr) in need.items():
            if e["known"].get(k, 0) < thr:
                e["h"].wait_ge(sem, thr)
                e["known"][k] = thr

    def _commit(self, tok, reads, writes):
        k = id(tok[0])
        for t in reads:
            if t.r.get(k, (None, 0))[1] < tok[1]:
                t.r[k] = tok
        for t in writes:
            t.w = {k: tok}
            t.r = {}

    def op(self, en, fn, reads=(), writes=(), sig=True):
        e = self.eng[en]
        reads = _trks(reads)
        writes = _trks(writes)
        self._emit_waits(e, self._need(e, reads, writes))
        inst = fn()
        if sig:
            e["cnt"] += 1
            inst.then_inc(e["sem"], 1)
            tok = (e["sem"], e["cnt"])
        else:
            assert en == "pe"
            tok = (e["sem"], e["cnt"] + 1)
        self._commit(tok, reads, writes)
        return tok

    def dma(self, qn, out_ap, in_ap, chan, reads=(), writes=(), **kw):
        e = self.eng[qn]
        reads = _trks(reads)
        writes = _trks(writes)
        self._emit_waits(e, self._need(e, reads, writes))
        chan.n += 1
        e["h"].dma_start(out=out_ap, in_=in_ap, **kw).then_inc(chan.sem, 16)
        tok = (chan.sem, 16 * chan.n)
        self._commit(tok, reads, writes)
        return tok

    def wait_tokens(self, en, toks):
        e = self.eng[en]
        need = {}
        for sem, thr in toks:
            k = id(sem)
            if need.get(k, (None, 0))[1] < thr:
                need[k] = (sem, thr)
        self._emit_waits(e, need)


def build(nseq=4, dbg=(), stop=None, nunits_override=None):
    nc = bass.Bass("TRN2", target_bir_lowering=False)
    ntok = nseq * SEQ
    dbg = set(dbg)
    dbg_out = {}

    def din(name, shape, dt=F32):
        return nc.dram_tensor(name, list(shape), dt, kind="ExternalInput").ap()

    x_d = din("x", [ntok, D])
    pos_d = din("pos", [nseq * 16, 128], I32)
    wg_d = [din("wg1", [D, DFF]), din("wg2", [D, DFF])]
    wu_d = [din("wu1", [D, DFF]), din("wu2", [D, DFF])]
    wd_d = [din("wd1", [DFF, D]), din("wd2", [DFF, D])]
    win_d = din("w_in", [D, DIN])
    wout_d = din("w_out", [DMIX, D])
    lng_d = [din("ln%d_g" % i, [1, D]) for i in (1, 2, 3)]
    lnb_d = [din("ln%d_b" % i, [1, D]) for i in (1, 2, 3)]
    convw_d = din("conv_w_l", [128, 14, 4])
    convb_d = din("conv_b_l", [128, 14])
    dtb_d = din("dt_bias", [1, H])
    alog_d = din("a_log", [1, H])
    dsk_d = din("d_skip", [1, H])
    anw_d = din("attn_norm_w", [1, 768])
    snw_d = din("ssd_norm_w", [1, 768])
    masks_d = din("c_masks", [128, 9 * 128], BF16)
    identb_d = din("c_identb", [128, 128], BF16)
    identf_d = din("c_identf", [128, 128])
    cU_d = din("c_U", [128, 128])
    cSL_d = din("c_SL", [128, 128])
    cones_d = din("c_ones", [128, 128])
    cneg_d = din("c_negb", [128, 128], BF16)
    invf_d = din("c_invf", [1, 8])
    out_d = nc.dram_tensor("out", [ntok, D], F32, kind="ExternalOutput").ap()

    def dscr(name, shape):
        return nc.dram_tensor(name, list(shape), BF16, kind="Internal").ap()

    sg_s = [dscr("s_wg1", [D, DFF]), dscr("s_wg2", [D, DFF])]
    su_s = [dscr("s_wu1", [D, DFF]), dscr("s_wu2", [D, DFF])]
    sd_s = [dscr("s_wd1", [DFF, D]), dscr("s_wd2", [DFF, D])]
    sin_s = dscr("s_win", [D, DIN])
    sout_s = dscr("s_wout", [DMIX, D])

    with contextlib.ExitStack() as es:
        cx = Ctx(nc, es)

        def sb(name, shape, dt):
            return es.enter_context(nc.sbuf_tensor(name, list(shape), dt))

        def mk(name, shape, dt, chan=False):
            t = sb(name, shape, dt)
            return Buf(t, [Trk()], cx.chan() if chan else None)

        kT = mk("kT", [128, 6, SEQ], BF16)
        Vp = mk("Vp", [128, 16, H * 65], BF16)
        hstate = mk("hstate", [128, 768], F32)
        prevb = mk("prevb", [128, 768], BF16)
        tail = mk("tail", [128, 14, 3], F32)
        masks = mk("masks", [128, 9 * 128], BF16, True)
        identb = mk("identb", [128, 128], BF16, True)
        identf = mk("identf", [128, 128], F32, True)
        cU = mk("cU", [128, 128], F32, True)
        cUb = mk("cUb", [128, 128], BF16)
        cSL = mk("cSL", [128, 128], F32, True)
        cones = mk("cones", [128, 128], F32, True)
        cnegb = mk("cnegb", [128, 128], BF16, True)
        invf = mk("invf", [128, 8], F32, True)
        cost = mk("cost", [128, 16, 8], F32)
        sint = mk("sint", [128, 16, 8], F32)
        lng = mk("lng", [128, D], F32, True)
        lnb = mk("lnb", [128, D], F32, True)
        snw = mk("snw", [128, 768], F32, True)
        convw = mk("convw", [128, 14, 4], F32, True)
        convb = mk("convb", [128, 14], F32, True)
        dtb = mk("dtb", [128, H], F32, True)
        aneg = mk("aneg", [128, H], F32, True)
        dsk = mk("dsk", [128, H], F32, True)
        hA_t = sb("hA", [128, NT, D], F32)
        hA = [Buf(hA_t[:, t, :], [Trk()], cx.chan(), cx.chan()) for t in range(NT)]
        hT_h = sb("hT", [128, 8, TU], BF16)
        hT_trk = [Trk() for _ in range(NT)]
        hT = Buf(hT_h, hT_trk)
        hT_t = [Buf(hT_h, [hT_trk[t]]) for t in range(NT)]
        sgt1 = mk("sgt1", [128, 512], F32)
        st_bn = mk("st_bn", [128, 2, 6], F32)
        st_mv = mk("st_mv", [128, 2], F32)
        st_sd = mk("st_sd", [128, 1], F32)
        st_rs = mk("st_rs", [128, 1], F32)
        st_ss = mk("st_ss", [128, 1], F32)
        st_rd = mk("st_rd", [128, H], F32)

        arena_t = sb("arena", [128, ARENA // 4], F32)
        blocks = [Trk() for _ in range(ARENA // BLK)]

        def av(off, nbytes, dt, chan=False, **rr):
            assert off % 4 == 0 and nbytes % 4 == 0 and off + nbytes <= ARENA
            ap = arena_t[:, off // 4:(off + nbytes) // 4]
            if dt is not F32:
                ap = ap.bitcast(dt)
            trks = blocks[off // BLK:(off + nbytes - 1) // BLK + 1]
            return Buf(ap, trks, cx.chan() if chan else None)

        win_slot = [av(0, 8192, BF16, True), av(8192, 8192, BF16, True)]
        A_AT = 16384
        aT = av(A_AT, 22528, BF16)
        sgt = [av(A_AT + 22528, 2048, F32), sgt1]
        wout_b = av(A_AT, 24576, BF16, True)
        A_WGU = 40960
        wg_slot = [av(A_WGU + s * 8192, 4096, BF16, True) for s in range(3)]
        wu_slot = [av(A_WGU + s * 8192 + 4096, 4096, BF16, True) for s in range(3)]
        A_WD = 65536
        wd_grp = [av(A_WD + g * 4096, 4096, BF16, True) for g in range(11)]
        o = A_WGU
        zs = av(o, 6144, BF16); o += 6144
        xs_tok = av(o, 6144, BF16); o += 6144
        B_tok = av(o, 4096, BF16); o += 4096
        BTb = av(o, 4096, BF16); o += 4096
        CTb = av(o, 4096, BF16); o += 4096
        assert o == A_WD
        pc = [av(o + i * 2304, 2060, F32) for i in range(2)]; o += 4608
        cacc = [av(o + i * 2048, 2048, F32) for i in range(2)]; o += 4096
        cout = [av(o + i * 1024, 1024, BF16) for i in range(2)]; o += 2048
        xdt = av(o, 1536, BF16); o += 1536
        xdtd = av(o, 1536, BF16); o += 1536
        LT = [av(o + i * 1536, 1536, F32) for i in range(4)]; o += 6144
        MT = [av(o + i * 768, 768, BF16) for i in range(4)]; o += 3072
        yrow = [av(o, 3072, F32)] * 2; o += 3072
        ytmp = [av(o, 768, F32)] * 2; o += 768
        wdt_b = av(o, 192, BF16, True); o += 256
        dtv = av(o, 192, F32); o += 256
        dAb = av(o, 192, F32); o += 256
        Eb = av(o, 576, F32); o += 768
        nac = av(o, 192, F32); o += 192
        ddb = av(o, 48, F32); o += 64
        draw = av(o, 48, F32); o += 128
        dAhi = av(o, 96, BF16); o += 256
        dAlo = av(o, 96, BF16); o += 256
        dAtmp = av(o, 192, F32); o += 256
        assert o <= 95232
        o = A_WD
        qT = av(o, 6144, BF16); o += 6144
        qkst = [av(o + i * 1024, 1024, BF16) for i in range(2)]; o += 2048
        rtmp = [av(o + i * 256, 256, F32) for i in range(4)]; o += 1024
        PT = [av(o + i * 1024, 1024, BF16) for i in range(4)] + [av(88064 + i * 1024, 1024, BF16) for i in range(6)]; o += 4096
        arow = [av(o + i * 3072, 3072, F32) for i in range(2)]; o += 6144
        anw = av(84992, 3072, F32, True)
        assert o <= 84992
        mixT = av(95232, 12288, BF16)
        mixn = [av(107520 + i * 1536, 1536, BF16) for i in range(2)]
        stg_f = [av(i * 19712, 19504, F32, True) for i in range(3)]
        stg_b = [av(59392 + i * 9984, 9752, BF16, True) for i in range(3)]

        ps_t = [es.enter_context(nc.psum_tensor("ps%d" % i, [128, 512], F32)) for i in range(8)]
        ps = [Buf(ps_t[i], [Trk(excl=True)]) for i in range(8)]

        class Pool:
            lo, hi, nxt = 0, 8, 0
            split = False

        class Aux:
            lo, hi, nxt = 4, 6, 4

        def auxget(n=1):
            if not Pool.split:
                return psget(n)
            p = Aux.nxt
            if p + n > Aux.hi:
                p = Aux.lo
            Aux.nxt = p + n
            if Aux.nxt >= Aux.hi:
                Aux.nxt = Aux.lo
            return [ps[p + i] for i in range(n)]

        def psget(n=1):
            p = Pool.nxt
            if p % n:
                p += n - p % n
            if p + n > Pool.hi:
                p = Pool.lo
            Pool.nxt = p + n
            if Pool.nxt >= Pool.hi:
                Pool.nxt = Pool.lo
            return [ps[p + i] for i in range(n)]

        V = nc.vector
        A = nc.scalar
        G = nc.gpsimd
        T = nc.tensor

        def dump(name, buf, shape, dt=F32):
            if name not in dbg:
                return
            d = nc.dram_tensor("dbg_" + name, list(shape), dt, kind="ExternalOutput").ap()
            ch = cx.chan()
            tok = cx.dma("sp", d, buf.ap[:] if isinstance(buf, Buf) else buf[0], ch,
                         reads=[buf] if isinstance(buf, Buf) else buf[1])
            cx.wait_tokens("sp", [tok])
            dbg_out[name] = None

        def cload(buf, src):
            cx.dma("sp", buf.ap[:], src, buf.chan, writes=[buf])

        cload(masks, masks_d)
        cload(identb, identb_d)
        cload(identf, identf_d)
        cload(cU, cU_d)
        cload(cSL, cSL_d)
        cload(cones, cones_d)
        cload(cnegb, cneg_d)
        cload(invf, invf_d[0:1, :].partition_broadcast(128))
        cload(snw, snw_d[0:1, :].partition_broadcast(128))
        cload(convw, convw_d)
        cload(convb, convb_d)
        cload(dtb, dtb_d[0:1, :].partition_broadcast(128))
        cload(aneg, alog_d[0:1, :].partition_broadcast(128))
        cload(dsk, dsk_d[0:1, :].partition_broadcast(128))
        cx.op("act", lambda: A.activation(out=aneg.ap[:], in_=aneg.ap[:], func=AF.Exp), reads=[aneg], writes=[aneg])
        cx.op("dve", lambda: V.tensor_scalar(out=aneg.ap[:], in0=aneg.ap[:], scalar1=-1.0, scalar2=None, op0=ALU.mult),
              reads=[aneg], writes=[aneg])
        cx.op("dve", lambda: V.tensor_copy(out=cUb.ap[:], in_=cU.ap[:]), reads=[cU], writes=[cUb])
        cx.op("pool", lambda: G.memset(Vp.ap[:], 1.0), writes=[Vp])

        RB = 95232
        nti = 16
        posi = av(RB, 512, I32, True)
        posf = av(RB + 512, 512, F32)
        post = av(RB + 1024, 64, F32)
        ang = av(RB + 1088, 512, F32)
        kint = av(RB + 1600, 512, I32)
        kf = av(RB + 2112, 512, F32)
        tgt = av(RB + 2624, 512, F32)
        TWO_PI = 2.0 * np.pi
        C1 = 6.28125
        C2 = float(TWO_PI - 6.28125)

        def sin_table(dst, shift):
            cx.op("dve", lambda: V.tensor_scalar(out=kint.ap[:, :], in0=ang.ap[:, :], scalar1=float(shift),
                                                 scalar2=float(1.0 / TWO_PI), op0=ALU.add, op1=ALU.mult),
                  reads=[ang], writes=[kint])
            cx.op("dve", lambda: V.tensor_copy(out=kf.ap[:, :], in_=kint.ap[:, :]), reads=[kint], writes=[kf])
            cx.op("dve", lambda: V.scalar_tensor_tensor(out=tgt.ap[:, :], in0=kf.ap[:, :], scalar=-C1, in1=ang.ap[:, :],
                                                        op0=ALU.mult, op1=ALU.add), reads=[kf, ang], writes=[tgt])
            cx.op("dve", lambda: V.scalar_tensor_tensor(out=tgt.ap[:, :], in0=kf.ap[:, :], scalar=-C2, in1=tgt.ap[:, :],
                                                        op0=ALU.mult, op1=ALU.add), reads=[kf, tgt], writes=[tgt])
            if shift:
                cx.op("dve", lambda: V.tensor_scalar(out=tgt.ap[:, :], in0=tgt.ap[:, :], scalar1=float(shift), scalar2=None,
                                                     op0=ALU.add), reads=[tgt], writes=[tgt])
            cx.op("dve", lambda: V.tensor_scalar(out=kf.ap[:, :], in0=tgt.ap[:, :], scalar1=float(np.pi), scalar2=float(-TWO_PI),
                                                 op0=ALU.is_gt, op1=ALU.mult), reads=[tgt], writes=[kf])
            cx.op("dve", lambda: V.tensor_tensor(out=tgt.ap[:, :], in0=tgt.ap[:, :], in1=kf.ap[:, :], op=ALU.add),
                  reads=[tgt, kf], writes=[tgt])
            cx.op("dve", lambda: V.tensor_scalar(out=tgt.ap[:, :], in0=tgt.ap[:, :], scalar1=float(-np.pi), scalar2=float(np.pi),
                                                 op0=ALU.max, op1=ALU.min), reads=[tgt], writes=[tgt])
            cx.op("act", lambda: A.activation(out=dst.ap[:, :, :].rearrange("p t f -> p (t f)"), in_=tgt.ap[:, :], func=AF.Sin),
                  reads=[tgt], writes=[dst])

        def rotary_tables(sq):
            cx.dma("sp", posi.ap[0:nti, :], pos_d[sq * 16:(sq + 1) * 16, :], posi.chan, writes=[posi])
            cx.op("dve", lambda: V.tensor_copy(out=posf.ap[0:nti, :], in_=posi.ap[0:nti, :]), reads=[posi], writes=[posf])
            bk = psget(1)[0]
            cx.op("pe", lambda: T.transpose(bk.ap[:, 0:nti], posf.ap[0:nti, :], identf.ap[0:nti, 0:nti]),
                  reads=[posf, identf], writes=[bk])
            cx.op("dve", lambda: V.tensor_copy(out=post.ap[:, :], in_=bk.ap[:, 0:nti]), reads=[bk], writes=[post])
            cx.op("dve", lambda: V.tensor_tensor(out=ang.ap[:, :].rearrange("p (t f) -> p t f", f=8),
                                                 in0=post.ap[:, :].unsqueeze(2).to_broadcast([128, nti, 8]),
                                                 in1=invf.ap[:, :].unsqueeze(1).to_broadcast([128, nti, 8]), op=ALU.mult),
                  reads=[post, invf], writes=[ang])
            sin_table(sint, 0.0)
            sin_table(cost, np.pi / 2)

        store_toks = []
        rr = [0]
        cast_eng = ["dve", "act"]

        NSTG = 9
        PCW = 2048
        stg_f = [av(i * 12288, 8192, F32, True) for i in range(NSTG)]
        stg_b = [av(i * 12288 + 8192, 4096, BF16, True) for i in range(NSTG)]

        def cast_matrix(src, dst, rows, cols):
            for rc in range(rows // 128):
                for c0 in range(0, cols, PCW):
                    n = min(PCW, cols - c0)
                    i = rr[0] % NSTG
                    e = cast_eng[rr[0] % len(cast_eng)]
                    rr[0] += 1
                    f, b = stg_f[i], stg_b[i]
                    cx.dma("sp", f.ap[:, 0:n], src[rc * 128:(rc + 1) * 128, c0:c0 + n], f.chan, writes=[f])
                    if e == "dve":
                        cx.op("dve", lambda: V.tensor_copy(out=b.ap[:, 0:n], in_=f.ap[:, 0:n]), reads=[f], writes=[b])
                    elif e == "act":
                        cx.op("act", lambda: A.copy(out=b.ap[:, 0:n], in_=f.ap[:, 0:n]), reads=[f], writes=[b])
                    else:
                        cx.op("pool", lambda: G.tensor_copy(out=b.ap[:, 0:n], in_=f.ap[:, 0:n]), reads=[f], writes=[b])
                    store_toks.append(cx.dma("act", dst[rc * 128:(rc + 1) * 128, c0:c0 + n], b.ap[:, 0:n], b.chan, reads=[b]))

        jobs = [(wg_d[0], sg_s[0], D, DFF), (wu_d[0], su_s[0], D, DFF), (wd_d[0], sd_s[0], DFF, D), (win_d, sin_s, D, DIN),
                (wout_d, sout_s, DMIX, D), (wg_d[1], sg_s[1], D, DFF), (wu_d[1], su_s[1], D, DFF), (wd_d[1], sd_s[1], DFF, D)]
        if stop == "const":
            jobs = []
        if stop and stop.startswith("cast:"):
            cast_eng = stop[5:].split(",")
            jobs = jobs[:1]
        if stop and stop.startswith("castn:"):
            sel = [int(v) for v in stop[6:].split(",")]
            jobs = [jobs[i] for i in sel]
        for jb in jobs:
            cast_matrix(*jb)
        cx.wait_tokens("sp", store_toks)
        STOP = stop

        def load_ln(k, final):
            cx.dma("sp", lng.ap[:], lng_d[k][0:1, :].partition_broadcast(128), lng.chan, writes=[lng])
            cx.dma("sp", lnb.ap[:], lnb_d[k][0:1, :].partition_broadcast(128), lnb.chan, writes=[lnb])
            if not final:
                cx.op("act", lambda: A.mul(out=lnb.ap[:], in_=lnb.ap[:], mul=ALPHA), reads=[lnb], writes=[lnb])

        def make_hT(t, src=None, scale=1.0 / ALPHA):
            h = hA[t] if src is None else src
            for half in range(2):
                bk = auxget(1)[0]
                for j in range(4):
                    c = half * 4 + j
                    cx.op("pe", lambda: T.transpose(bk.ap[:, j * 128:(j + 1) * 128], h.ap[:, c * 128:(c + 1) * 128], identf.ap[:]),
                          reads=[h, identf], writes=[bk], sig=(j == 3))
                cx.op("act", lambda: A.activation(out=hT.ap[:, half * 4:half * 4 + 4, t * 128:(t + 1) * 128],
                                                  in_=bk.ap[:, :].rearrange("p (c k) -> p c k", c=4),
                                                  func=AF.Copy, scale=float(scale)),
                      reads=[bk], writes=[hT_t[t]])

        def layer_norm(t, final, lnexp=False):
            h = hA[t]
            for i in range(2):
                cx.op("dve", lambda: V.bn_stats(out=st_bn.ap[:, i, :], in_=h.ap[:, i * 512:(i + 1) * 512]),
                      reads=[h], writes=[st_bn])
            cx.op("dve", lambda: V.bn_aggr(out=st_mv.ap[:], in_=st_bn.ap[:, :, :].rearrange("p a b -> p (a b)")),
                  reads=[st_bn], writes=[st_mv])
            sc = 1.0 if final else 1.0 / (ALPHA * ALPHA)
            if lnexp:
                cx.op("act", lambda: A.activation(out=st_sd.ap[:], in_=st_mv.ap[:, 1:2], func=AF.Ln, scale=float(sc),
                                                  bias=float(LN_EPS * sc)), reads=[st_mv], writes=[st_sd])
                cx.op("act", lambda: A.activation(out=st_rs.ap[:], in_=st_sd.ap[:], func=AF.Exp, scale=-0.5), reads=[st_sd], writes=[st_rs])
            else:
                cx.op("act", lambda: A.activation(out=st_sd.ap[:], in_=st_mv.ap[:, 1:2], func=AF.Sqrt, scale=float(sc),
                                                  bias=float(LN_EPS * sc)), reads=[st_mv], writes=[st_sd])
                cx.op("dve", lambda: V.reciprocal(out=st_rs.ap[:], in_=st_sd.ap[:]), reads=[st_sd], writes=[st_rs])
            cx.op("dve", lambda: V.scalar_tensor_tensor(out=h.ap[:, :], in0=h.ap[:, :], scalar=st_mv.ap[:, 0:1], in1=lng.ap[:, :],
                                                        op0=ALU.subtract, op1=ALU.mult), reads=[h, st_mv, lng], writes=[h])
            cx.op("dve", lambda: V.scalar_tensor_tensor(out=h.ap[:, :], in0=h.ap[:, :], scalar=st_rs.ap[:, 0:1], in1=lnb.ap[:, :],
                                                        op0=ALU.mult, op1=ALU.add), reads=[h, st_rs, lnb], writes=[h])

        slot_rr = [0]
        NUNITS = [0]

        xst = [av(t * 4096, 4096, F32, True) for t in range(NT)]

        def load_x(u, t):
            tk = u * TU
            cx.dma("sp", xst[t].ap[:, :], x_d[tk + t * 128:tk + (t + 1) * 128, :], xst[t].chan, writes=[xst[t]])

        def x_to_hT(t):
            make_hT(t, src=xst[t], scale=1.0)

        def x_to_hA(t):
            h = hA[t]
            cx.op("act", lambda: A.mul(out=h.ap[:, :], in_=xst[t].ap[:, :], mul=ALPHA), reads=[xst[t]], writes=[h])

        def ffn_phase(k, u, final):
            tok0 = u * TU
            sg_, su_, sd_ = sg_s[k], su_s[k], sd_s[k]
            state = {"nld": 0}

            def load_gu(fcg):
                s = slot_rr[0] % 3
                slot_rr[0] += 1
                c0 = fcg * 256
                cx.dma("sp", wg_slot[s].ap[:, :].rearrange("p (c n) -> p c n", c=8),
                       sg_[:, c0:c0 + 256].rearrange("(c p) n -> p c n", p=128), wg_slot[s].chan, writes=[wg_slot[s]])
                cx.dma("sp", wu_slot[s].ap[:, :].rearrange("p (c n) -> p c n", c=8),
                       su_[:, c0:c0 + 256].rearrange("(c p) n -> p c n", p=128), wu_slot[s].chan, writes=[wu_slot[s]])
                return s

            def load_d(g):
                cx.dma("sp", wd_grp[g].ap[:, :].rearrange("p (j n) -> p j n", j=2),
                       sd_[g * 256:(g + 1) * 256, :].rearrange("(j p) n -> p j n", p=128), wd_grp[g].chan, writes=[wd_grp[g]])

            slots = {}
            for g in range(3):
                slots[g] = load_gu(g)
            load_ln(0 if k == 0 else 2, final)
            load_d(0)
            for fcg in range(11):
                s = slots[fcg]
                wg3 = wg_slot[s].ap[:, :].rearrange("p (c n) -> p c n", c=8)
                wu3 = wu_slot[s].ap[:, :].rearrange("p (c n) -> p c n", c=8)
                for j in range(2):
                    fc = fcg * 2 + j
                    bg, bu = psget(2)
                    for c in range(8):
                        cx.op("pe", lambda: T.matmul(bg.ap[:, :], lhsT=wg3[:, c, j * 128:(j + 1) * 128], rhs=hT.ap[:, c, :],
                                                     start=(c == 0), stop=(c == 7)),
                              reads=[wg_slot[s], hT], writes=[bg], sig=(c == 7))
                    for c in range(8):
                        cx.op("pe", lambda: T.matmul(bu.ap[:, :], lhsT=wu3[:, c, j * 128:(j + 1) * 128], rhs=hT.ap[:, c, :],
                                                     start=(c == 0), stop=(c == 7)),
                              reads=[wu_slot[s], hT], writes=[bu], sig=(c == 7))
                    sgb = sgt[fc % 2]
                    cx.op("act", lambda: A.activation(out=sgb.ap[:, :], in_=bg.ap[:, :], func=AF.Silu), reads=[bg], writes=[sgb])
                    cx.op("dve", lambda: V.tensor_tensor(out=aT.ap[:, fc * 512:(fc + 1) * 512], in0=sgb.ap[:, :], in1=bu.ap[:, :],
                                                         op=ALU.mult), reads=[sgb, bu], writes=[aT])
                if fcg + 3 < 11:
                    slots[fcg + 3] = load_gu(fcg + 3)
                if fcg + 1 < 11:
                    load_d(fcg + 1)
            pending = None
            if final and u + 1 < NUNITS[0]:
                for t in range(NT):
                    load_x(u + 1, t)
            for t in range(NT):
                b0, b1 = psget(2)
                for half, bk in ((0, b0), (1, b1)):
                    for fc in range(NFC):
                        g = fc // 2
                        wd3 = wd_grp[g].ap[:, :].rearrange("p (j n) -> p j n", j=2)
                        cx.op("pe", lambda: T.matmul(bk.ap[:, :], lhsT=aT.ap[:, fc * 512 + t * 128:fc * 512 + (t + 1) * 128],
                                                     rhs=wd3[:, fc % 2, half * 512:(half + 1) * 512],
                                                     start=(fc == 0), stop=(fc == NFC - 1)),
                              reads=[aT, wd_grp[g]], writes=[bk], sig=(fc == NFC - 1))
                if pending is not None:
                    pending()
                    pending = None
                h = hA[t]
                for half, bk in ((0, b0), (1, b1)):
                    cx.op("dve", lambda: V.scalar_tensor_tensor(out=h.ap[:, half * 512:(half + 1) * 512], in0=bk.ap[:, :], scalar=0.5,
                                                                in1=h.ap[:, half * 512:(half + 1) * 512], op0=ALU.mult, op1=ALU.add),
                          reads=[bk, h], writes=[h])
                layer_norm(t, final)
                if final:
                    cx.dma("act", out_d[tok0 + t * 128:tok0 + (t + 1) * 128, :], h.ap[:, :], h.chan2, reads=[h])
                    if u + 1 < NUNITS[0]:
                        pending = (lambda tt: (lambda: x_to_hT(tt)))(t)
                else:
                    pending = (lambda tt: (lambda: make_hT(tt)))(t)
            if pending is not None:
                pending()

        def load_win(slot, col0, ncols):
            b = win_slot[slot]
            cx.dma("sp", b.ap[:, :].rearrange("p (c n) -> p c n", c=8)[:, :, 0:ncols],
                   sin_s[:, col0:col0 + ncols].rearrange("(c p) n -> p c n", p=128), b.chan, writes=[b])

        def rms_norm(row, wbc, mslot):
            mn = mixn[mslot]
            cx.op("act", lambda: A.activation(out=mn.ap[:, :], in_=row.ap[:, :], func=AF.Square, accum_out=st_ss.ap[:, 0:1]),
                  reads=[row], writes=[mn, st_ss])
            cx.op("act", lambda: A.activation(out=st_sd.ap[:], in_=st_ss.ap[:], func=AF.Ln, scale=float(1.0 / 768.0),
                                              bias=float(RMS_EPS)), reads=[st_ss], writes=[st_sd])
            cx.op("act", lambda: A.activation(out=st_rs.ap[:], in_=st_sd.ap[:], func=AF.Exp, scale=-0.5), reads=[st_sd], writes=[st_rs])
            cx.op("dve", lambda: V.scalar_tensor_tensor(out=mn.ap[:, :], in0=row.ap[:, :], scalar=st_rs.ap[:, 0:1], in1=wbc.ap[:, :],
                                                        op0=ALU.mult, op1=ALU.mult), reads=[row, st_rs, wbc], writes=[mn])

            def do_T(t, c0):
                bk = auxget(1)[0]
                pb = bk.ap[:, :].bitcast(BF16)
                for c in range(6):
                    cx.op("pe", lambda: T.transpose(pb[:, c * 128:(c + 1) * 128], mn.ap[:, c * 128:(c + 1) * 128], identb.ap[:]),
                          reads=[mn, identb], writes=[bk], sig=(c == 5))
                m3 = mixT.ap[:, :].rearrange("p (c k) -> p c k", c=12)
                cx.op("act", lambda: A.copy(out=m3[:, c0:c0 + 6, t * 128:(t + 1) * 128],
                                            in_=pb[:, 0:768].rearrange("p (c k) -> p c k", c=6)), reads=[bk], writes=[mixT])
            return do_T

        def chk(name):
            if STOP == name:
                raise _StopNow()

        def mixer_phase(u):
            q = u % UPS
            sq = u // UPS
            tok0 = u * TU
            m3 = mixT.ap[:, :].rearrange("p (c k) -> p c k", c=12)
            ws = [0]

            def next_seg(col0, ncols):
                s = ws[0] % 2
                ws[0] += 1
                load_win(s, col0, ncols)
                return s

            if q == 0:
                cx.op("pool", lambda: G.memset(hstate.ap[:], 0.0), writes=[hstate])
                cx.op("pool", lambda: G.memset(prevb.ap[:], 0.0), writes=[prevb])
                cx.op("pool", lambda: G.memset(tail.ap[:], 0.0), writes=[tail])

            def proj_tok(slot, ncols, t, bk):
                w3 = win_slot[slot].ap[:, :].rearrange("p (c n) -> p c n", c=8)
                for c in range(8):
                    cx.op("pe", lambda: T.matmul(bk.ap[:, 0:ncols], lhsT=hT.ap[:, c, t * 128:(t + 1) * 128], rhs=w3[:, c, 0:ncols],
                                                 start=(c == 0), stop=(c == 7)),
                          reads=[hT_t[t], win_slot[slot]], writes=[bk], sig=(c == 7))

            z3 = zs.ap[:, :].rearrange("p (t n) -> p t n", t=NT)
            dt3 = dtv.ap[:, :].rearrange("p (t n) -> p t n", t=NT)
            dA3 = dAb.ap[:, :].rearrange("p (t n) -> p t n", t=NT)
            hi3 = dAhi.ap[:, :].rearrange("p (t n) -> p t n", t=NT)
            lo3 = dAlo.ap[:, :].rearrange("p (t n) -> p t n", t=NT)
            wdt3 = wdt_b.ap[:, :].rearrange("p (c n) -> p c n", c=8)
            xs4 = xs_tok.ap[:, :].rearrange("p (t n) -> p t n", t=NT)
            Bt4 = B_tok.ap[:, :].rearrange("p (t n) -> p t n", t=NT)
            BT3 = BTb.ap[:, :].rearrange("p (g k) -> p g k", g=NG)
            CT3 = CTb.ap[:, :].rearrange("p (g k) -> p g k", g=NG)

            def z_step(s_, ncols, t, zoff):
                def f():
                    bk = psget(1)[0]
                    proj_tok(s_(), ncols, t, bk)
                    cx.op("act", lambda: A.activation(out=z3[:, t, zoff:zoff + ncols], in_=bk.ap[:, 0:ncols], func=AF.Silu),
                          reads=[bk], writes=[zs])
                return f

            def dt_step(t):
                def f():
                    if t == 0:
                        cx.dma("sp", wdt_b.ap[:, :].rearrange("p (c n) -> p c n", c=8),
                               sin_s[:, COL_DT:COL_DT + H].rearrange("(c p) n -> p c n", p=128), wdt_b.chan, writes=[wdt_b])
                    bk = psget(1)[0]
                    for c in range(8):
                        cx.op("pe", lambda: T.matmul(bk.ap[:, 0:H], lhsT=hT.ap[:, c, t * 128:(t + 1) * 128], rhs=wdt3[:, c, :],
                                                     start=(c == 0), stop=(c == 7)), reads=[hT_t[t], wdt_b], writes=[bk], sig=(c == 7))
                    cx.op("dve", lambda: V.tensor_tensor(out=draw.ap[:, :], in0=bk.ap[:, 0:H], in1=dtb.ap[:, :], op=ALU.add),
                          reads=[bk, dtb], writes=[draw])
                    cx.op("act", lambda: A.activation(out=draw.ap[:, :], in_=draw.ap[:, :], func=AF.Exp), reads=[draw], writes=[draw])
                    cx.op("act", lambda: A.activation(out=dt3[:, t, :], in_=draw.ap[:, :], func=AF.Ln, bias=1.0), reads=[draw], writes=[dtv])
                    cx.op("dve", lambda: V.tensor_tensor(out=dA3[:, t, :], in0=dt3[:, t, :], in1=aneg.ap[:, :], op=ALU.mult),
                          reads=[dtv, aneg], writes=[dAb])
                    if t == NT - 1:
                        cx.op("dve", lambda: V.tensor_copy(out=dAhi.ap[:, :], in_=dAb.ap[:, :]), reads=[dAb], writes=[dAhi])
                        cx.op("dve", lambda: V.tensor_tensor(out=dAtmp.ap[:, :], in0=dAb.ap[:, :], in1=dAhi.ap[:, :], op=ALU.subtract),
                              reads=[dAb, dAhi], writes=[dAtmp])
                        cx.op("dve", lambda: V.tensor_copy(out=dAlo.ap[:, :], in_=dAtmp.ap[:, :]), reads=[dAtmp], writes=[dAlo])
                return f

            zslot = {}
            extra = []
            zo = 0
            for zi, (c0, ncols) in enumerate(SEG_Z):
                for t in range(NT):
                    extra.append(("z", zi, c0, ncols, t, zo))
                zo += ncols
            for t in range(NT):
                extra.append(("dt", t))

            def run_extra():
                if not extra:
                    return
                e = extra.pop(0)
                if e[0] == "z":
                    _, zi, c0, ncols, t, zo_ = e
                    if zi not in zslot:
                        zslot[zi] = next_seg(c0, ncols)
                    z_step(lambda: zslot[zi], ncols, t, zo_)()
                else:
                    dt_step(e[1])()

            for _ in range(NT):
                e_ = [x for x in extra if x[0] == "dt"][0]
                extra.remove(e_)
                dt_step(e_[1])()
            early = [x for x in extra if x[0] == "z" and x[1] == 0 and x[4] < NT - 1]
            for e_ in early:
                extra.remove(e_)
                extra.insert(0, e_)
            for _ in range(len(early)):
                run_extra()
            ch = 0
            deferred = []

            def conv_T(ch, dst_b, dst):
                def f():
                    bt = psget(1)[0]
                    pb = bt.ap[:, :].bitcast(BF16)
                    for t in range(NT):
                        cx.op("pe", lambda: T.transpose(pb[:, t * 128:(t + 1) * 128], dst[:, t * 128:(t + 1) * 128], identb.ap[:]),
                              reads=[dst_b, identb], writes=[bt], sig=(t == NT - 1))
                    if ch < 6:
                        cx.op("dve", lambda: V.tensor_copy(out=xs4[:, :, ch * 128:(ch + 1) * 128],
                                                           in_=pb[:, 0:512].rearrange("p (t k) -> p t k", t=NT)),
                              reads=[bt], writes=[xs_tok])
                    else:
                        g = ch - 6
                        cx.op("dve", lambda: V.tensor_copy(out=Bt4[:, :, g * 128:(g + 1) * 128],
                                                           in_=pb[:, 0:512].rearrange("p (t k) -> p t k", t=NT)),
                              reads=[bt], writes=[B_tok])
                return f

            for (c0, ncols) in SEG_X:
                s = next_seg(c0, ncols)
                w3 = win_slot[s].ap[:, :].rearrange("p (c n) -> p c n", c=8)
                for lc in range(ncols // 128):
                    bk = psget(1)[0]
                    for c in range(8):
                        cx.op("pe", lambda: T.matmul(bk.ap[:, :], lhsT=w3[:, c, lc * 128:(lc + 1) * 128], rhs=hT.ap[:, c, :],
                                                     start=(c == 0), stop=(c == 7)), reads=[win_slot[s], hT], writes=[bk], sig=(c == 7))
                    if len(deferred) >= 2:
                        deferred.pop(0)()
                    p_ = pc[ch % 2]
                    ca = cacc[ch % 2]
                    cx.op("pool", lambda: G.tensor_copy(out=p_.ap[:, 0:3], in_=tail.ap[:, ch, :]), reads=[tail], writes=[p_])
                    cx.op("act", lambda: A.copy(out=p_.ap[:, 3:515], in_=bk.ap[:, :]), reads=[bk], writes=[p_])
                    cx.op("pool", lambda: G.tensor_copy(out=tail.ap[:, ch, :], in_=p_.ap[:, 512:515]), reads=[p_], writes=[tail])
                    cx.op("act", lambda: A.activation(out=ca.ap[:, :], in_=p_.ap[:, 0:512], func=AF.Copy, scale=convw.ap[:, ch, 0:1]),
                          reads=[p_, convw], writes=[ca])
                    for kk in range(1, 4):
                        cx.op("dve", lambda: V.scalar_tensor_tensor(out=ca.ap[:, :], in0=p_.ap[:, kk:kk + 512],
                                                                    scalar=convw.ap[:, ch, kk:kk + 1], in1=ca.ap[:, :],
                                                                    op0=ALU.mult, op1=ALU.add), reads=[p_, convw, ca], writes=[ca])
                    if ch < 6:
                        dst_b, dst = cout[ch % 2], cout[ch % 2].ap[:, :]
                    elif ch < 10:
                        dst_b, dst = BTb, BT3[:, ch - 6, :]
                    else:
                        dst_b, dst = CTb, CT3[:, ch - 10, :]
                    cx.op("act", lambda: A.activation(out=dst, in_=ca.ap[:, :], func=AF.Silu, bias=convb.ap[:, ch:ch + 1]),
                          reads=[ca, convb], writes=[dst_b])
                    if ch < 10:
                        deferred.append(conv_T(ch, dst_b, dst))
                    ch += 1
                    if extra and extra[0][0] == "z":
                        run_extra()
            while deferred:
                deferred.pop(0)()
            while extra:
                run_extra()
            while deferred:
                deferred.pop(0)()
            assert ch == 14
            chk("mix_conv")
            dump("zs", zs, [128, NT * 768], BF16)
            dump("xs_tok", xs_tok, [128, NT * 768], BF16)
            dump("B_tok", B_tok, [128, NT * 512], BF16)
            dump("CTb", CTb, [128, NG * 512], BF16)
            dump("dtv", dtv, [128, NT * H])
            pre_seg = [next_seg(*SEG_QKV[0]), next_seg(*SEG_QKV[1])]
            cx.dma("sp", wout_b.ap[:, :].rearrange("p (c n) -> p c n", c=12),
                   sout_s[:, :].rearrange("(c p) n -> p c n", p=128), wout_b.chan, writes=[wout_b])
            load_ln(1, False)
            q3 = qT.ap[:, :].rearrange("p (c k) -> p c k", c=6)
            kbase = q * TU
            PQ = []
            seg_slot = {0: pre_seg[0], 1: pre_seg[1]}
            qkv_steps = [(si, c0, ncols, t) for si, (c0, ncols) in enumerate(SEG_QKV) for t in range(NT)]

            def run_qkv():
                if not qkv_steps:
                    return
                si, c0, ncols, t = qkv_steps.pop(0)
                if si not in seg_slot:
                    seg_slot[si] = next_seg(c0, ncols)
                if t == 0 and si + 1 < len(SEG_QKV) and (si + 1) not in seg_slot:
                    seg_slot[si + 1] = next_seg(*SEG_QKV[si + 1])
                s = seg_slot[si]
                gt = q * NT + t
                bk = psget(1)[0]
                proj_tok(s, ncols, t, bk)
                if len(PQ) >= 2:
                    PQ.pop(0)()
                if si < 3:
                    nh = ncols // 64
                    st = qkst[(si * NT + t) % 2]
                    b3 = bk.ap[:, 0:ncols].rearrange("p (h d) -> p h d", h=nh)
                    s3 = st.ap[:, 0:ncols].rearrange("p (h d) -> p h d", h=nh)
                    cb = cost.ap[:, gt, :].unsqueeze(1).to_broadcast([128, nh, 8])
                    sn = sint.ap[:, gt, :].unsqueeze(1).to_broadcast([128, nh, 8])
                    r = [rt.ap[:, :].rearrange("p (h d) -> p h d", h=8)[:, 0:nh, :] for rt in rtmp]
                    cx.op("act", lambda: A.copy(out=st.ap[:, 0:ncols], in_=bk.ap[:, 0:ncols]), reads=[bk], writes=[st])
                    cx.op("dve", lambda: V.tensor_tensor(out=r[0], in0=b3[:, :, 0:8], in1=cb, op=ALU.mult), reads=[bk, cost], writes=[rtmp[0]])
                    cx.op("dve", lambda: V.tensor_tensor(out=r[1], in0=b3[:, :, 8:16], in1=sn, op=ALU.mult), reads=[bk, sint], writes=[rtmp[1]])
                    cx.op("dve", lambda: V.tensor_tensor(out=r[2], in0=b3[:, :, 8:16], in1=cb, op=ALU.mult), reads=[bk, cost], writes=[rtmp[2]])
                    cx.op("dve", lambda: V.tensor_tensor(out=r[3], in0=b3[:, :, 0:8], in1=sn, op=ALU.mult), reads=[bk, sint], writes=[rtmp[3]])
                    cx.op("dve", lambda: V.tensor_tensor(out=s3[:, :, 0:8], in0=r[0], in1=r[1], op=ALU.subtract),
                          reads=[rtmp[0], rtmp[1]], writes=[st])
                    cx.op("dve", lambda: V.tensor_tensor(out=s3[:, :, 8:16], in0=r[2], in1=r[3], op=ALU.add),
                          reads=[rtmp[2], rtmp[3]], writes=[st])

                    def qk_T(st=st, ncols=ncols, c0=c0, t=t):
                        bt = psget(1)[0]
                        pb = bt.ap[:, :].bitcast(BF16)
                        npair = ncols // 128
                        for pi in range(npair):
                            cx.op("pe", lambda: T.transpose(pb[:, pi * 128:(pi + 1) * 128], st.ap[:, pi * 128:(pi + 1) * 128], identb.ap[:]),
                                  reads=[st, identb], writes=[bt], sig=(pi == npair - 1))
                        gp0 = c0 // 128
                        nq = max(0, min(6 - gp0, npair))
                        if nq > 0:
                            cx.op("act", lambda: A.copy(out=q3[:, gp0:gp0 + nq, t * 128:(t + 1) * 128],
                                                        in_=pb[:, 0:nq * 128].rearrange("p (c k) -> p c k", c=nq)),
                                  reads=[bt], writes=[qT])
                        if npair - nq > 0:
                            nk = npair - nq
                            kp0 = gp0 + nq - 6
                            cx.op("act", lambda: A.copy(out=kT.ap[:, kp0:kp0 + nk, kbase + t * 128:kbase + (t + 1) * 128],
                                                        in_=pb[:, nq * 128:npair * 128].rearrange("p (c k) -> p c k", c=nk)),
                                  reads=[bt], writes=[kT])
                    PQ.append(qk_T)
                else:
                    nh = ncols // 64
                    h0 = (c0 - 1536) // 64
                    vv = Vp.ap[:, gt, :].rearrange("p (h e) -> p h e", e=65)
                    cx.op("act", lambda: A.copy(out=vv[:, h0:h0 + nh, 0:64], in_=bk.ap[:, 0:ncols].rearrange("p (h d) -> p h d", h=nh)),
                          reads=[bk], writes=[Vp])

            Pool.lo, Pool.hi, Pool.nxt = 0, 6, 0
            E3 = Eb.ap[:, :]
            bS = [ps[6], ps[7]]
            h3 = hstate.ap[:, :].rearrange("p (h d) -> p h d", h=H)
            pend_T = None
            bE = psget(1)[0]
            for t in range(NT):
                for i, cm in enumerate((cU, cSL, cones)):
                    cx.op("pe", lambda: T.matmul(bE.ap[:, t * 36 + i * H:t * 36 + (i + 1) * H], lhsT=cm.ap[:, :], rhs=dA3[:, t, :],
                                                 start=True, stop=True), reads=[cm, dAb], writes=[bE], sig=(i == 2 and t == NT - 1))
            cx.op("act", lambda: A.activation(out=Eb.ap[:, :], in_=bE.ap[:, 0:NT * 36], func=AF.Exp), reads=[bE], writes=[Eb])
            cx.op("dve", lambda: V.tensor_scalar(out=nac.ap[:, :].rearrange("p (t h) -> p t h", t=NT),
                                                 in0=bE.ap[:, 0:NT * 36].rearrange("p (t k) -> p t k", t=NT)[:, :, 0:H],
                                                 scalar1=-1.0, scalar2=None, op0=ALU.mult), reads=[bE], writes=[nac])
            for t in range(NT):
                tc = slice(t * 128, (t + 1) * 128)
                E3 = Eb.ap[:, t * 36:(t + 1) * 36]
                nact = nac.ap[:, t * H:(t + 1) * H]
                cx.op("dve", lambda: V.tensor_tensor(out=ddb.ap[:, :], in0=dt3[:, t, :], in1=E3[:, H:2 * H], op=ALU.mult),
                      reads=[dtv, Eb], writes=[ddb])
                xs_h = xs4[:, t, :].rearrange("p (h d) -> p h d", h=H)
                cx.op("pool", lambda: G.tensor_tensor(out=xdt.ap[:, :].rearrange("p (h d) -> p h d", h=H), in0=xs_h,
                                                      in1=dt3[:, t, :].unsqueeze(2).to_broadcast([128, H, HD]), op=ALU.mult),
                      reads=[xs_tok, dtv], writes=[xdt])
                cx.op("pool", lambda: G.tensor_tensor(out=xdtd.ap[:, :].rearrange("p (h d) -> p h d", h=H), in0=xs_h,
                                                      in1=ddb.ap[:, :].unsqueeze(2).to_broadcast([128, H, HD]), op=ALU.mult),
                      reads=[xs_tok, ddb], writes=[xdtd])
                yr = yrow[t % 2]
                banks = {}

                def stage1(g):
                    bX = psget(1)[0]
                    banks[g] = bX
                    lt = LT[g]
                    mt = MT[g]
                    lt3 = lt.ap[:, :].rearrange("p (j k) -> p j k", j=3)
                    mt3 = mt.ap[:, :].rearrange("p (j k) -> p j k", j=3)
                    cx.op("pe", lambda: T.matmul(bX.ap[:, 384:512], lhsT=BT3[:, g, tc], rhs=CT3[:, g, tc], start=True, stop=True),
                          reads=[BTb, CTb], writes=[bX], sig=False)
                    for j in range(3):
                        hh = 3 * g + j
                        cx.op("pe", lambda: T.matmul(bX.ap[:, j * 128:(j + 1) * 128], lhsT=hi3[:, t, hh:hh + 1].to_broadcast([128, 128]),
                                                     rhs=cUb.ap[:, :], start=True, stop=False), reads=[dAhi, cUb], writes=[bX], sig=False)
                        cx.op("pe", lambda: T.matmul(bX.ap[:, j * 128:(j + 1) * 128], lhsT=lo3[:, t, hh:hh + 1].to_broadcast([128, 128]),
                                                     rhs=cUb.ap[:, :], start=False, stop=False), reads=[dAlo, cUb], writes=[bX], sig=False)
                        cx.op("pe", lambda: T.matmul(bX.ap[:, j * 128:(j + 1) * 128], lhsT=identb.ap[:, :], rhs=cnegb.ap[:, :],
                                                     start=False, stop=True), reads=[identb, cnegb], writes=[bX], sig=(j == 2))
                    for j in range(3):
                        hh = 3 * g + j
                        cx.op("act", lambda: A.activation(out=lt3[:, j, :], in_=bX.ap[:, j * 128:(j + 1) * 128], func=AF.Exp,
                                                          bias=nact[:, hh:hh + 1]), reads=[bX, nac], writes=[lt])
                    cx.op("dve", lambda: V.tensor_tensor(out=mt3, in0=lt3, in1=bX.ap[:, 384:512].unsqueeze(1).to_broadcast([128, 3, 128]),
                                                         op=ALU.mult), reads=[lt, bX], writes=[mt])

                def stage2(g):
                    bY = psget(1)[0]
                    mt = MT[g]
                    yt_ = ytmp[g % 2]
                    mt3 = mt.ap[:, :].rearrange("p (j k) -> p j k", j=3)
                    for j in range(3):
                        hh = 3 * g + j
                        cx.op("pe", lambda: T.matmul(bY.ap[:, j * 64:(j + 1) * 64], lhsT=mt3[:, j, :],
                                                     rhs=xdt.ap[:, hh * 64:(hh + 1) * 64], start=True, stop=True),
                              reads=[mt, xdt], writes=[bY], sig=False)
                    cx.op("pe", lambda: T.matmul(bY.ap[:, 192:384], lhsT=CT3[:, g, tc], rhs=prevb.ap[:, g * 192:(g + 1) * 192],
                                                 start=True, stop=True), reads=[CTb, prevb], writes=[bY])
                    bSg = bS[g // 2]
                    cx.op("pe", lambda: T.matmul(bSg.ap[:, (g % 2) * 192:(g % 2) * 192 + 192], lhsT=Bt4[:, t, g * 128:(g + 1) * 128],
                                                 rhs=xdtd.ap[:, g * 192:(g + 1) * 192], start=True, stop=True),
                          reads=[B_tok, xdtd], writes=[bSg])
                    yg = yr.ap[:, g * 192:(g + 1) * 192].rearrange("p (j d) -> p j d", j=3)
                    xg = xs4[:, t, g * 192:(g + 1) * 192].rearrange("p (j d) -> p j d", j=3)
                    cx.op("pool", lambda: G.tensor_tensor(out=yg, in0=xg, in1=dsk.ap[:, 3 * g:3 * g + 3].unsqueeze(2).to_broadcast([128, 3, HD]),
                                                          op=ALU.mult), reads=[xs_tok, dsk], writes=[yr])
                    cx.op("dve", lambda: V.tensor_tensor(out=yt_.ap[:, :].rearrange("p (j d) -> p j d", j=3),
                                                         in0=bY.ap[:, 192:384].rearrange("p (j d) -> p j d", j=3),
                                                         in1=E3[:, 3 * g:3 * g + 3].unsqueeze(2).to_broadcast([128, 3, HD]), op=ALU.mult),
                          reads=[bY, Eb], writes=[yt_])
                    cx.op("dve", lambda: V.tensor_tensor(out=yr.ap[:, g * 192:(g + 1) * 192], in0=yr.ap[:, g * 192:(g + 1) * 192],
                                                         in1=yt_.ap[:, :], op=ALU.add), reads=[yr, yt_], writes=[yr])
                    cx.op("dve", lambda: V.tensor_tensor(out=yr.ap[:, g * 192:(g + 1) * 192], in0=yr.ap[:, g * 192:(g + 1) * 192],
                                                         in1=bY.ap[:, 0:192], op=ALU.add), reads=[yr, bY], writes=[yr])

                for g in range(NG):
                    stage1(g)
                run_qkv()
                run_qkv()
                if pend_T is not None:
                    pend_T()
                    pend_T = None
                for g in range(NG):
                    stage2(g)
                run_qkv()
                run_qkv()
                run_qkv()
                if "ypre" in dbg and t == 0:
                    dump("ypre", yr, [128, 768])
                cx.op("dve", lambda: V.tensor_tensor(out=yr.ap[:, :], in0=yr.ap[:, :], in1=z3[:, t, :], op=ALU.mult),
                      reads=[yr, zs], writes=[yr])
                dT = rms_norm(yr, snw, t % 2)
                pend_T = (lambda f, tt: (lambda: f(tt, 6)))(dT, t)
                cx.op("pool", lambda: G.tensor_tensor(out=h3, in0=h3, in1=E3[:, 2 * H:3 * H].unsqueeze(2).to_broadcast([128, H, HD]),
                                                      op=ALU.mult), reads=[hstate, Eb], writes=[hstate])
                for i in range(2):
                    cx.op("dve", lambda: V.tensor_tensor(out=hstate.ap[:, i * 384:(i + 1) * 384], in0=hstate.ap[:, i * 384:(i + 1) * 384],
                                                         in1=bS[i].ap[:, 0:384], op=ALU.add), reads=[hstate, bS[i]], writes=[hstate])
                cx.op("act", lambda: A.copy(out=prevb.ap[:, :], in_=hstate.ap[:, :]), reads=[hstate], writes=[prevb])
            Pool.lo, Pool.hi, Pool.nxt = 0, 8, 0
            chk("mix_ssd")
            cx.dma("sp", anw.ap[:, :], anw_d[0:1, :].partition_broadcast(128), anw.chan, writes=[anw])
            while qkv_steps:
                run_qkv()
            if pend_T is not None:
                pend_T()
                pend_T = None
            while PQ:
                PQ.pop(0)()
            chk("mix_qkv")
            dump("qT", qT, [128, 6 * 512], BF16)

            Pool.lo, Pool.hi, Pool.nxt = 0, 4, 0
            Pool.split = True
            accA, accB = ps[6], ps[7]
            mk3 = masks.ap
            wo3 = wout_b.ap[:, :].rearrange("p (c n) -> p c n", c=12)
            NPT = len(PT)
            LAG = NPT // 2 - 2

            def stage_a(t):
                gt = q * NT + t
                ngrp = gt // 4 + 1
                items = []
                for pair in range(6):
                    for g in range(ngrp):
                        bmax = min(4 * g + 3, gt)
                        nb = bmax - 4 * g + 1
                        js = [gt - b for b in range(bmax, 4 * g - 1, -1)]
                        items.append((pair, g, nb, js))
                stash = {}

                def emit_qk(i):
                    pair, g, nb, js = items[i]
                    bks = [psget(1)[0], psget(1)[0]]
                    for idx, j in enumerate(js):
                        for hb in range(2):
                            bp = hb * 64
                            bk = bks[hb]
                            cx.op("pe", lambda: T.matmul(bk.ap[:, idx * 128:(idx + 1) * 128], lhsT=kT.ap[bp:bp + 64, pair, j * 128:(j + 1) * 128],
                                                         rhs=q3[bp:bp + 64, pair, t * 128:(t + 1) * 128], start=True, stop=True),
                                  reads=[kT, qT], writes=[bk], sig=(idx == nb - 1))
                    ms = MSTART[min(g, 2)]
                    pts = []
                    for hb in range(2):
                        pt = PT[(2 * i + hb) % NPT]
                        bk = bks[hb]
                        cx.op("act", lambda: A.activation(out=pt.ap[:, 0:nb * 128], in_=bk.ap[:, 0:nb * 128], func=AF.Exp, scale=0.125),
                              reads=[bk], writes=[pt])
                        cx.op("dve", lambda: V.tensor_tensor(out=pt.ap[:, 0:nb * 128], in0=pt.ap[:, 0:nb * 128],
                                                             in1=mk3[:, (ms + 4 - nb) * 128:(ms + 4) * 128], op=ALU.mult),
                              reads=[pt, masks], writes=[pt])
                        pts.append(pt)
                    stash[i] = pts

                def emit_pv(i):
                    pair, g, nb, js = items[i]
                    pts = stash.pop(i)
                    for hb in range(2):
                        hh = 2 * pair + hb
                        pt = pts[hb]
                        acc = accA if hb == 0 else accB
                        col = pair * 65
                        for idx, j in enumerate(js):
                            first = (g == 0 and idx == 0)
                            last = (g == ngrp - 1 and idx == nb - 1)
                            cx.op("pe", lambda: T.matmul(acc.ap[:, col:col + 65], lhsT=pt.ap[:, idx * 128:(idx + 1) * 128],
                                                         rhs=Vp.ap[:, j, hh * 65:(hh + 1) * 65], start=first, stop=last),
                                  reads=[pt, Vp], writes=[acc], sig=(idx == nb - 1))

                n = len(items)
                for i in range(n + LAG):
                    if i < n:
                        emit_qk(i)
                    if i - LAG >= 0:
                        emit_pv(i - LAG)
                ar = arow[t % 2]
                for i, acc in enumerate((accA, accB)):
                    a3_ = acc.ap[:, 0:390].rearrange("p (h e) -> p h e", e=65)
                    cx.op("dve", lambda: V.reciprocal(out=st_rd.ap[:, i * 6:(i + 1) * 6], in_=a3_[:, :, 64]), reads=[acc], writes=[st_rd])
                    cx.op("dve", lambda: V.tensor_tensor(out=ar.ap[:, :].rearrange("p (c b d) -> p c b d", c=6, b=2)[:, :, i, :],
                                                         in0=a3_[:, :, 0:64],
                                                         in1=st_rd.ap[:, i * 6:(i + 1) * 6].unsqueeze(2).to_broadcast([128, 6, HD]),
                                                         op=ALU.mult), reads=[acc, st_rd], writes=[ar])
                if "arow" in dbg and t == 0:
                    dump("arow", ar, [128, 768])
                return rms_norm(ar, anw, t % 2)

            def stage_b(t, dT):
                dT(t, 0)
                b0, b1 = auxget(1)[0], auxget(1)[0]
                for half, bk in ((0, b0), (1, b1)):
                    for c in range(12):
                        cx.op("pe", lambda: T.matmul(bk.ap[:, :], lhsT=m3[:, c, t * 128:(t + 1) * 128], rhs=wo3[:, c, half * 512:(half + 1) * 512],
                                                     start=(c == 0), stop=(c == 11)), reads=[mixT, wout_b], writes=[bk], sig=(c == 11))
                h = hA[t]
                for half, bk in ((0, b0), (1, b1)):
                    cx.op("dve", lambda: V.tensor_tensor(out=h.ap[:, half * 512:(half + 1) * 512], in0=h.ap[:, half * 512:(half + 1) * 512],
                                                         in1=bk.ap[:, :], op=ALU.add), reads=[h, bk], writes=[h])
                if "mixres" in dbg and t == 0:
                    dump("mixres", h, [128, D])
                layer_norm(t, False, lnexp=True)

            dTs = {}
            for step in range(NT + 2):
                if step < NT:
                    dTs[step] = stage_a(step)
                    chk("mix_attn")
                if 0 <= step - 2 < NT:
                    make_hT(step - 2)
                if 0 <= step - 1 < NT:
                    stage_b(step - 1, dTs.pop(step - 1))
            Pool.lo, Pool.hi, Pool.nxt = 0, 8, 0
            Pool.split = False

        nunits = nseq * UPS if nunits_override is None else nunits_override
        if STOP == "prologue" or (STOP and (STOP == "const" or STOP.startswith("cast"))):
            nunits = 0
        NUNITS[0] = nunits
        for u in range(nunits):
            tok0 = u * TU
            if u % UPS == 0:
                rotary_tables(u // UPS)
                if u == 0:
                    dump("cost", cost, [128, 16, 8])
                    dump("sint", sint, [128, 16, 8])
            if u == 0:
                for t in range(NT):
                    load_x(0, t)
                    x_to_hT(t)
            for t in range(NT):
                x_to_hA(t)
            if STOP == "x":
                break
            ffn_phase(0, u, False)
            if u == 0:
                dump("h1", (hA_t[:, :, :], [hh_ for hh_ in hA]), [128, NT, D])
            if STOP == "ffn1":
                break
            try:
                mixer_phase(u)
            except _StopNow:
                Pool.lo, Pool.hi, Pool.nxt = 0, 8, 0
                Pool.split = False
                break
            if u == 0:
                dump("h2", (hA_t[:, :, :], [hh_ for hh_ in hA]), [128, NT, D])
            if STOP == "mixer":
                break
            ffn_phase(1, u, True)

        cx.wait_tokens("act", [(h.chan2.sem, 16 * h.chan2.n) for h in hA])
        cx.wait_tokens("sp", [(c.sem, 16 * c.n) for c in cx.chans if c.n > 0])
    return nc, sorted(dbg_out)


def _consts():
    s = np.arange(128)[:, None]
    t = np.arange(128)[None, :]

    def cb(b):
        delta = 128 * b + t - s
        c = ((delta <= 128).astype(np.float32) + ((delta % 4 == 0) & (delta <= 512)).astype(np.float32)
             + (delta % 16 == 0).astype(np.float32))
        return c * (delta >= 0)

    masks = np.zeros((128, 9 * 128), np.float32)
    for i, b in enumerate((8, 7, 6, 5, 4, 3, 2, 1, 0)):
        masks[:, i * 128:(i + 1) * 128] = cb(b)
    k_ = np.arange(128)[:, None]
    l_ = np.arange(128)[None, :]
    U = (k_ <= l_).astype(np.float32)
    SL = (k_ > l_).astype(np.float32)
    neg = np.where(l_ >= k_, 0.0, NEG).astype(np.float32)
    invf = (np.float32(500000.0) ** (-np.arange(0, 16, 2, dtype=np.float32) / np.float32(16))).astype(np.float32)
    return {
        "c_masks": masks.astype(ml_dtypes.bfloat16),
        "c_identb": np.eye(128, dtype=np.float32).astype(ml_dtypes.bfloat16),
        "c_identf": np.eye(128, dtype=np.float32),
        "c_U": U, "c_SL": SL, "c_ones": np.ones((128, 128), np.float32), "c_negb": neg.astype(ml_dtypes.bfloat16),
        "c_invf": invf.reshape(1, 8),
    }


def make_in_maps(inputs, ncore, nseq):
    f = lambda a: np.ascontiguousarray(np.asarray(a, dtype=np.float32))
    shared = {
        "wg1": f(inputs["ffn1_gate"][0]), "wu1": f(inputs["ffn1_up"][0]), "wd1": f(inputs["ffn1_down"][0]),
        "wg2": f(inputs["ffn2_gate"][0]), "wu2": f(inputs["ffn2_up"][0]), "wd2": f(inputs["ffn2_down"][0]),
        "w_in": f(inputs["w_in"][0]), "w_out": f(inputs["w_out"][0]),
        "ln1_g": f(inputs["ln1_g"]), "ln1_b": f(inputs["ln1_b"]),
        "ln2_g": f(inputs["ln2_g"]), "ln2_b": f(inputs["ln2_b"]),
        "ln3_g": f(inputs["ln3_g"]), "ln3_b": f(inputs["ln3_b"]),
        "conv_w_l": f(np.asarray(inputs["conv_w"][0]).T.reshape(14, 128, 4).transpose(1, 0, 2)),
        "conv_b_l": f(np.asarray(inputs["conv_b"][0]).reshape(14, 128).T),
        "dt_bias": f(inputs["dt_bias"]), "a_log": f(inputs["a_log"]), "d_skip": f(inputs["d_skip"]),
        "attn_norm_w": f(inputs["attn_norm_w"]), "ssd_norm_w": f(inputs["ssd_norm_w"]),
    }
    shared.update(_consts())
    x = np.asarray(inputs["x"], dtype=np.float32)
    pos = np.asarray(inputs["positions"]).astype(np.int32)
    maps = []
    for c in range(ncore):
        m = dict(shared)
        m["x"] = np.ascontiguousarray(x[c * nseq:(c + 1) * nseq].reshape(nseq * SEQ, D))
        m["pos"] = np.ascontiguousarray(pos[c * nseq:(c + 1) * nseq].reshape(nseq * 16, 128))
        maps.append(m)
    return maps


_CACHE = {}


def kernel(**inputs):
    x = np.asarray(inputs["x"])
    B = x.shape[0]
    nseq = B // NCORE
    if nseq not in _CACHE:
        _CACHE[nseq] = build(nseq)[0]
    nc = _CACHE[nseq]
    maps = make_in_maps(inputs, NCORE, nseq)
    res = run_bass_kernel_spmd(nc, maps, core_ids=list(range(NCORE)))
    outs = [np.asarray(r["out"], dtype=np.float32).reshape(nseq, SEQ, D) for r in res.results]
    return np.concatenate(outs, axis=0)
```

```python
import contextlib
import numpy as np
import ml_dtypes
import concourse.bass as bass
import concourse.mybir as mybir
from concourse.bass_utils import run_bass_kernel_spmd

F32 = mybir.dt.float32
BF16 = mybir.dt.bfloat16
I32 = mybir.dt.int32
AF = mybir.ActivationFunctionType
ALU = mybir.AluOpType

D = 1024
DFF = 2816
NFC = 22
SEQ = 2048
H = 12
HD = 64
NG = 4
DIN = 4876
DMIX = 1536
TU = 512
NT = 4
UPS = SEQ // TU
NCORE = 8
ALPHA = float(2.0 ** 0.25)
LN_EPS = 1e-5
RMS_EPS = 1e-6
NEG = -30000.0
MSTART = {0: 5, 1: 1, 2: 0}

SEG_QKV = [(0, 512), (512, 512), (1024, 512), (1536, 512), (2048, 256)]
SEG_Z = [(2304, 512), (2816, 256)]
SEG_X = [(3072, 512), (3584, 512), (4096, 512), (4608, 256)]
COL_DT = 4864

ARENA = 110592
BLK = 256


class _StopNow(Exception):
    pass


class Trk:
    __slots__ = ("w", "r", "excl")

    def __init__(self, excl=False):
        self.w = {}
        self.r = {}
        self.excl = excl


class Chan:
    __slots__ = ("sem", "n")

    def __init__(self, sem):
        self.sem = sem
        self.n = 0


class Buf:
    __slots__ = ("ap", "trks", "chan", "chan2")

    def __init__(self, ap, trks, chan=None, chan2=None):
        self.ap = ap
        self.trks = trks
        self.chan = chan
        self.chan2 = chan2


def _trks(lst):
    out = []
    for b in lst:
        if isinstance(b, Trk):
            out.append(b)
        else:
            out.extend(b.trks)
    return out


class Ctx:
    def __init__(self, nc, es):
        self.nc = nc
        self.es = es
        self.eng = {}
        for name, h in (("pe", nc.tensor), ("act", nc.scalar), ("dve", nc.vector),
                        ("pool", nc.gpsimd), ("sp", nc.sync)):
            sem = es.enter_context(nc.semaphore("sem_" + name))
            self.eng[name] = {"h": h, "sem": sem, "cnt": 0, "known": {}}
        self.nchan = 0
        self.chans = []

    def chan(self):
        self.nchan += 1
        c = Chan(self.es.enter_context(self.nc.semaphore("dch%d" % self.nchan)))
        self.chans.append(c)
        return c

    def _need(self, e, reads, writes):
        need = {}

        def add(d, skip_self):
            for k, (sem, thr) in d.items():
                if skip_self and sem is e["sem"]:
                    continue
                if need.get(k, (None, 0))[1] < thr:
                    need[k] = (sem, thr)

        for t in reads:
            add(t.w, False)
            if t.excl:
                add(t.r, True)
        for t in writes:
            add(t.r, True)
            add(t.w, True)
        return need

    def _emit_waits(self, e, need):
        for k, (sem, thr) in need.items():
            if e["known"].get(k, 0) < thr:
                e["h"].wait_ge(sem, thr)
                e["known"][k] = thr

    def _commit(self, tok, reads, writes):
        k = id(tok[0])
        for t in reads:
            if t.r.get(k, (None, 0))[1] < tok[1]:
                t.r[k] = tok
        for t in writes:
            t.w = {k: tok}
            t.r = {}

    def op(self, en, fn, reads=(), writes=(), sig=True):
        e = self.eng[en]
        reads = _trks(reads)
        writes = _trks(writes)
        self._emit_waits(e, self._need(e, reads, writes))
        inst = fn()
        if sig:
            e["cnt"] += 1
            inst.then_inc(e["sem"], 1)
            tok = (e["sem"], e["cnt"])
        else:
            assert en == "pe"
            tok = (e["sem"], e["cnt"] + 1)
        self._commit(tok, reads, writes)
        return tok

    def dma(self, qn, out_ap, in_ap, chan, reads=(), writes=(), **kw):
        e = self.eng[qn]
        reads = _trks(reads)
        writes = _trks(writes)
        self._emit_waits(e, self._need(e, reads, writes))
        chan.n += 1
        e["h"].dma_start(out=out_ap, in_=in_ap, **kw).then_inc(chan.sem, 16)
        tok = (chan.sem, 16 * chan.n)
        self._commit(tok, reads, writes)
        return tok

    def wait_tokens(self, en, toks):
        e = self.eng[en]
        need = {}
        for sem, thr in toks:
            k = id(sem)
            if need.get(k, (None, 0))[1] < thr:
                need[k] = (sem, thr)
        self._emit_waits(e, need)


def build(nseq=4, dbg=(), stop=None, nunits_override=None):
    nc = bass.Bass("TRN2", target_bir_lowering=False)
    ntok = nseq * SEQ
    dbg = set(dbg)
    dbg_out = {}

    def din(name, shape, dt=F32):
        return nc.dram_tensor(name, list(shape), dt, kind="ExternalInput").ap()

    x_d = din("x", [ntok, D])
    pos_d = din("pos", [nseq * 16, 128], I32)
    wg_d = [din("wg1", [D, DFF]), din("wg2", [D, DFF])]
    wu_d = [din("wu1", [D, DFF]), din("wu2", [D, DFF])]
    wd_d = [din("wd1", [DFF, D]), din("wd2", [DFF, D])]
    win_d = din("w_in", [D, DIN])
    wout_d = din("w_out", [DMIX, D])
    lng_d = [din("ln%d_g" % i, [1, D]) for i in (1, 2, 3)]
    lnb_d = [din("ln%d_b" % i, [1, D]) for i in (1, 2, 3)]
    convw_d = din("conv_w_l", [128, 14, 4])
    convb_d = din("conv_b_l", [128, 14])
    dtb_d = din("dt_bias", [1, H])
    alog_d = din("a_log", [1, H])
    dsk_d = din("d_skip", [1, H])
    anw_d = din("attn_norm_w", [1, 768])
    snw_d = din("ssd_norm_w", [1, 768])
    masks_d = din("c_masks", [128, 9 * 128], BF16)
    identb_d = din("c_identb", [128, 128], BF16)
    identf_d = din("c_identf", [128, 128])
    cU_d = din("c_U", [128, 128])
    cSL_d = din("c_SL", [128, 128])
    cones_d = din("c_ones", [128, 128])
    cneg_d = din("c_negb", [128, 128], BF16)
    invf_d = din("c_invf", [1, 8])
    out_d = nc.dram_tensor("out", [ntok, D], F32, kind="ExternalOutput").ap()

    def dscr(name, shape):
        return nc.dram_tensor(name, list(shape), BF16, kind="Internal").ap()

    sg_s = [dscr("s_wg1", [D, DFF]), dscr("s_wg2", [D, DFF])]
    su_s = [dscr("s_wu1", [D, DFF]), dscr("s_wu2", [D, DFF])]
    sd_s = [dscr("s_wd1", [DFF, D]), dscr("s_wd2", [DFF, D])]
    sin_s = dscr("s_win", [D, DIN])
    sout_s = dscr("s_wout", [DMIX, D])

    with contextlib.ExitStack() as es:
        cx = Ctx(nc, es)

        def sb(name, shape, dt):
            return es.enter_context(nc.sbuf_tensor(name, list(shape), dt))

        def mk(name, shape, dt, chan=False):
            t = sb(name, shape, dt)
            return Buf(t, [Trk()], cx.chan() if chan else None)

        kT = mk("kT", [128, 6, SEQ], BF16)
        Vp = mk("Vp", [128, 16, H * 65], BF16)
        hstate = mk("hstate", [128, 768], F32)
        prevb = mk("prevb", [128, 768], BF16)
        tail = mk("tail", [128, 14, 3], F32)
        masks = mk("masks", [128, 9 * 128], BF16, True)
        identb = mk("identb", [128, 128], BF16, True)
        identf = mk("identf", [128, 128], F32, True)
        cU = mk("cU", [128, 128], F32, True)
        cUb = mk("cUb", [128, 128], BF16)
        cSL = mk("cSL", [128, 128], F32, True)
        cones = mk("cones", [128, 128], F32, True)
        cnegb = mk("cnegb", [128, 128], BF16, True)
        invf = mk("invf", [128, 8], F32, True)
        cost = mk("cost", [128, 16, 8], F32)
        sint = mk("sint", [128, 16, 8], F32)
        lng = mk("lng", [128, D], F32, True)
        lnb = mk("lnb", [128, D], F32, True)
        snw = mk("snw", [128, 768], F32, True)
        convw = mk("convw", [128, 14, 4], F32, True)
        convb = mk("convb", [128, 14], F32, True)
        dtb = mk("dtb", [128, H], F32, True)
        aneg = mk("aneg", [128, H], F32, True)
        dsk = mk("dsk", [128, H], F32, True)
        hA_t = sb("hA", [128, NT, D], F32)
        hA = [Buf(hA_t[:, t, :], [Trk()], cx.chan(), cx.chan()) for t in range(NT)]
        hT_h = sb("hT", [128, 8, TU], BF16)
        hT_trk = [Trk() for _ in range(NT)]
        hT = Buf(hT_h, hT_trk)
        hT_t = [Buf(hT_h, [hT_trk[t]]) for t in range(NT)]
        sgt1 = mk("sgt1", [128, 512], F32)
        st_bn = mk("st_bn", [128, 2, 6], F32)
        st_mv = mk("st_mv", [128, 2], F32)
        st_sd = mk("st_sd", [128, 1], F32)
        st_rs = mk("st_rs", [128, 1], F32)
        st_ss = mk("st_ss", [128, 1], F32)
        st_rd = mk("st_rd", [128, H], F32)

        arena_t = sb("arena", [128, ARENA // 4], F32)
        blocks = [Trk() for _ in range(ARENA // BLK)]

        def av(off, nbytes, dt, chan=False, **rr):
            assert off % 4 == 0 and nbytes % 4 == 0 and off + nbytes <= ARENA
            ap = arena_t[:, off // 4:(off + nbytes) // 4]
            if dt is not F32:
                ap = ap.bitcast(dt)
            trks = blocks[off // BLK:(off + nbytes - 1) // BLK + 1]
            return Buf(ap, trks, cx.chan() if chan else None)

        win_slot = [av(0, 8192, BF16, True), av(8192, 8192, BF16, True)]
        A_AT = 16384
        aT = av(A_AT, 22528, BF16)
        sgt = [av(A_AT + 22528, 2048, F32), sgt1]
        wout_b = av(A_AT, 24576, BF16, True)
        A_WGU = 40960
        wg_slot = [av(A_WGU + s * 8192, 4096, BF16, True) for s in range(3)]
        wu_slot = [av(A_WGU + s * 8192 + 4096, 4096, BF16, True) for s in range(3)]
        A_WD = 65536
        wd_grp = [av(A_WD + g * 4096, 4096, BF16, True) for g in range(11)]
        o = A_WGU
        zs = av(o, 6144, BF16); o += 6144
        xs_tok = av(o, 6144, BF16); o += 6144
        B_tok = av(o, 4096, BF16); o += 4096
        BTb = av(o, 4096, BF16); o += 4096
        CTb = av(o, 4096, BF16); o += 4096
        assert o == A_WD
        pc = [av(o + i * 2304, 2060, F32) for i in range(2)]; o += 4608
        cacc = [av(o + i * 2048, 2048, F32) for i in range(2)]; o += 4096
        cout = [av(o + i * 1024, 1024, BF16) for i in range(2)]; o += 2048
        xdt = av(o, 1536, BF16); o += 1536
        xdtd = av(o, 1536, BF16); o += 1536
        LT = [av(o + i * 1536, 1536, F32) for i in range(4)]; o += 6144
        MT = [av(o + i * 768, 768, BF16) for i in range(4)]; o += 3072
        yrow = [av(o, 3072, F32)] * 2; o += 3072
        ytmp = [av(o, 768, F32)] * 2; o += 768
        wdt_b = av(o, 192, BF16, True); o += 256
        dtv = av(o, 192, F32); o += 256
        dAb = av(o, 192, F32); o += 256
        Eb = av(o, 576, F32); o += 768
        nac = av(o, 192, F32); o += 192
        ddb = av(o, 48, F32); o += 64
        draw = av(o, 48, F32); o += 128
        dAhi = av(o, 96, BF16); o += 256
        dAlo = av(o, 96, BF16); o += 256
        dAtmp = av(o, 192, F32); o += 256
        assert o <= 95232
        o = A_WD
        qT = av(o, 6144, BF16); o += 6144
        qkst = [av(o + i * 1024, 1024, BF16) for i in range(2)]; o += 2048
        rtmp = [av(o + i * 256, 256, F32) for i in range(4)]; o += 1024
        PT = [av(o + i * 1024, 1024, BF16) for i in range(4)] + [av(88064 + i * 1024, 1024, BF16) for i in range(6)]; o += 4096
        arow = [av(o + i * 3072, 3072, F32) for i in range(2)]; o += 6144
        anw = av(84992, 3072, F32, True)
        assert o <= 84992
        mixT = av(95232, 12288, BF16)
        mixn = [av(107520 + i * 1536, 1536, BF16) for i in range(2)]
        stg_f = [av(i * 19712, 19504, F32, True) for i in range(3)]
        stg_b = [av(59392 + i * 9984, 9752, BF16, True) for i in range(3)]

        ps_t = [es.enter_context(nc.psum_tensor("ps%d" % i, [128, 512], F32)) for i in range(8)]
        ps = [Buf(ps_t[i], [Trk(excl=True)]) for i in range(8)]

        class Pool:
            lo, hi, nxt = 0, 8, 0
            split = False

        class Aux:
            lo, hi, nxt = 4, 6, 4

        def auxget(n=1):
            if not Pool.split:
                return psget(n)
            p = Aux.nxt
            if p + n > Aux.hi:
                p = Aux.lo
            Aux.nxt = p + n
            if Aux.nxt >= Aux.hi:
                Aux.nxt = Aux.lo
            return [ps[p + i] for i in range(n)]

        def psget(n=1):
            p = Pool.nxt
            if p % n:
                p += n - p % n
            if p + n > Pool.hi:
                p = Pool.lo
            Pool.nxt = p + n
            if Pool.nxt >= Pool.hi:
                Pool.nxt = Pool.lo
            return [ps[p + i] for i in range(n)]

        V = nc.vector
        A = nc.scalar
        G = nc.gpsimd
        T = nc.tensor

        def dump(name, buf, shape, dt=F32):
            if name not in dbg:
                return
            d = nc.dram_tensor("dbg_" + name, list(shape), dt, kind="ExternalOutput").ap()
            ch = cx.chan()
            tok = cx.dma("sp", d, buf.ap[:] if isinstance(buf, Buf) else buf[0], ch,
                         reads=[buf] if isinstance(buf, Buf) else buf[1])
            cx.wait_tokens("sp", [tok])
            dbg_out[name] = None

        def cload(buf, src):
            cx.dma("sp", buf.ap[:], src, buf.chan, writes=[buf])

        cload(masks, masks_d)
        cload(identb, identb_d)
        cload(identf, identf_d)
        cload(cU, cU_d)
        cload(cSL, cSL_d)
        cload(cones, cones_d)
        cload(cnegb, cneg_d)
        cload(invf, invf_d[0:1, :].partition_broadcast(128))
        cload(snw, snw_d[0:1, :].partition_broadcast(128))
        cload(convw, convw_d)
        cload(convb, convb_d)
        cload(dtb, dtb_d[0:1, :].partition_broadcast(128))
        cload(aneg, alog_d[0:1, :].partition_broadcast(128))
        cload(dsk, dsk_d[0:1, :].partition_broadcast(128))
        cx.op("act", lambda: A.activation(out=aneg.ap[:], in_=aneg.ap[:], func=AF.Exp), reads=[aneg], writes=[aneg])
        cx.op("dve", lambda: V.tensor_scalar(out=aneg.ap[:], in0=aneg.ap[:], scalar1=-1.0, scalar2=None, op0=ALU.mult),
              reads=[aneg], writes=[aneg])
        cx.op("dve", lambda: V.tensor_copy(out=cUb.ap[:], in_=cU.ap[:]), reads=[cU], writes=[cUb])
        cx.op("pool", lambda: G.memset(Vp.ap[:], 1.0), writes=[Vp])

        RB = 95232
        nti = 16
        posi = av(RB, 512, I32, True)
        posf = av(RB + 512, 512, F32)
        post = av(RB + 1024, 64, F32)
        ang = av(RB + 1088, 512, F32)
        kint = av(RB + 1600, 512, I32)
        kf = av(RB + 2112, 512, F32)
        tgt = av(RB + 2624, 512, F32)
        TWO_PI = 2.0 * np.pi
        C1 = 6.28125
        C2 = float(TWO_PI - 6.28125)

        def sin_table(dst, shift):
            cx.op("dve", lambda: V.tensor_scalar(out=kint.ap[:, :], in0=ang.ap[:, :], scalar1=float(shift),
                                                 scalar2=float(1.0 / TWO_PI), op0=ALU.add, op1=ALU.mult),
                  reads=[ang], writes=[kint])
            cx.op("dve", lambda: V.tensor_copy(out=kf.ap[:, :], in_=kint.ap[:, :]), reads=[kint], writes=[kf])
            cx.op("dve", lambda: V.scalar_tensor_tensor(out=tgt.ap[:, :], in0=kf.ap[:, :], scalar=-C1, in1=ang.ap[:, :],
                                                        op0=ALU.mult, op1=ALU.add), reads=[kf, ang], writes=[tgt])
            cx.op("dve", lambda: V.scalar_tensor_tensor(out=tgt.ap[:, :], in0=kf.ap[:, :], scalar=-C2, in1=tgt.ap[:, :],
                                                        op0=ALU.mult, op1=ALU.add), reads=[kf, tgt], writes=[tgt])
            if shift:
                cx.op("dve", lambda: V.tensor_scalar(out=tgt.ap[:, :], in0=tgt.ap[:, :], scalar1=float(shift), scalar2=None,
                                                     op0=ALU.add), reads=[tgt], writes=[tgt])
            cx.op("dve", lambda: V.tensor_scalar(out=kf.ap[:, :], in0=tgt.ap[:, :], scalar1=float(np.pi), scalar2=float(-TWO_PI),
                                                 op0=ALU.is_gt, op1=ALU.mult), reads=[tgt], writes=[kf])
            cx.op("dve", lambda: V.tensor_tensor(out=tgt.ap[:, :], in0=tgt.ap[:, :], in1=kf.ap[:, :], op=ALU.add),
                  reads=[tgt, kf], writes=[tgt])
            cx.op("dve", lambda: V.tensor_scalar(out=tgt.ap[:, :], in0=tgt.ap[:, :], scalar1=float(-np.pi), scalar2=float(np.pi),
                                                 op0=ALU.max, op1=ALU.min), reads=[tgt], writes=[tgt])
            cx.op("act", lambda: A.activation(out=dst.ap[:, :, :].rearrange("p t f -> p (t f)"), in_=tgt.ap[:, :], func=AF.Sin),
                  reads=[tgt], writes=[dst])

        def rotary_tables(sq):
            cx.dma("sp", posi.ap[0:nti, :], pos_d[sq * 16:(sq + 1) * 16, :], posi.chan, writes=[posi])
            cx.op("dve", lambda: V.tensor_copy(out=posf.ap[0:nti, :], in_=posi.ap[0:nti, :]), reads=[posi], writes=[posf])
            bk = psget(1)[0]
            cx.op("pe", lambda: T.transpose(bk.ap[:, 0:nti], posf.ap[0:nti, :], identf.ap[0:nti, 0:nti]),
                  reads=[posf, identf], writes=[bk])
            cx.op("dve", lambda: V.tensor_copy(out=post.ap[:, :], in_=bk.ap[:, 0:nti]), reads=[bk], writes=[post])
            cx.op("dve", lambda: V.tensor_tensor(out=ang.ap[:, :].rearrange("p (t f) -> p t f", f=8),
                                                 in0=post.ap[:, :].unsqueeze(2).to_broadcast([128, nti, 8]),
                                                 in1=invf.ap[:, :].unsqueeze(1).to_broadcast([128, nti, 8]), op=ALU.mult),
                  reads=[post, invf], writes=[ang])
            sin_table(sint, 0.0)
            sin_table(cost, np.pi / 2)

        store_toks = []
        rr = [0]
        cast_eng = ["dve", "act"]

        NSTG = 9
        PCW = 2048
        stg_f = [av(i * 12288, 8192, F32, True) for i in range(NSTG)]
        stg_b = [av(i * 12288 + 8192, 4096, BF16, True) for i in range(NSTG)]

        def cast_matrix(src, dst, rows, cols):
            for rc in range(rows // 128):
                for c0 in range(0, cols, PCW):
                    n = min(PCW, cols - c0)
                    i = rr[0] % NSTG
                    e = cast_eng[rr[0] % len(cast_eng)]
                    rr[0] += 1
                    f, b = stg_f[i], stg_b[i]
                    cx.dma("sp", f.ap[:, 0:n], src[rc * 128:(rc + 1) * 128, c0:c0 + n], f.chan, writes=[f])
                    if e == "dve":
                        cx.op("dve", lambda: V.tensor_copy(out=b.ap[:, 0:n], in_=f.ap[:, 0:n]), reads=[f], writes=[b])
                    elif e == "act":
                        cx.op("act", lambda: A.copy(out=b.ap[:, 0:n], in_=f.ap[:, 0:n]), reads=[f], writes=[b])
                    else:
                        cx.op("pool", lambda: G.tensor_copy(out=b.ap[:, 0:n], in_=f.ap[:, 0:n]), reads=[f], writes=[b])
                    store_toks.append(cx.dma("act", dst[rc * 128:(rc + 1) * 128, c0:c0 + n], b.ap[:, 0:n], b.chan, reads=[b]))

        jobs = [(wg_d[0], sg_s[0], D, DFF), (wu_d[0], su_s[0], D, DFF), (wd_d[0], sd_s[0], DFF, D), (win_d, sin_s, D, DIN),
                (wout_d, sout_s, DMIX, D), (wg_d[1], sg_s[1], D, DFF), (wu_d[1], su_s[1], D, DFF), (wd_d[1], sd_s[1], DFF, D)]
        if stop == "const":
            jobs = []
        if stop and stop.startswith("cast:"):
            cast_eng = stop[5:].split(",")
            jobs = jobs[:1]
        if stop and stop.startswith("castn:"):
            sel = [int(v) for v in stop[6:].split(",")]
            jobs = [jobs[i] for i in sel]
        for jb in jobs:
            cast_matrix(*jb)
        cx.wait_tokens("sp", store_toks)
        STOP = stop

        def load_ln(k, final):
            cx.dma("sp", lng.ap[:], lng_d[k][0:1, :].partition_broadcast(128), lng.chan, writes=[lng])
            cx.dma("sp", lnb.ap[:], lnb_d[k][0:1, :].partition_broadcast(128), lnb.chan, writes=[lnb])
            if not final:
                cx.op("act", lambda: A.mul(out=lnb.ap[:], in_=lnb.ap[:], mul=ALPHA), reads=[lnb], writes=[lnb])

        def make_hT(t, src=None, scale=1.0 / ALPHA):
            h = hA[t] if src is None else src
            for half in range(2):
                bk = auxget(1)[0]
                for j in range(4):
                    c = half * 4 + j
                    cx.op("pe", lambda: T.transpose(bk.ap[:, j * 128:(j + 1) * 128], h.ap[:, c * 128:(c + 1) * 128], identf.ap[:]),
                          reads=[h, identf], writes=[bk], sig=(j == 3))
                cx.op("act", lambda: A.activation(out=hT.ap[:, half * 4:half * 4 + 4, t * 128:(t + 1) * 128],
                                                  in_=bk.ap[:, :].rearrange("p (c k) -> p c k", c=4),
                                                  func=AF.Copy, scale=float(scale)),
                      reads=[bk], writes=[hT_t[t]])

        def layer_norm(t, final, lnexp=False):
            h = hA[t]
            for i in range(2):
                cx.op("dve", lambda: V.bn_stats(out=st_bn.ap[:, i, :], in_=h.ap[:, i * 512:(i + 1) * 512]),
                      reads=[h], writes=[st_bn])
            cx.op("dve", lambda: V.bn_aggr(out=st_mv.ap[:], in_=st_bn.ap[:, :, :].rearrange("p a b -> p (a b)")),
                  reads=[st_bn], writes=[st_mv])
            sc = 1.0 if final else 1.0 / (ALPHA * ALPHA)
            if lnexp:
                cx.op("act", lambda: A.activation(out=st_sd.ap[:], in_=st_mv.ap[:, 1:2], func=AF.Ln, scale=float(sc),
                                                  bias=float(LN_EPS * sc)), reads=[st_mv], writes=[st_sd])
                cx.op("act", lambda: A.activation(out=st_rs.ap[:], in_=st_sd.ap[:], func=AF.Exp, scale=-0.5), reads=[st_sd], writes=[st_rs])
            else:
                cx.op("act", lambda: A.activation(out=st_sd.ap[:], in_=st_mv.ap[:, 1:2], func=AF.Sqrt, scale=float(sc),
                                                  bias=float(LN_EPS * sc)), reads=[st_mv], writes=[st_sd])
                cx.op("dve", lambda: V.reciprocal(out=st_rs.ap[:], in_=st_sd.ap[:]), reads=[st_sd], writes=[st_rs])
            cx.op("dve", lambda: V.scalar_tensor_tensor(out=h.ap[:, :], in0=h.ap[:, :], scalar=st_mv.ap[:, 0:1], in1=lng.ap[:, :],
                                                        op0=ALU.subtract, op1=ALU.mult), reads=[h, st_mv, lng], writes=[h])
            cx.op("dve", lambda: V.scalar_tensor_tensor(out=h.ap[:, :], in0=h.ap[:, :], scalar=st_rs.ap[:, 0:1], in1=lnb.ap[:, :],
                                                        op0=ALU.mult, op1=ALU.add), reads=[h, st_rs, lnb], writes=[h])

        slot_rr = [0]
        NUNITS = [0]

        xst = [av(t * 4096, 4096, F32, True) for t in range(NT)]

        def load_x(u, t):
            tk = u * TU
            cx.dma("sp", xst[t].ap[:, :], x_d[tk + t * 128:tk + (t + 1) * 128, :], xst[t].chan, writes=[xst[t]])

        def x_to_hT(t):
            make_hT(t, src=xst[t], scale=1.0)

        def x_to_hA(t):
            h = hA[t]
            cx.op("act", lambda: A.mul(out=h.ap[:, :], in_=xst[t].ap[:, :], mul=ALPHA), reads=[xst[t]], writes=[h])

        def ffn_phase(k, u, final, pre_final=None):
            tok0 = u * TU
            sg_, su_, sd_ = sg_s[k], su_s[k], sd_s[k]
            state = {"nld": 0}

            def load_gu(fcg):
                s = slot_rr[0] % 3
                slot_rr[0] += 1
                c0 = fcg * 256
                cx.dma("sp", wg_slot[s].ap[:, :].rearrange("p (c n) -> p c n", c=8),
                       sg_[:, c0:c0 + 256].rearrange("(c p) n -> p c n", p=128), wg_slot[s].chan, writes=[wg_slot[s]])
                cx.dma("sp", wu_slot[s].ap[:, :].rearrange("p (c n) -> p c n", c=8),
                       su_[:, c0:c0 + 256].rearrange("(c p) n -> p c n", p=128), wu_slot[s].chan, writes=[wu_slot[s]])
                return s

            def load_d(g):
                cx.dma("sp", wd_grp[g].ap[:, :].rearrange("p (j n) -> p j n", j=2),
                       sd_[g * 256:(g + 1) * 256, :].rearrange("(j p) n -> p j n", p=128), wd_grp[g].chan, writes=[wd_grp[g]])

            slots = {}
            for g in range(3):
                slots[g] = load_gu(g)
            load_ln(0 if k == 0 else 2, final)
            load_d(0)
            early = {}
            EC = (NT - 1) * 128
            if pre_final is not None:
                for fc in range(3):
                    fcg_, j_ = divmod(fc, 2)
                    s_ = slots[fcg_]
                    bg, bu = psget(2)
                    early[fc] = (bg, bu)
                    for bank, wsl in ((bg, wg_slot[s_]), (bu, wu_slot[s_])):
                        w3_ = wsl.ap[:, :].rearrange("p (c n) -> p c n", c=8)
                        for c in range(8):
                            cx.op("pe", lambda: T.matmul(bank.ap[:, 0:EC], lhsT=w3_[:, c, j_ * 128:(j_ + 1) * 128], rhs=hT.ap[:, c, 0:EC],
                                                         start=(c == 0), stop=(c == 7)),
                                  reads=[wsl] + hT_t[0:NT - 1], writes=[bank], sig=(c == 7))
                pre_final()
            for fcg in range(11):
                s = slots[fcg]
                wg3 = wg_slot[s].ap[:, :].rearrange("p (c n) -> p c n", c=8)
                wu3 = wu_slot[s].ap[:, :].rearrange("p (c n) -> p c n", c=8)
                for j in range(2):
                    fc = fcg * 2 + j
                    if fc in early:
                        bg, bu = early[fc]
                        c_lo = EC
                        rd = [hT_t[NT - 1]]
                    else:
                        bg, bu = psget(2)
                        c_lo = 0
                        rd = [hT]
                    for c in range(8):
                        cx.op("pe", lambda: T.matmul(bg.ap[:, c_lo:TU], lhsT=wg3[:, c, j * 128:(j + 1) * 128], rhs=hT.ap[:, c, c_lo:TU],
                                                     start=(c == 0), stop=(c == 7)),
                              reads=[wg_slot[s]] + rd, writes=[bg], sig=(c == 7))
                    for c in range(8):
                        cx.op("pe", lambda: T.matmul(bu.ap[:, c_lo:TU], lhsT=wu3[:, c, j * 128:(j + 1) * 128], rhs=hT.ap[:, c, c_lo:TU],
                                                     start=(c == 0), stop=(c == 7)),
                              reads=[wu_slot[s]] + rd, writes=[bu], sig=(c == 7))
                    sgb = sgt[fc % 2]
                    cx.op("act", lambda: A.activation(out=sgb.ap[:, :], in_=bg.ap[:, :], func=AF.Silu), reads=[bg], writes=[sgb])
                    cx.op("dve", lambda: V.tensor_tensor(out=aT.ap[:, fc * 512:(fc + 1) * 512], in0=sgb.ap[:, :], in1=bu.ap[:, :],
                                                         op=ALU.mult), reads=[sgb, bu], writes=[aT])
                if fcg + 3 < 11:
                    slots[fcg + 3] = load_gu(fcg + 3)
                if fcg + 1 < 11:
                    load_d(fcg + 1)
            pending = None
            if final and u + 1 < NUNITS[0]:
                for t in range(NT):
                    load_x(u + 1, t)
            for t in range(NT):
                b0, b1 = psget(2)
                for half, bk in ((0, b0), (1, b1)):
                    for fc in range(NFC):
                        g = fc // 2
                        wd3 = wd_grp[g].ap[:, :].rearrange("p (j n) -> p j n", j=2)
                        cx.op("pe", lambda: T.matmul(bk.ap[:, :], lhsT=aT.ap[:, fc * 512 + t * 128:fc * 512 + (t + 1) * 128],
                                                     rhs=wd3[:, fc % 2, half * 512:(half + 1) * 512],
                                                     start=(fc == 0), stop=(fc == NFC - 1)),
                              reads=[aT, wd_grp[g]], writes=[bk], sig=(fc == NFC - 1))
                if pending is not None:
                    pending()
                    pending = None
                h = hA[t]
                for half, bk in ((0, b0), (1, b1)):
                    cx.op("dve", lambda: V.scalar_tensor_tensor(out=h.ap[:, half * 512:(half + 1) * 512], in0=bk.ap[:, :], scalar=0.5,
                                                                in1=h.ap[:, half * 512:(half + 1) * 512], op0=ALU.mult, op1=ALU.add),
                          reads=[bk, h], writes=[h])
                layer_norm(t, final)
                if final:
                    cx.dma("act", out_d[tok0 + t * 128:tok0 + (t + 1) * 128, :], h.ap[:, :], h.chan2, reads=[h])
                    if u + 1 < NUNITS[0]:
                        pending = (lambda tt: (lambda: x_to_hT(tt)))(t)
                else:
                    pending = (lambda tt: (lambda: make_hT(tt)))(t)
            if pending is not None:
                pending()

        def load_win(slot, col0, ncols):
            b = win_slot[slot]
            cx.dma("sp", b.ap[:, :].rearrange("p (c n) -> p c n", c=8)[:, :, 0:ncols],
                   sin_s[:, col0:col0 + ncols].rearrange("(c p) n -> p c n", p=128), b.chan, writes=[b])

        def rms_norm(row, wbc, mslot):
            mn = mixn[mslot]
            cx.op("act", lambda: A.activation(out=mn.ap[:, :], in_=row.ap[:, :], func=AF.Square, accum_out=st_ss.ap[:, 0:1]),
                  reads=[row], writes=[mn, st_ss])
            cx.op("act", lambda: A.activation(out=st_sd.ap[:], in_=st_ss.ap[:], func=AF.Ln, scale=float(1.0 / 768.0),
                                              bias=float(RMS_EPS)), reads=[st_ss], writes=[st_sd])
            cx.op("act", lambda: A.activation(out=st_rs.ap[:], in_=st_sd.ap[:], func=AF.Exp, scale=-0.5), reads=[st_sd], writes=[st_rs])
            cx.op("dve", lambda: V.scalar_tensor_tensor(out=mn.ap[:, :], in0=row.ap[:, :], scalar=st_rs.ap[:, 0:1], in1=wbc.ap[:, :],
                                                        op0=ALU.mult, op1=ALU.mult), reads=[row, st_rs, wbc], writes=[mn])

            def do_T(t, c0):
                bk = auxget(1)[0]
                pb = bk.ap[:, :].bitcast(BF16)
                for c in range(6):
                    cx.op("pe", lambda: T.transpose(pb[:, c * 128:(c + 1) * 128], mn.ap[:, c * 128:(c + 1) * 128], identb.ap[:]),
                          reads=[mn, identb], writes=[bk], sig=(c == 5))
                m3 = mixT.ap[:, :].rearrange("p (c k) -> p c k", c=12)
                cx.op("act", lambda: A.copy(out=m3[:, c0:c0 + 6, t * 128:(t + 1) * 128],
                                            in_=pb[:, 0:768].rearrange("p (c k) -> p c k", c=6)), reads=[bk], writes=[mixT])
            return do_T

        def chk(name):
            if STOP == name:
                raise _StopNow()

        def mixer_phase(u, LAST_HT=None):
            q = u % UPS
            sq = u // UPS
            tok0 = u * TU
            m3 = mixT.ap[:, :].rearrange("p (c k) -> p c k", c=12)
            ws = [0]

            def next_seg(col0, ncols):
                s = ws[0] % 2
                ws[0] += 1
                load_win(s, col0, ncols)
                return s

            if q == 0:
                cx.op("pool", lambda: G.memset(hstate.ap[:], 0.0), writes=[hstate])
                cx.op("pool", lambda: G.memset(prevb.ap[:], 0.0), writes=[prevb])
                cx.op("pool", lambda: G.memset(tail.ap[:], 0.0), writes=[tail])

            def proj_tok(slot, ncols, t, bk):
                w3 = win_slot[slot].ap[:, :].rearrange("p (c n) -> p c n", c=8)
                for c in range(8):
                    cx.op("pe", lambda: T.matmul(bk.ap[:, 0:ncols], lhsT=hT.ap[:, c, t * 128:(t + 1) * 128], rhs=w3[:, c, 0:ncols],
                                                 start=(c == 0), stop=(c == 7)),
                          reads=[hT_t[t], win_slot[slot]], writes=[bk], sig=(c == 7))

            z3 = zs.ap[:, :].rearrange("p (t n) -> p t n", t=NT)
            dt3 = dtv.ap[:, :].rearrange("p (t n) -> p t n", t=NT)
            dA3 = dAb.ap[:, :].rearrange("p (t n) -> p t n", t=NT)
            hi3 = dAhi.ap[:, :].rearrange("p (t n) -> p t n", t=NT)
            lo3 = dAlo.ap[:, :].rearrange("p (t n) -> p t n", t=NT)
            wdt3 = wdt_b.ap[:, :].rearrange("p (c n) -> p c n", c=8)
            xs4 = xs_tok.ap[:, :].rearrange("p (t n) -> p t n", t=NT)
            Bt4 = B_tok.ap[:, :].rearrange("p (t n) -> p t n", t=NT)
            BT3 = BTb.ap[:, :].rearrange("p (g k) -> p g k", g=NG)
            CT3 = CTb.ap[:, :].rearrange("p (g k) -> p g k", g=NG)

            def z_step(s_, ncols, t, zoff):
                def f():
                    bk = psget(1)[0]
                    proj_tok(s_(), ncols, t, bk)
                    cx.op("act", lambda: A.activation(out=z3[:, t, zoff:zoff + ncols], in_=bk.ap[:, 0:ncols], func=AF.Silu),
                          reads=[bk], writes=[zs])
                return f

            def dt_step(t):
                def f():
                    if t == 0:
                        cx.dma("sp", wdt_b.ap[:, :].rearrange("p (c n) -> p c n", c=8),
                               sin_s[:, COL_DT:COL_DT + H].rearrange("(c p) n -> p c n", p=128), wdt_b.chan, writes=[wdt_b])
                    bk = psget(1)[0]
                    for c in range(8):
                        cx.op("pe", lambda: T.matmul(bk.ap[:, 0:H], lhsT=hT.ap[:, c, t * 128:(t + 1) * 128], rhs=wdt3[:, c, :],
                                                     start=(c == 0), stop=(c == 7)), reads=[hT_t[t], wdt_b], writes=[bk], sig=(c == 7))
                    cx.op("dve", lambda: V.tensor_tensor(out=draw.ap[:, :], in0=bk.ap[:, 0:H], in1=dtb.ap[:, :], op=ALU.add),
                          reads=[bk, dtb], writes=[draw])
                    cx.op("act", lambda: A.activation(out=draw.ap[:, :], in_=draw.ap[:, :], func=AF.Exp), reads=[draw], writes=[draw])
                    cx.op("act", lambda: A.activation(out=dt3[:, t, :], in_=draw.ap[:, :], func=AF.Ln, bias=1.0), reads=[draw], writes=[dtv])
                    cx.op("dve", lambda: V.tensor_tensor(out=dA3[:, t, :], in0=dt3[:, t, :], in1=aneg.ap[:, :], op=ALU.mult),
                          reads=[dtv, aneg], writes=[dAb])
                    if t == NT - 1:
                        cx.op("dve", lambda: V.tensor_copy(out=dAhi.ap[:, :], in_=dAb.ap[:, :]), reads=[dAb], writes=[dAhi])
                        cx.op("dve", lambda: V.tensor_tensor(out=dAtmp.ap[:, :], in0=dAb.ap[:, :], in1=dAhi.ap[:, :], op=ALU.subtract),
                              reads=[dAb, dAhi], writes=[dAtmp])
                        cx.op("dve", lambda: V.tensor_copy(out=dAlo.ap[:, :], in_=dAtmp.ap[:, :]), reads=[dAtmp], writes=[dAlo])
                return f

            zslot = {}
            extra = []
            zo = 0
            for zi, (c0, ncols) in enumerate(SEG_Z):
                for t in range(NT):
                    extra.append(("z", zi, c0, ncols, t, zo))
                zo += ncols
            for t in range(NT):
                extra.append(("dt", t))

            def run_extra():
                if not extra:
                    return
                e = extra.pop(0)
                if e[0] == "z":
                    _, zi, c0, ncols, t, zo_ = e
                    if zi not in zslot:
                        zslot[zi] = next_seg(c0, ncols)
                    z_step(lambda: zslot[zi], ncols, t, zo_)()
                else:
                    dt_step(e[1])()

            for _ in range(NT):
                e_ = [x for x in extra if x[0] == "dt"][0]
                extra.remove(e_)
                dt_step(e_[1])()
            early = [x for x in extra if x[0] == "z" and x[1] == 0 and x[4] < NT - 1]
            for e_ in early:
                extra.remove(e_)
                extra.insert(0, e_)
            for _ in range(len(early)):
                run_extra()
            ch = 0
            deferred = []

            def conv_T(ch, dst_b, dst):
                def f():
                    bt = psget(1)[0]
                    pb = bt.ap[:, :].bitcast(BF16)
                    for t in range(NT):
                        cx.op("pe", lambda: T.transpose(pb[:, t * 128:(t + 1) * 128], dst[:, t * 128:(t + 1) * 128], identb.ap[:]),
                              reads=[dst_b, identb], writes=[bt], sig=(t == NT - 1))
                    if ch < 6:
                        cx.op("dve", lambda: V.tensor_copy(out=xs4[:, :, ch * 128:(ch + 1) * 128],
                                                           in_=pb[:, 0:512].rearrange("p (t k) -> p t k", t=NT)),
                              reads=[bt], writes=[xs_tok])
                    else:
                        g = ch - 6
                        cx.op("dve", lambda: V.tensor_copy(out=Bt4[:, :, g * 128:(g + 1) * 128],
                                                           in_=pb[:, 0:512].rearrange("p (t k) -> p t k", t=NT)),
                              reads=[bt], writes=[B_tok])
                return f

            for (c0, ncols) in SEG_X:
                s = next_seg(c0, ncols)
                w3 = win_slot[s].ap[:, :].rearrange("p (c n) -> p c n", c=8)
                for lc in range(ncols // 128):
                    bk = psget(1)[0]
                    for c in range(8):
                        cx.op("pe", lambda: T.matmul(bk.ap[:, :], lhsT=w3[:, c, lc * 128:(lc + 1) * 128], rhs=hT.ap[:, c, :],
                                                     start=(c == 0), stop=(c == 7)), reads=[win_slot[s], hT], writes=[bk], sig=(c == 7))
                    if len(deferred) >= 2:
                        deferred.pop(0)()
                    p_ = pc[ch % 2]
                    ca = cacc[ch % 2]
                    cx.op("pool", lambda: G.tensor_copy(out=p_.ap[:, 0:3], in_=tail.ap[:, ch, :]), reads=[tail], writes=[p_])
                    cx.op("act", lambda: A.copy(out=p_.ap[:, 3:515], in_=bk.ap[:, :]), reads=[bk], writes=[p_])
                    cx.op("pool", lambda: G.tensor_copy(out=tail.ap[:, ch, :], in_=p_.ap[:, 512:515]), reads=[p_], writes=[tail])
                    cx.op("act", lambda: A.activation(out=ca.ap[:, :], in_=p_.ap[:, 0:512], func=AF.Copy, scale=convw.ap[:, ch, 0:1]),
                          reads=[p_, convw], writes=[ca])
                    for kk in range(1, 4):
                        cx.op("dve", lambda: V.scalar_tensor_tensor(out=ca.ap[:, :], in0=p_.ap[:, kk:kk + 512],
                                                                    scalar=convw.ap[:, ch, kk:kk + 1], in1=ca.ap[:, :],
                                                                    op0=ALU.mult, op1=ALU.add), reads=[p_, convw, ca], writes=[ca])
                    if ch < 6:
                        dst_b, dst = cout[ch % 2], cout[ch % 2].ap[:, :]
                    elif ch < 10:
                        dst_b, dst = BTb, BT3[:, ch - 6, :]
                    else:
                        dst_b, dst = CTb, CT3[:, ch - 10, :]
                    cx.op("act", lambda: A.activation(out=dst, in_=ca.ap[:, :], func=AF.Silu, bias=convb.ap[:, ch:ch + 1]),
                          reads=[ca, convb], writes=[dst_b])
                    if ch < 10:
                        deferred.append(conv_T(ch, dst_b, dst))
                    ch += 1
                    if extra and extra[0][0] == "z":
                        run_extra()
            while deferred:
                deferred.pop(0)()
            while extra:
                run_extra()
            while deferred:
                deferred.pop(0)()
            assert ch == 14
            chk("mix_conv")
            dump("zs", zs, [128, NT * 768], BF16)
            dump("xs_tok", xs_tok, [128, NT * 768], BF16)
            dump("B_tok", B_tok, [128, NT * 512], BF16)
            dump("CTb", CTb, [128, NG * 512], BF16)
            dump("dtv", dtv, [128, NT * H])
            pre_seg = [next_seg(*SEG_QKV[0]), next_seg(*SEG_QKV[1])]
            cx.dma("sp", wout_b.ap[:, :].rearrange("p (c n) -> p c n", c=12),
                   sout_s[:, :].rearrange("(c p) n -> p c n", p=128), wout_b.chan, writes=[wout_b])
            load_ln(1, False)
            q3 = qT.ap[:, :].rearrange("p (c k) -> p c k", c=6)
            kbase = q * TU
            PQ = []
            seg_slot = {0: pre_seg[0], 1: pre_seg[1]}
            qkv_steps = [(si, c0, ncols, t) for si, (c0, ncols) in enumerate(SEG_QKV) for t in range(NT)]

            def run_qkv():
                if not qkv_steps:
                    return
                si, c0, ncols, t = qkv_steps.pop(0)
                if si not in seg_slot:
                    seg_slot[si] = next_seg(c0, ncols)
                if t == 0 and si + 1 < len(SEG_QKV) and (si + 1) not in seg_slot:
                    seg_slot[si + 1] = next_seg(*SEG_QKV[si + 1])
                s = seg_slot[si]
                gt = q * NT + t
                bk = psget(1)[0]
                proj_tok(s, ncols, t, bk)
                if len(PQ) >= 2:
                    PQ.pop(0)()
                if si < 3:
                    nh = ncols // 64
                    st = qkst[(si * NT + t) % 2]
                    b3 = bk.ap[:, 0:ncols].rearrange("p (h d) -> p h d", h=nh)
                    s3 = st.ap[:, 0:ncols].rearrange("p (h d) -> p h d", h=nh)
                    cb = cost.ap[:, gt, :].unsqueeze(1).to_broadcast([128, nh, 8])
                    sn = sint.ap[:, gt, :].unsqueeze(1).to_broadcast([128, nh, 8])
                    r = [rt.ap[:, :].rearrange("p (h d) -> p h d", h=8)[:, 0:nh, :] for rt in rtmp]
                    cx.op("act", lambda: A.copy(out=st.ap[:, 0:ncols], in_=bk.ap[:, 0:ncols]), reads=[bk], writes=[st])
                    cx.op("dve", lambda: V.tensor_tensor(out=r[0], in0=b3[:, :, 0:8], in1=cb, op=ALU.mult), reads=[bk, cost], writes=[rtmp[0]])
                    cx.op("dve", lambda: V.tensor_tensor(out=r[1], in0=b3[:, :, 8:16], in1=sn, op=ALU.mult), reads=[bk, sint], writes=[rtmp[1]])
                    cx.op("dve", lambda: V.tensor_tensor(out=r[2], in0=b3[:, :, 8:16], in1=cb, op=ALU.mult), reads=[bk, cost], writes=[rtmp[2]])
                    cx.op("dve", lambda: V.tensor_tensor(out=r[3], in0=b3[:, :, 0:8], in1=sn, op=ALU.mult), reads=[bk, sint], writes=[rtmp[3]])
                    cx.op("dve", lambda: V.tensor_tensor(out=s3[:, :, 0:8], in0=r[0], in1=r[1], op=ALU.subtract),
                          reads=[rtmp[0], rtmp[1]], writes=[st])
                    cx.op("dve", lambda: V.tensor_tensor(out=s3[:, :, 8:16], in0=r[2], in1=r[3], op=ALU.add),
                          reads=[rtmp[2], rtmp[3]], writes=[st])

                    def qk_T(st=st, ncols=ncols, c0=c0, t=t):
                        bt = psget(1)[0]
                        pb = bt.ap[:, :].bitcast(BF16)
                        npair = ncols // 128
                        for pi in range(npair):
                            cx.op("pe", lambda: T.transpose(pb[:, pi * 128:(pi + 1) * 128], st.ap[:, pi * 128:(pi + 1) * 128], identb.ap[:]),
                                  reads=[st, identb], writes=[bt], sig=(pi == npair - 1))
                        gp0 = c0 // 128
                        nq = max(0, min(6 - gp0, npair))
                        if nq > 0:
                            cx.op("act", lambda: A.copy(out=q3[:, gp0:gp0 + nq, t * 128:(t + 1) * 128],
                                                        in_=pb[:, 0:nq * 128].rearrange("p (c k) -> p c k", c=nq)),
                                  reads=[bt], writes=[qT])
                        if npair - nq > 0:
                            nk = npair - nq
                            kp0 = gp0 + nq - 6
                            cx.op("act", lambda: A.copy(out=kT.ap[:, kp0:kp0 + nk, kbase + t * 128:kbase + (t + 1) * 128],
                                                        in_=pb[:, nq * 128:npair * 128].rearrange("p (c k) -> p c k", c=nk)),
                                  reads=[bt], writes=[kT])
                    PQ.append(qk_T)
                else:
                    nh = ncols // 64
                    h0 = (c0 - 1536) // 64
                    vv = Vp.ap[:, gt, :].rearrange("p (h e) -> p h e", e=65)
                    cx.op("act", lambda: A.copy(out=vv[:, h0:h0 + nh, 0:64], in_=bk.ap[:, 0:ncols].rearrange("p (h d) -> p h d", h=nh)),
                          reads=[bk], writes=[Vp])

            Pool.lo, Pool.hi, Pool.nxt = 0, 6, 0
            E3 = Eb.ap[:, :]
            bS = [ps[6], ps[7]]
            h3 = hstate.ap[:, :].rearrange("p (h d) -> p h d", h=H)
            pend_T = None
            bE = psget(1)[0]
            for t in range(NT):
                for i, cm in enumerate((cU, cSL, cones)):
                    cx.op("pe", lambda: T.matmul(bE.ap[:, t * 36 + i * H:t * 36 + (i + 1) * H], lhsT=cm.ap[:, :], rhs=dA3[:, t, :],
                                                 start=True, stop=True), reads=[cm, dAb], writes=[bE], sig=(i == 2 and t == NT - 1))
            cx.op("act", lambda: A.activation(out=Eb.ap[:, :], in_=bE.ap[:, 0:NT * 36], func=AF.Exp), reads=[bE], writes=[Eb])
            cx.op("dve", lambda: V.tensor_scalar(out=nac.ap[:, :].rearrange("p (t h) -> p t h", t=NT),
                                                 in0=bE.ap[:, 0:NT * 36].rearrange("p (t k) -> p t k", t=NT)[:, :, 0:H],
                                                 scalar1=-1.0, scalar2=None, op0=ALU.mult), reads=[bE], writes=[nac])
            for t in range(NT):
                tc = slice(t * 128, (t + 1) * 128)
                E3 = Eb.ap[:, t * 36:(t + 1) * 36]
                nact = nac.ap[:, t * H:(t + 1) * H]
                cx.op("dve", lambda: V.tensor_tensor(out=ddb.ap[:, :], in0=dt3[:, t, :], in1=E3[:, H:2 * H], op=ALU.mult),
                      reads=[dtv, Eb], writes=[ddb])
                xs_h = xs4[:, t, :].rearrange("p (h d) -> p h d", h=H)
                cx.op("pool", lambda: G.tensor_tensor(out=xdt.ap[:, :].rearrange("p (h d) -> p h d", h=H), in0=xs_h,
                                                      in1=dt3[:, t, :].unsqueeze(2).to_broadcast([128, H, HD]), op=ALU.mult),
                      reads=[xs_tok, dtv], writes=[xdt])
                cx.op("pool", lambda: G.tensor_tensor(out=xdtd.ap[:, :].rearrange("p (h d) -> p h d", h=H), in0=xs_h,
                                                      in1=ddb.ap[:, :].unsqueeze(2).to_broadcast([128, H, HD]), op=ALU.mult),
                      reads=[xs_tok, ddb], writes=[xdtd])
                yr = yrow[t % 2]
                banks = {}

                def stage1(g):
                    bX = psget(1)[0]
                    banks[g] = bX
                    lt = LT[g]
                    mt = MT[g]
                    lt3 = lt.ap[:, :].rearrange("p (j k) -> p j k", j=3)
                    mt3 = mt.ap[:, :].rearrange("p (j k) -> p j k", j=3)
                    cx.op("pe", lambda: T.matmul(bX.ap[:, 384:512], lhsT=BT3[:, g, tc], rhs=CT3[:, g, tc], start=True, stop=True),
                          reads=[BTb, CTb], writes=[bX], sig=False)
                    for j in range(3):
                        hh = 3 * g + j
                        cx.op("pe", lambda: T.matmul(bX.ap[:, j * 128:(j + 1) * 128], lhsT=hi3[:, t, hh:hh + 1].to_broadcast([128, 128]),
                                                     rhs=cUb.ap[:, :], start=True, stop=False), reads=[dAhi, cUb], writes=[bX], sig=False)
                        cx.op("pe", lambda: T.matmul(bX.ap[:, j * 128:(j + 1) * 128], lhsT=lo3[:, t, hh:hh + 1].to_broadcast([128, 128]),
                                                     rhs=cUb.ap[:, :], start=False, stop=False), reads=[dAlo, cUb], writes=[bX], sig=False)
                        cx.op("pe", lambda: T.matmul(bX.ap[:, j * 128:(j + 1) * 128], lhsT=identb.ap[:, :], rhs=cnegb.ap[:, :],
                                                     start=False, stop=True), reads=[identb, cnegb], writes=[bX], sig=(j == 2))
                    for j in range(3):
                        hh = 3 * g + j
                        cx.op("act", lambda: A.activation(out=lt3[:, j, :], in_=bX.ap[:, j * 128:(j + 1) * 128], func=AF.Exp,
                                                          bias=nact[:, hh:hh + 1]), reads=[bX, nac], writes=[lt])
                    cx.op("dve", lambda: V.tensor_tensor(out=mt3, in0=lt3, in1=bX.ap[:, 384:512].unsqueeze(1).to_broadcast([128, 3, 128]),
                                                         op=ALU.mult), reads=[lt, bX], writes=[mt])

                def stage2(g):
                    bY = psget(1)[0]
                    mt = MT[g]
                    yt_ = ytmp[g % 2]
                    mt3 = mt.ap[:, :].rearrange("p (j k) -> p j k", j=3)
                    for j in range(3):
                        hh = 3 * g + j
                        cx.op("pe", lambda: T.matmul(bY.ap[:, j * 64:(j + 1) * 64], lhsT=mt3[:, j, :],
                                                     rhs=xdt.ap[:, hh * 64:(hh + 1) * 64], start=True, stop=True),
                              reads=[mt, xdt], writes=[bY], sig=False)
                    cx.op("pe", lambda: T.matmul(bY.ap[:, 192:384], lhsT=CT3[:, g, tc], rhs=prevb.ap[:, g * 192:(g + 1) * 192],
                                                 start=True, stop=True), reads=[CTb, prevb], writes=[bY])
                    bSg = bS[g // 2]
                    cx.op("pe", lambda: T.matmul(bSg.ap[:, (g % 2) * 192:(g % 2) * 192 + 192], lhsT=Bt4[:, t, g * 128:(g + 1) * 128],
                                                 rhs=xdtd.ap[:, g * 192:(g + 1) * 192], start=True, stop=True),
                          reads=[B_tok, xdtd], writes=[bSg])
                    yg = yr.ap[:, g * 192:(g + 1) * 192].rearrange("p (j d) -> p j d", j=3)
                    xg = xs4[:, t, g * 192:(g + 1) * 192].rearrange("p (j d) -> p j d", j=3)
                    cx.op("pool", lambda: G.tensor_tensor(out=yg, in0=xg, in1=dsk.ap[:, 3 * g:3 * g + 3].unsqueeze(2).to_broadcast([128, 3, HD]),
                                                          op=ALU.mult), reads=[xs_tok, dsk], writes=[yr])
                    cx.op("dve", lambda: V.tensor_tensor(out=yt_.ap[:, :].rearrange("p (j d) -> p j d", j=3),
                                                         in0=bY.ap[:, 192:384].rearrange("p (j d) -> p j d", j=3),
                                                         in1=E3[:, 3 * g:3 * g + 3].unsqueeze(2).to_broadcast([128, 3, HD]), op=ALU.mult),
                          reads=[bY, Eb], writes=[yt_])
                    cx.op("dve", lambda: V.tensor_tensor(out=yr.ap[:, g * 192:(g + 1) * 192], in0=yr.ap[:, g * 192:(g + 1) * 192],
                                                         in1=yt_.ap[:, :], op=ALU.add), reads=[yr, yt_], writes=[yr])
                    cx.op("dve", lambda: V.tensor_tensor(out=yr.ap[:, g * 192:(g + 1) * 192], in0=yr.ap[:, g * 192:(g + 1) * 192],
                                                         in1=bY.ap[:, 0:192], op=ALU.add), reads=[yr, bY], writes=[yr])

                for g in range(NG):
                    stage1(g)
                run_qkv()
                run_qkv()
                if pend_T is not None:
                    pend_T()
                    pend_T = None
                for g in range(NG):
                    stage2(g)
                run_qkv()
                run_qkv()
                run_qkv()
                if "ypre" in dbg and t == 0:
                    dump("ypre", yr, [128, 768])
                cx.op("dve", lambda: V.tensor_tensor(out=yr.ap[:, :], in0=yr.ap[:, :], in1=z3[:, t, :], op=ALU.mult),
                      reads=[yr, zs], writes=[yr])
                dT = rms_norm(yr, snw, t % 2)
                pend_T = (lambda f, tt: (lambda: f(tt, 6)))(dT, t)
                cx.op("pool", lambda: G.tensor_tensor(out=h3, in0=h3, in1=E3[:, 2 * H:3 * H].unsqueeze(2).to_broadcast([128, H, HD]),
                                                      op=ALU.mult), reads=[hstate, Eb], writes=[hstate])
                for i in range(2):
                    cx.op("dve", lambda: V.tensor_tensor(out=hstate.ap[:, i * 384:(i + 1) * 384], in0=hstate.ap[:, i * 384:(i + 1) * 384],
                                                         in1=bS[i].ap[:, 0:384], op=ALU.add), reads=[hstate, bS[i]], writes=[hstate])
                cx.op("act", lambda: A.copy(out=prevb.ap[:, :], in_=hstate.ap[:, :]), reads=[hstate], writes=[prevb])
            Pool.lo, Pool.hi, Pool.nxt = 0, 8, 0
            chk("mix_ssd")
            cx.dma("sp", anw.ap[:, :], anw_d[0:1, :].partition_broadcast(128), anw.chan, writes=[anw])
            while qkv_steps:
                run_qkv()
            if pend_T is not None:
                pend_T()
                pend_T = None
            while PQ:
                PQ.pop(0)()
            chk("mix_qkv")
            dump("qT", qT, [128, 6 * 512], BF16)

            Pool.lo, Pool.hi, Pool.nxt = 0, 4, 0
            Pool.split = True
            accA, accB = ps[6], ps[7]
            mk3 = masks.ap
            wo3 = wout_b.ap[:, :].rearrange("p (c n) -> p c n", c=12)
            NPT = len(PT)
            LAG = NPT // 2 - 2

            def stage_a(t):
                gt = q * NT + t
                ngrp = gt // 4 + 1
                items = []
                for pair in range(6):
                    for g in range(ngrp):
                        bmax = min(4 * g + 3, gt)
                        nb = bmax - 4 * g + 1
                        js = [gt - b for b in range(bmax, 4 * g - 1, -1)]
                        items.append((pair, g, nb, js))
                stash = {}

                def emit_qk(i):
                    pair, g, nb, js = items[i]
                    bks = [psget(1)[0], psget(1)[0]]
                    for idx, j in enumerate(js):
                        for hb in range(2):
                            bp = hb * 64
                            bk = bks[hb]
                            cx.op("pe", lambda: T.matmul(bk.ap[:, idx * 128:(idx + 1) * 128], lhsT=kT.ap[bp:bp + 64, pair, j * 128:(j + 1) * 128],
                                                         rhs=q3[bp:bp + 64, pair, t * 128:(t + 1) * 128], start=True, stop=True),
                                  reads=[kT, qT], writes=[bk], sig=(idx == nb - 1))
                    ms = MSTART[min(g, 2)]
                    pts = []
                    for hb in range(2):
                        pt = PT[(2 * i + hb) % NPT]
                        bk = bks[hb]
                        cx.op("act", lambda: A.activation(out=pt.ap[:, 0:nb * 128], in_=bk.ap[:, 0:nb * 128], func=AF.Exp, scale=0.125),
                              reads=[bk], writes=[pt])
                        cx.op("dve", lambda: V.tensor_tensor(out=pt.ap[:, 0:nb * 128], in0=pt.ap[:, 0:nb * 128],
                                                             in1=mk3[:, (ms + 4 - nb) * 128:(ms + 4) * 128], op=ALU.mult),
                              reads=[pt, masks], writes=[pt])
                        pts.append(pt)
                    stash[i] = pts

                def emit_pv(i):
                    pair, g, nb, js = items[i]
                    pts = stash.pop(i)
                    for hb in range(2):
                        hh = 2 * pair + hb
                        pt = pts[hb]
                        acc = accA if hb == 0 else accB
                        col = pair * 65
                        for idx, j in enumerate(js):
                            first = (g == 0 and idx == 0)
                            last = (g == ngrp - 1 and idx == nb - 1)
                            cx.op("pe", lambda: T.matmul(acc.ap[:, col:col + 65], lhsT=pt.ap[:, idx * 128:(idx + 1) * 128],
                                                         rhs=Vp.ap[:, j, hh * 65:(hh + 1) * 65], start=first, stop=last),
                                  reads=[pt, Vp], writes=[acc], sig=(idx == nb - 1))

                n = len(items)
                for i in range(n + LAG):
                    if i < n:
                        emit_qk(i)
                    if i - LAG >= 0:
                        emit_pv(i - LAG)
                ar = arow[t % 2]
                for i, acc in enumerate((accA, accB)):
                    a3_ = acc.ap[:, 0:390].rearrange("p (h e) -> p h e", e=65)
                    cx.op("dve", lambda: V.reciprocal(out=st_rd.ap[:, i * 6:(i + 1) * 6], in_=a3_[:, :, 64]), reads=[acc], writes=[st_rd])
                    cx.op("dve", lambda: V.tensor_tensor(out=ar.ap[:, :].rearrange("p (c b d) -> p c b d", c=6, b=2)[:, :, i, :],
                                                         in0=a3_[:, :, 0:64],
                                                         in1=st_rd.ap[:, i * 6:(i + 1) * 6].unsqueeze(2).to_broadcast([128, 6, HD]),
                                                         op=ALU.mult), reads=[acc, st_rd], writes=[ar])
                if "arow" in dbg and t == 0:
                    dump("arow", ar, [128, 768])
                return rms_norm(ar, anw, t % 2)

            def stage_b(t, dT):
                dT(t, 0)
                b0, b1 = auxget(1)[0], auxget(1)[0]
                for half, bk in ((0, b0), (1, b1)):
                    for c in range(12):
                        cx.op("pe", lambda: T.matmul(bk.ap[:, :], lhsT=m3[:, c, t * 128:(t + 1) * 128], rhs=wo3[:, c, half * 512:(half + 1) * 512],
                                                     start=(c == 0), stop=(c == 11)), reads=[mixT, wout_b], writes=[bk], sig=(c == 11))
                h = hA[t]
                for half, bk in ((0, b0), (1, b1)):
                    cx.op("dve", lambda: V.tensor_tensor(out=h.ap[:, half * 512:(half + 1) * 512], in0=h.ap[:, half * 512:(half + 1) * 512],
                                                         in1=bk.ap[:, :], op=ALU.add), reads=[h, bk], writes=[h])
                if "mixres" in dbg and t == 0:
                    dump("mixres", h, [128, D])
                layer_norm(t, False, lnexp=True)

            dTs = {}
            for step in range(NT + 2):
                if step < NT:
                    dTs[step] = stage_a(step)
                    chk("mix_attn")
                if 0 <= step - 2 < NT:
                    if step - 2 == NT - 1 and LAST_HT is not None:
                        LAST_HT.append(lambda: make_hT(NT - 1))
                    else:
                        make_hT(step - 2)
                if 0 <= step - 1 < NT:
                    stage_b(step - 1, dTs.pop(step - 1))
            Pool.lo, Pool.hi, Pool.nxt = 0, 8, 0
            Pool.split = False

        nunits = nseq * UPS if nunits_override is None else nunits_override
        if STOP == "prologue" or (STOP and (STOP == "const" or STOP.startswith("cast"))):
            nunits = 0
        NUNITS[0] = nunits
        for u in range(nunits):
            tok0 = u * TU
            if u % UPS == 0:
                rotary_tables(u // UPS)
                if u == 0:
                    dump("cost", cost, [128, 16, 8])
                    dump("sint", sint, [128, 16, 8])
            if u == 0:
                for t in range(NT):
                    load_x(0, t)
                    x_to_hT(t)
            for t in range(NT):
                x_to_hA(t)
            if STOP == "x":
                break
            ffn_phase(0, u, False)
            if u == 0:
                dump("h1", (hA_t[:, :, :], [hh_ for hh_ in hA]), [128, NT, D])
            if STOP == "ffn1":
                break
            last_ht = []
            try:
                mixer_phase(u, LAST_HT=last_ht)
            except _StopNow:
                Pool.lo, Pool.hi, Pool.nxt = 0, 8, 0
                Pool.split = False
                break
            if u == 0:
                dump("h2", (hA_t[:, :, :], [hh_ for hh_ in hA]), [128, NT, D])
            if STOP == "mixer":
                for f_ in last_ht:
                    f_()
                break
            ffn_phase(1, u, True, pre_final=(last_ht[0] if last_ht else None))

        cx.wait_tokens("act", [(h.chan2.sem, 16 * h.chan2.n) for h in hA])
        cx.wait_tokens("sp", [(c.sem, 16 * c.n) for c in cx.chans if c.n > 0])
    return nc, sorted(dbg_out)


def _consts():
    s = np.arange(128)[:, None]
    t = np.arange(128)[None, :]

    def cb(b):
        delta = 128 * b + t - s
        c = ((delta <= 128).astype(np.float32) + ((delta % 4 == 0) & (delta <= 512)).astype(np.float32)
             + (delta % 16 == 0).astype(np.float32))
        return c * (delta >= 0)

    masks = np.zeros((128, 9 * 128), np.float32)
    for i, b in enumerate((8, 7, 6, 5, 4, 3, 2, 1, 0)):
        masks[:, i * 128:(i + 1) * 128] = cb(b)
    k_ = np.arange(128)[:, None]
    l_ = np.arange(128)[None, :]
    U = (k_ <= l_).astype(np.float32)
    SL = (k_ > l_).astype(np.float32)
    neg = np.where(l_ >= k_, 0.0, NEG).astype(np.float32)
    invf = (np.float32(500000.0) ** (-np.arange(0, 16, 2, dtype=np.float32) / np.float32(16))).astype(np.float32)
    return {
        "c_masks": masks.astype(ml_dtypes.bfloat16),
        "c_identb": np.eye(128, dtype=np.float32).astype(ml_dtypes.bfloat16),
        "c_identf": np.eye(128, dtype=np.float32),
        "c_U": U, "c_SL": SL, "c_ones": np.ones((128, 128), np.float32), "c_negb": neg.astype(ml_dtypes.bfloat16),
        "c_invf": invf.reshape(1, 8),
    }


def make_in_maps(inputs, ncore, nseq):
    f = lambda a: np.ascontiguousarray(np.asarray(a, dtype=np.float32))
    shared = {
        "wg1": f(inputs["ffn1_gate"][0]), "wu1": f(inputs["ffn1_up"][0]), "wd1": f(inputs["ffn1_down"][0]),
        "wg2": f(inputs["ffn2_gate"][0]), "wu2": f(inputs["ffn2_up"][0]), "wd2": f(inputs["ffn2_down"][0]),
        "w_in": f(inputs["w_in"][0]), "w_out": f(inputs["w_out"][0]),
        "ln1_g": f(inputs["ln1_g"]), "ln1_b": f(inputs["ln1_b"]),
        "ln2_g": f(inputs["ln2_g"]), "ln2_b": f(inputs["ln2_b"]),
        "ln3_g": f(inputs["ln3_g"]), "ln3_b": f(inputs["ln3_b"]),
        "conv_w_l": f(np.asarray(inputs["conv_w"][0]).T.reshape(14, 128, 4).transpose(1, 0, 2)),
        "conv_b_l": f(np.asarray(inputs["conv_b"][0]).reshape(14, 128).T),
        "dt_bias": f(inputs["dt_bias"]), "a_log": f(inputs["a_log"]), "d_skip": f(inputs["d_skip"]),
        "attn_norm_w": f(inputs["attn_norm_w"]), "ssd_norm_w": f(inputs["ssd_norm_w"]),
    }
    shared.update(_consts())
    x = np.asarray(inputs["x"], dtype=np.float32)
    pos = np.asarray(inputs["positions"]).astype(np.int32)
    maps = []
    for c in range(ncore):
        m = dict(shared)
        m["x"] = np.ascontiguousarray(x[c * nseq:(c + 1) * nseq].reshape(nseq * SEQ, D))
        m["pos"] = np.ascontiguousarray(pos[c * nseq:(c + 1) * nseq].reshape(nseq * 16, 128))
        maps.append(m)
    return maps


_CACHE = {}


def kernel(**inputs):
    x = np.asarray(inputs["x"])
    B = x.shape[0]
    nseq = B // NCORE
    if nseq not in _CACHE:
        _CACHE[nseq] = build(nseq)[0]
    nc = _CACHE[nseq]
    maps = make_in_maps(inputs, NCORE, nseq)
    res = run_bass_kernel_spmd(nc, maps, core_ids=list(range(NCORE)))
    outs = [np.asarray(r["out"], dtype=np.float32).reshape(nseq, SEQ, D) for r in res.results]
    return np.concatenate(outs, axis=0)
```

```python
import contextlib
import numpy as np
import ml_dtypes
import concourse.bass as bass
import concourse.mybir as mybir
from concourse.bass_utils import run_bass_kernel_spmd

F32 = mybir.dt.float32
BF16 = mybir.dt.bfloat16
I32 = mybir.dt.int32
AF = mybir.ActivationFunctionType
ALU = mybir.AluOpType

D = 1024
DFF = 2816
NFC = 22
SEQ = 2048
H = 12
HD = 64
NG = 4
DIN = 4876
DMIX = 1536
TU = 512
NT = 4
UPS = SEQ // TU
NCORE = 8
ALPHA = float(2.0 ** 0.25)
LN_EPS = 1e-5
RMS_EPS = 1e-6
NEG = -30000.0
MSTART = {0: 5, 1: 1, 2: 0}

SEG_QKV = [(0, 512), (512, 512), (1024, 512), (1536, 512), (2048, 256)]
SEG_Z = [(2304, 512), (2816, 256)]
SEG_X = [(3072, 512), (3584, 512), (4096, 512), (4608, 256)]
COL_DT = 4864

ARENA = 110592
BLK = 256


class _StopNow(Exception):
    pass


class Trk:
    __slots__ = ("w", "r", "excl")

    def __init__(self, excl=False):
        self.w = {}
        self.r = {}
        self.excl = excl


class Chan:
    __slots__ = ("sem", "n")

    def __init__(self, sem):
        self.sem = sem
        self.n = 0


class Buf:
    __slots__ = ("ap", "trks", "chan", "chan2")

    def __init__(self, ap, trks, chan=None, chan2=None):
        self.ap = ap
        self.trks = trks
        self.chan = chan
        self.chan2 = chan2


def _trks(lst):
    out = []
    for b in lst:
        if isinstance(b, Trk):
            out.append(b)
        else:
            out.extend(b.trks)
    return out


class Ctx:
    def __init__(self, nc, es):
        self.nc = nc
        self.es = es
        self.eng = {}
        for name, h in (("pe", nc.tensor), ("act", nc.scalar), ("dve", nc.vector),
                        ("pool", nc.gpsimd), ("sp", nc.sync)):
            sem = es.enter_context(nc.semaphore("sem_" + name))
            self.eng[name] = {"h": h, "sem": sem, "cnt": 0, "known": {}}
        self.nchan = 0
        self.chans = []

    def chan(self):
        self.nchan += 1
        c = Chan(self.es.enter_context(self.nc.semaphore("dch%d" % self.nchan)))
        self.chans.append(c)
        return c

    def _need(self, e, reads, writes):
        need = {}

        def add(d, skip_self):
            for k, (sem, thr) in d.items():
                if skip_self and sem is e["sem"]:
                    continue
                if need.get(k, (None, 0))[1] < thr:
                    need[k] = (sem, thr)

        for t in reads:
            add(t.w, False)
            if t.excl:
                add(t.r, True)
        for t in writes:
            add(t.r, True)
            add(t.w, True)
        return need

    def _emit_waits(self, e, need):
        for k, (sem, thr) in need.items():
            if e["known"].get(k, 0) < thr:
                e["h"].wait_ge(sem, thr)
                e["known"][k] = thr

    def _commit(self, tok, reads, writes):
        k = id(tok[0])
        for t in reads:
            if t.r.get(k, (None, 0))[1] < tok[1]:
                t.r[k] = tok
        for t in writes:
            t.w = {k: tok}
            t.r = {}

    def op(self, en, fn, reads=(), writes=(), sig=True):
        e = self.eng[en]
        reads = _trks(reads)
        writes = _trks(writes)
        self._emit_waits(e, self._need(e, reads, writes))
        inst = fn()
        if sig:
            e["cnt"] += 1
            inst.then_inc(e["sem"], 1)
            tok = (e["sem"], e["cnt"])
        else:
            assert en == "pe"
            tok = (e["sem"], e["cnt"] + 1)
        self._commit(tok, reads, writes)
        return tok

    def dma(self, qn, out_ap, in_ap, chan, reads=(), writes=(), **kw):
        e = self.eng[qn]
        reads = _trks(reads)
        writes = _trks(writes)
        self._emit_waits(e, self._need(e, reads, writes))
        chan.n += 1
        e["h"].dma_start(out=out_ap, in_=in_ap, **kw).then_inc(chan.sem, 16)
        tok = (chan.sem, 16 * chan.n)
        self._commit(tok, reads, writes)
        return tok

    def wait_tokens(self, en, toks):
        e = self.eng[en]
        need = {}
        for sem, thr in toks:
            k = id(sem)
            if need.get(k, (None, 0))[1] < thr:
                need[k] = (sem, thr)
        self._emit_waits(e, need)


def build(nseq=4, dbg=(), stop=None, nunits_override=None):
    nc = bass.Bass("TRN2", target_bir_lowering=False)
    ntok = nseq * SEQ
    dbg = set(dbg)
    dbg_out = {}

    def din(name, shape, dt=F32):
        return nc.dram_tensor(name, list(shape), dt, kind="ExternalInput").ap()

    x_d = din("x", [ntok, D])
    pos_d = din("pos", [nseq * 16, 128], I32)
    wg_d = [din("wg1", [D, DFF]), din("wg2", [D, DFF])]
    wu_d = [din("wu1", [D, DFF]), din("wu2", [D, DFF])]
    wd_d = [din("wd1", [DFF, D]), din("wd2", [DFF, D])]
    win_d = din("w_in", [D, DIN])
    wout_d = din("w_out", [DMIX, D])
    lng_d = [din("ln%d_g" % i, [1, D]) for i in (1, 2, 3)]
    lnb_d = [din("ln%d_b" % i, [1, D]) for i in (1, 2, 3)]
    convw_d = din("conv_w_l", [128, 14, 4])
    convb_d = din("conv_b_l", [128, 14])
    dtb_d = din("dt_bias", [1, H])
    alog_d = din("a_log", [1, H])
    dsk_d = din("d_skip", [1, H])
    anw_d = din("attn_norm_w", [1, 768])
    snw_d = din("ssd_norm_w", [1, 768])
    masks_d = din("c_masks", [128, 9 * 128], BF16)
    identb_d = din("c_identb", [128, 128], BF16)
    identf_d = din("c_identf", [128, 128])
    cU_d = din("c_U", [128, 128])
    cSL_d = din("c_SL", [128, 128])
    cones_d = din("c_ones", [128, 128])
    cneg_d = din("c_negb", [128, 128], BF16)
    invf_d = din("c_invf", [1, 8])
    out_d = nc.dram_tensor("out", [ntok, D], F32, kind="ExternalOutput").ap()

    def dscr(name, shape):
        return nc.dram_tensor(name, list(shape), BF16, kind="Internal").ap()

    sg_s = [dscr("s_wg1", [D, DFF]), dscr("s_wg2", [D, DFF])]
    su_s = [dscr("s_wu1", [D, DFF]), dscr("s_wu2", [D, DFF])]
    sd_s = [dscr("s_wd1", [DFF, D]), dscr("s_wd2", [DFF, D])]
    sin_s = dscr("s_win", [D, DIN])
    sout_s = dscr("s_wout", [DMIX, D])

    with contextlib.ExitStack() as es:
        cx = Ctx(nc, es)

        def sb(name, shape, dt):
            return es.enter_context(nc.sbuf_tensor(name, list(shape), dt))

        def mk(name, shape, dt, chan=False):
            t = sb(name, shape, dt)
            return Buf(t, [Trk()], cx.chan() if chan else None)

        kT = mk("kT", [128, 6, SEQ], BF16)
        Vp = mk("Vp", [128, 16, H * 65], BF16)
        hstate = mk("hstate", [128, 768], F32)
        prevb = mk("prevb", [128, 768], BF16)
        tail = mk("tail", [128, 14, 3], F32)
        masks = mk("masks", [128, 9 * 128], BF16, True)
        identb = mk("identb", [128, 128], BF16, True)
        identf = mk("identf", [128, 128], F32, True)
        cU = mk("cU", [128, 128], F32, True)
        cUb = mk("cUb", [128, 128], BF16)
        cSL = mk("cSL", [128, 128], F32, True)
        cones = mk("cones", [128, 128], F32, True)
        cnegb = mk("cnegb", [128, 128], BF16, True)
        invf = mk("invf", [128, 8], F32, True)
        cost = mk("cost", [128, 16, 8], F32)
        sint = mk("sint", [128, 16, 8], F32)
        lng = mk("lng", [128, D], F32, True)
        lnb = mk("lnb", [128, D], F32, True)
        snw = mk("snw", [128, 768], F32, True)
        convw = mk("convw", [128, 14, 4], F32, True)
        convb = mk("convb", [128, 14], F32, True)
        dtb = mk("dtb", [128, H], F32, True)
        aneg = mk("aneg", [128, H], F32, True)
        dsk = mk("dsk", [128, H], F32, True)
        hA_t = sb("hA", [128, NT, D], F32)
        hA = [Buf(hA_t[:, t, :], [Trk()], cx.chan(), cx.chan()) for t in range(NT)]
        hT_h = sb("hT", [128, 8, TU], BF16)
        hT_trk = [Trk() for _ in range(NT)]
        hT = Buf(hT_h, hT_trk)
        hT_t = [Buf(hT_h, [hT_trk[t]]) for t in range(NT)]
        sgt1 = mk("sgt1", [128, 512], F32)
        st_bn = mk("st_bn", [128, 2, 6], F32)
        st_mv = mk("st_mv", [128, 2], F32)
        st_sd = mk("st_sd", [128, 1], F32)
        st_rs = mk("st_rs", [128, 1], F32)
        st_ss = mk("st_ss", [128, 1], F32)
        st_rd = mk("st_rd", [128, H], F32)

        arena_t = sb("arena", [128, ARENA // 4], F32)
        blocks = [Trk() for _ in range(ARENA // BLK)]

        def av(off, nbytes, dt, chan=False, **rr):
            assert off % 4 == 0 and nbytes % 4 == 0 and off + nbytes <= ARENA
            ap = arena_t[:, off // 4:(off + nbytes) // 4]
            if dt is not F32:
                ap = ap.bitcast(dt)
            trks = blocks[off // BLK:(off + nbytes - 1) // BLK + 1]
            return Buf(ap, trks, cx.chan() if chan else None)

        win_slot = [av(0, 8192, BF16, True), av(8192, 8192, BF16, True)]
        A_AT = 16384
        aT = av(A_AT, 22528, BF16)
        sgt = [av(A_AT + 22528, 2048, F32), sgt1]
        wout_b = av(A_AT, 24576, BF16, True)
        A_WGU = 40960
        wg_slot = [av(A_WGU + s * 8192, 4096, BF16, True) for s in range(3)]
        wu_slot = [av(A_WGU + s * 8192 + 4096, 4096, BF16, True) for s in range(3)]
        A_WD = 65536
        wd_grp = [av(A_WD + g * 4096, 4096, BF16, True) for g in range(11)]
        o = A_WGU
        zs = av(o, 6144, BF16); o += 6144
        xs_tok = av(o, 6144, BF16); o += 6144
        B_tok = av(o, 4096, BF16); o += 4096
        BTb = av(o, 4096, BF16); o += 4096
        CTb = av(o, 4096, BF16); o += 4096
        assert o == A_WD
        pc = [av(o + i * 2304, 2060, F32) for i in range(2)]; o += 4608
        cacc = [av(o + i * 2048, 2048, F32) for i in range(2)]; o += 4096
        cout = [av(o + i * 1024, 1024, BF16) for i in range(2)]; o += 2048
        xdt = av(o, 1536, BF16); o += 1536
        xdtd = av(o, 1536, BF16); o += 1536
        LT = [av(o + i * 1536, 1536, F32) for i in range(4)]; o += 6144
        MT = [av(o + i * 768, 768, BF16) for i in range(4)]; o += 3072
        yrow = [av(o, 3072, F32)] * 2; o += 3072
        ytmp = [av(o, 768, F32)] * 2; o += 768
        wdt_b = av(o, 192, BF16, True); o += 256
        dtv = av(o, 192, F32); o += 256
        dAb = av(o, 192, F32); o += 256
        Eb = av(o, 576, F32); o += 768
        nac = av(o, 192, F32); o += 192
        ddb = av(o, 48, F32); o += 64
        draw = av(o, 48, F32); o += 128
        dAhi = av(o, 96, BF16); o += 256
        dAlo = av(o, 96, BF16); o += 256
        dAtmp = av(o, 192, F32); o += 256
        assert o <= 95232
        o = A_WD
        qT = av(o, 6144, BF16); o += 6144
        qkst = [av(o + i * 1024, 1024, BF16) for i in range(2)]; o += 2048
        rtmp = [av(o + i * 256, 256, F32) for i in range(4)]; o += 1024
        PT = [av(o + i * 1024, 1024, BF16) for i in range(4)] + [av(88064 + i * 1024, 1024, BF16) for i in range(6)]; o += 4096
        arow = [av(o + i * 3072, 3072, F32) for i in range(2)]; o += 6144
        anw = av(84992, 3072, F32, True)
        assert o <= 84992
        mixT = av(95232, 12288, BF16)
        mixn = [av(107520 + i * 1536, 1536, BF16) for i in range(2)]
        stg_f = [av(i * 19712, 19504, F32, True) for i in range(3)]
        stg_b = [av(59392 + i * 9984, 9752, BF16, True) for i in range(3)]

        ps_t = [es.enter_context(nc.psum_tensor("ps%d" % i, [128, 512], F32)) for i in range(8)]
        ps = [Buf(ps_t[i], [Trk(excl=True)]) for i in range(8)]

        class Pool:
            lo, hi, nxt = 0, 8, 0
            split = False

        class Aux:
            lo, hi, nxt = 4, 6, 4

        def auxget(n=1):
            if not Pool.split:
                return psget(n)
            p = Aux.nxt
            if p + n > Aux.hi:
                p = Aux.lo
            Aux.nxt = p + n
            if Aux.nxt >= Aux.hi:
                Aux.nxt = Aux.lo
            return [ps[p + i] for i in range(n)]

        def psget(n=1):
            p = Pool.nxt
            if p % n:
                p += n - p % n
            if p + n > Pool.hi:
                p = Pool.lo
            Pool.nxt = p + n
            if Pool.nxt >= Pool.hi:
                Pool.nxt = Pool.lo
            return [ps[p + i] for i in range(n)]

        V = nc.vector
        A = nc.scalar
        G = nc.gpsimd
        T = nc.tensor

        def dump(name, buf, shape, dt=F32):
            if name not in dbg:
                return
            d = nc.dram_tensor("dbg_" + name, list(shape), dt, kind="ExternalOutput").ap()
            ch = cx.chan()
            tok = cx.dma("sp", d, buf.ap[:] if isinstance(buf, Buf) else buf[0], ch,
                         reads=[buf] if isinstance(buf, Buf) else buf[1])
            cx.wait_tokens("sp", [tok])
            dbg_out[name] = None

        def cload(buf, src):
            cx.dma("sp", buf.ap[:], src, buf.chan, writes=[buf])

        cload(masks, masks_d)
        cload(identb, identb_d)
        cload(identf, identf_d)
        cload(cU, cU_d)
        cload(cSL, cSL_d)
        cload(cones, cones_d)
        cload(cnegb, cneg_d)
        cload(invf, invf_d[0:1, :].partition_broadcast(128))
        cload(snw, snw_d[0:1, :].partition_broadcast(128))
        cload(convw, convw_d)
        cload(convb, convb_d)
        cload(dtb, dtb_d[0:1, :].partition_broadcast(128))
        cload(aneg, alog_d[0:1, :].partition_broadcast(128))
        cload(dsk, dsk_d[0:1, :].partition_broadcast(128))
        cx.op("act", lambda: A.activation(out=aneg.ap[:], in_=aneg.ap[:], func=AF.Exp), reads=[aneg], writes=[aneg])
        cx.op("dve", lambda: V.tensor_scalar(out=aneg.ap[:], in0=aneg.ap[:], scalar1=-1.0, scalar2=None, op0=ALU.mult),
              reads=[aneg], writes=[aneg])
        cx.op("dve", lambda: V.tensor_copy(out=cUb.ap[:], in_=cU.ap[:]), reads=[cU], writes=[cUb])
        cx.op("pool", lambda: G.memset(Vp.ap[:], 1.0), writes=[Vp])

        RB = 95232
        nti = 16
        posi = av(RB, 512, I32, True)
        posf = av(RB + 512, 512, F32)
        post = av(RB + 1024, 64, F32)
        ang = av(RB + 1088, 512, F32)
        kint = av(RB + 1600, 512, I32)
        kf = av(RB + 2112, 512, F32)
        tgt = av(RB + 2624, 512, F32)
        TWO_PI = 2.0 * np.pi
        C1 = 6.28125
        C2 = float(TWO_PI - 6.28125)

        def sin_table(dst, shift):
            cx.op("dve", lambda: V.tensor_scalar(out=kint.ap[:, :], in0=ang.ap[:, :], scalar1=float(shift),
                                                 scalar2=float(1.0 / TWO_PI), op0=ALU.add, op1=ALU.mult),
                  reads=[ang], writes=[kint])
            cx.op("dve", lambda: V.tensor_copy(out=kf.ap[:, :], in_=kint.ap[:, :]), reads=[kint], writes=[kf])
            cx.op("dve", lambda: V.scalar_tensor_tensor(out=tgt.ap[:, :], in0=kf.ap[:, :], scalar=-C1, in1=ang.ap[:, :],
                                                        op0=ALU.mult, op1=ALU.add), reads=[kf, ang], writes=[tgt])
            cx.op("dve", lambda: V.scalar_tensor_tensor(out=tgt.ap[:, :], in0=kf.ap[:, :], scalar=-C2, in1=tgt.ap[:, :],
                                                        op0=ALU.mult, op1=ALU.add), reads=[kf, tgt], writes=[tgt])
            if shift:
                cx.op("dve", lambda: V.tensor_scalar(out=tgt.ap[:, :], in0=tgt.ap[:, :], scalar1=float(shift), scalar2=None,
                                                     op0=ALU.add), reads=[tgt], writes=[tgt])
            cx.op("dve", lambda: V.tensor_scalar(out=kf.ap[:, :], in0=tgt.ap[:, :], scalar1=float(np.pi), scalar2=float(-TWO_PI),
                                                 op0=ALU.is_gt, op1=ALU.mult), reads=[tgt], writes=[kf])
            cx.op("dve", lambda: V.tensor_tensor(out=tgt.ap[:, :], in0=tgt.ap[:, :], in1=kf.ap[:, :], op=ALU.add),
                  reads=[tgt, kf], writes=[tgt])
            cx.op("dve", lambda: V.tensor_scalar(out=tgt.ap[:, :], in0=tgt.ap[:, :], scalar1=float(-np.pi), scalar2=float(np.pi),
                                                 op0=ALU.max, op1=ALU.min), reads=[tgt], writes=[tgt])
            cx.op("act", lambda: A.activation(out=dst.ap[:, :, :].rearrange("p t f -> p (t f)"), in_=tgt.ap[:, :], func=AF.Sin),
                  reads=[tgt], writes=[dst])

        def rotary_tables(sq):
            cx.dma("sp", posi.ap[0:nti, :], pos_d[sq * 16:(sq + 1) * 16, :], posi.chan, writes=[posi])
            cx.op("dve", lambda: V.tensor_copy(out=posf.ap[0:nti, :], in_=posi.ap[0:nti, :]), reads=[posi], writes=[posf])
            bk = psget(1)[0]
            cx.op("pe", lambda: T.transpose(bk.ap[:, 0:nti], posf.ap[0:nti, :], identf.ap[0:nti, 0:nti]),
                  reads=[posf, identf], writes=[bk])
            cx.op("dve", lambda: V.tensor_copy(out=post.ap[:, :], in_=bk.ap[:, 0:nti]), reads=[bk], writes=[post])
            cx.op("dve", lambda: V.tensor_tensor(out=ang.ap[:, :].rearrange("p (t f) -> p t f", f=8),
                                                 in0=post.ap[:, :].unsqueeze(2).to_broadcast([128, nti, 8]),
                                                 in1=invf.ap[:, :].unsqueeze(1).to_broadcast([128, nti, 8]), op=ALU.mult),
                  reads=[post, invf], writes=[ang])
            sin_table(sint, 0.0)
            sin_table(cost, np.pi / 2)

        store_toks = []
        rr = [0]
        cast_eng = ["dve", "act"]

        NSTG = 9
        PCW = 2048
        stg_f = [av(i * 12288, 8192, F32, True) for i in range(NSTG)]
        stg_b = [av(i * 12288 + 8192, 4096, BF16, True) for i in range(NSTG)]

        def cast_matrix(src, dst, rows, cols):
            for rc in range(rows // 128):
                for c0 in range(0, cols, PCW):
                    n = min(PCW, cols - c0)
                    i = rr[0] % NSTG
                    e = cast_eng[rr[0] % len(cast_eng)]
                    rr[0] += 1
                    f, b = stg_f[i], stg_b[i]
                    cx.dma("sp", f.ap[:, 0:n], src[rc * 128:(rc + 1) * 128, c0:c0 + n], f.chan, writes=[f])
                    if e == "dve":
                        cx.op("dve", lambda: V.tensor_copy(out=b.ap[:, 0:n], in_=f.ap[:, 0:n]), reads=[f], writes=[b])
                    elif e == "act":
                        cx.op("act", lambda: A.copy(out=b.ap[:, 0:n], in_=f.ap[:, 0:n]), reads=[f], writes=[b])
                    else:
                        cx.op("pool", lambda: G.tensor_copy(out=b.ap[:, 0:n], in_=f.ap[:, 0:n]), reads=[f], writes=[b])
                    store_toks.append(cx.dma("act", dst[rc * 128:(rc + 1) * 128, c0:c0 + n], b.ap[:, 0:n], b.chan, reads=[b]))

        jobs = [(wg_d[0], sg_s[0], D, DFF), (wu_d[0], su_s[0], D, DFF), (wd_d[0], sd_s[0], DFF, D), (win_d, sin_s, D, DIN),
                (wout_d, sout_s, DMIX, D), (wg_d[1], sg_s[1], D, DFF), (wu_d[1], su_s[1], D, DFF), (wd_d[1], sd_s[1], DFF, D)]
        if stop == "const":
            jobs = []
        if stop and stop.startswith("cast:"):
            cast_eng = stop[5:].split(",")
            jobs = jobs[:1]
        if stop and stop.startswith("castn:"):
            sel = [int(v) for v in stop[6:].split(",")]
            jobs = [jobs[i] for i in sel]
        for jb in jobs:
            cast_matrix(*jb)
        cx.wait_tokens("sp", store_toks)
        STOP = stop

        def load_ln(k, final):
            cx.dma("sp", lng.ap[:], lng_d[k][0:1, :].partition_broadcast(128), lng.chan, writes=[lng])
            cx.dma("sp", lnb.ap[:], lnb_d[k][0:1, :].partition_broadcast(128), lnb.chan, writes=[lnb])
            if not final:
                cx.op("act", lambda: A.mul(out=lnb.ap[:], in_=lnb.ap[:], mul=ALPHA), reads=[lnb], writes=[lnb])

        def make_hT(t, src=None, scale=1.0 / ALPHA):
            h = hA[t] if src is None else src
            for half in range(2):
                bk = auxget(1)[0]
                for j in range(4):
                    c = half * 4 + j
                    cx.op("pe", lambda: T.transpose(bk.ap[:, j * 128:(j + 1) * 128], h.ap[:, c * 128:(c + 1) * 128], identf.ap[:]),
                          reads=[h, identf], writes=[bk], sig=(j == 3))
                cx.op("act", lambda: A.activation(out=hT.ap[:, half * 4:half * 4 + 4, t * 128:(t + 1) * 128],
                                                  in_=bk.ap[:, :].rearrange("p (c k) -> p c k", c=4),
                                                  func=AF.Copy, scale=float(scale)),
                      reads=[bk], writes=[hT_t[t]])

        def layer_norm(t, final, lnexp=False):
            h = hA[t]
            for i in range(2):
                cx.op("dve", lambda: V.bn_stats(out=st_bn.ap[:, i, :], in_=h.ap[:, i * 512:(i + 1) * 512]),
                      reads=[h], writes=[st_bn])
            cx.op("dve", lambda: V.bn_aggr(out=st_mv.ap[:], in_=st_bn.ap[:, :, :].rearrange("p a b -> p (a b)")),
                  reads=[st_bn], writes=[st_mv])
            sc = 1.0 if final else 1.0 / (ALPHA * ALPHA)
            if lnexp:
                cx.op("act", lambda: A.activation(out=st_sd.ap[:], in_=st_mv.ap[:, 1:2], func=AF.Ln, scale=float(sc),
                                                  bias=float(LN_EPS * sc)), reads=[st_mv], writes=[st_sd])
                cx.op("act", lambda: A.activation(out=st_rs.ap[:], in_=st_sd.ap[:], func=AF.Exp, scale=-0.5), reads=[st_sd], writes=[st_rs])
            else:
                cx.op("act", lambda: A.activation(out=st_sd.ap[:], in_=st_mv.ap[:, 1:2], func=AF.Sqrt, scale=float(sc),
                                                  bias=float(LN_EPS * sc)), reads=[st_mv], writes=[st_sd])
                cx.op("dve", lambda: V.reciprocal(out=st_rs.ap[:], in_=st_sd.ap[:]), reads=[st_sd], writes=[st_rs])
            cx.op("dve", lambda: V.scalar_tensor_tensor(out=h.ap[:, :], in0=h.ap[:, :], scalar=st_mv.ap[:, 0:1], in1=lng.ap[:, :],
                                                        op0=ALU.subtract, op1=ALU.mult), reads=[h, st_mv, lng], writes=[h])
            cx.op("dve", lambda: V.scalar_tensor_tensor(out=h.ap[:, :], in0=h.ap[:, :], scalar=st_rs.ap[:, 0:1], in1=lnb.ap[:, :],
                                                        op0=ALU.mult, op1=ALU.add), reads=[h, st_rs, lnb], writes=[h])

        slot_rr = [0]
        NUNITS = [0]

        xst = [av(t * 4096, 4096, F32, True) for t in range(NT)]

        def load_x(u, t):
            tk = u * TU
            cx.dma("sp", xst[t].ap[:, :], x_d[tk + t * 128:tk + (t + 1) * 128, :], xst[t].chan, writes=[xst[t]])

        def x_to_hT(t):
            make_hT(t, src=xst[t], scale=1.0)

        def x_to_hA(t):
            h = hA[t]
            cx.op("act", lambda: A.mul(out=h.ap[:, :], in_=xst[t].ap[:, :], mul=ALPHA), reads=[xst[t]], writes=[h])

        def ffn_phase(k, u, final, pre_final=None):
            tok0 = u * TU
            sg_, su_, sd_ = sg_s[k], su_s[k], sd_s[k]
            state = {"nld": 0}

            def load_gu(fcg):
                s = slot_rr[0] % 3
                slot_rr[0] += 1
                c0 = fcg * 256
                cx.dma("sp", wg_slot[s].ap[:, :].rearrange("p (c n) -> p c n", c=8),
                       sg_[:, c0:c0 + 256].rearrange("(c p) n -> p c n", p=128), wg_slot[s].chan, writes=[wg_slot[s]])
                cx.dma("sp", wu_slot[s].ap[:, :].rearrange("p (c n) -> p c n", c=8),
                       su_[:, c0:c0 + 256].rearrange("(c p) n -> p c n", p=128), wu_slot[s].chan, writes=[wu_slot[s]])
                return s

            def load_d(g):
                cx.dma("sp", wd_grp[g].ap[:, :].rearrange("p (j n) -> p j n", j=2),
                       sd_[g * 256:(g + 1) * 256, :].rearrange("(j p) n -> p j n", p=128), wd_grp[g].chan, writes=[wd_grp[g]])

            slots = {}
            for g in range(3):
                slots[g] = load_gu(g)
            load_ln(0 if k == 0 else 2, final)
            load_d(0)
            early = {}
            EC = (NT - 1) * 128
            if pre_final is not None:
                for fc in range(3):
                    fcg_, j_ = divmod(fc, 2)
                    s_ = slots[fcg_]
                    bg, bu = psget(2)
                    early[fc] = (bg, bu)
                    for bank, wsl in ((bg, wg_slot[s_]), (bu, wu_slot[s_])):
                        w3_ = wsl.ap[:, :].rearrange("p (c n) -> p c n", c=8)
                        for c in range(8):
                            cx.op("pe", lambda: T.matmul(bank.ap[:, 0:EC], lhsT=w3_[:, c, j_ * 128:(j_ + 1) * 128], rhs=hT.ap[:, c, 0:EC],
                                                         start=(c == 0), stop=(c == 7)),
                                  reads=[wsl] + hT_t[0:NT - 1], writes=[bank], sig=(c == 7))
                pre_final()
            for fcg in range(11):
                s = slots[fcg]
                wg3 = wg_slot[s].ap[:, :].rearrange("p (c n) -> p c n", c=8)
                wu3 = wu_slot[s].ap[:, :].rearrange("p (c n) -> p c n", c=8)
                for j in range(2):
                    fc = fcg * 2 + j
                    if fc in early:
                        bg, bu = early[fc]
                        c_lo = EC
                        rd = [hT_t[NT - 1]]
                    else:
                        bg, bu = psget(2)
                        c_lo = 0
                        rd = [hT]
                    for c in range(8):
                        cx.op("pe", lambda: T.matmul(bg.ap[:, c_lo:TU], lhsT=wg3[:, c, j * 128:(j + 1) * 128], rhs=hT.ap[:, c, c_lo:TU],
                                                     start=(c == 0), stop=(c == 7)),
                              reads=[wg_slot[s]] + rd, writes=[bg], sig=(c == 7))
                    for c in range(8):
                        cx.op("pe", lambda: T.matmul(bu.ap[:, c_lo:TU], lhsT=wu3[:, c, j * 128:(j + 1) * 128], rhs=hT.ap[:, c, c_lo:TU],
                                                     start=(c == 0), stop=(c == 7)),
                              reads=[wu_slot[s]] + rd, writes=[bu], sig=(c == 7))
                    sgb = sgt[fc % 2]
                    cx.op("act", lambda: A.activation(out=sgb.ap[:, :], in_=bg.ap[:, :], func=AF.Silu), reads=[bg], writes=[sgb])
                    cx.op("dve", lambda: V.tensor_tensor(out=aT.ap[:, fc * 512:(fc + 1) * 512], in0=sgb.ap[:, :], in1=bu.ap[:, :],
                                                         op=ALU.mult), reads=[sgb, bu], writes=[aT])
                if fcg + 3 < 11:
                    slots[fcg + 3] = load_gu(fcg + 3)
                if fcg + 1 < 11:
                    load_d(fcg + 1)
            pending = None
            if final and u + 1 < NUNITS[0]:
                for t in range(NT):
                    load_x(u + 1, t)
            for t in range(NT):
                b0, b1 = psget(2)
                for half, bk in ((0, b0), (1, b1)):
                    for fc in range(NFC):
                        g = fc // 2
                        wd3 = wd_grp[g].ap[:, :].rearrange("p (j n) -> p j n", j=2)
                        cx.op("pe", lambda: T.matmul(bk.ap[:, :], lhsT=aT.ap[:, fc * 512 + t * 128:fc * 512 + (t + 1) * 128],
                                                     rhs=wd3[:, fc % 2, half * 512:(half + 1) * 512],
                                                     start=(fc == 0), stop=(fc == NFC - 1)),
                              reads=[aT, wd_grp[g]], writes=[bk], sig=(fc == NFC - 1))
                if pending is not None:
                    pending()
                    pending = None
                h = hA[t]
                for half, bk in ((0, b0), (1, b1)):
                    cx.op("dve", lambda: V.scalar_tensor_tensor(out=h.ap[:, half * 512:(half + 1) * 512], in0=bk.ap[:, :], scalar=0.5,
                                                                in1=h.ap[:, half * 512:(half + 1) * 512], op0=ALU.mult, op1=ALU.add),
                          reads=[bk, h], writes=[h])
                layer_norm(t, final)
                if final:
                    cx.dma("act", out_d[tok0 + t * 128:tok0 + (t + 1) * 128, :], h.ap[:, :], h.chan2, reads=[h])
                    if u + 1 < NUNITS[0]:
                        pending = (lambda tt: (lambda: x_to_hT(tt)))(t)
                else:
                    pending = (lambda tt: (lambda: make_hT(tt)))(t)
            if pending is not None:
                pending()

        def load_win(slot, col0, ncols):
            b = win_slot[slot]
            cx.dma("sp", b.ap[:, :].rearrange("p (c n) -> p c n", c=8)[:, :, 0:ncols],
                   sin_s[:, col0:col0 + ncols].rearrange("(c p) n -> p c n", p=128), b.chan, writes=[b])

        def rms_norm(row, wbc, mslot):
            mn = mixn[mslot]
            cx.op("act", lambda: A.activation(out=mn.ap[:, :], in_=row.ap[:, :], func=AF.Square, accum_out=st_ss.ap[:, 0:1]),
                  reads=[row], writes=[mn, st_ss])
            cx.op("act", lambda: A.activation(out=st_sd.ap[:], in_=st_ss.ap[:], func=AF.Ln, scale=float(1.0 / 768.0),
                                              bias=float(RMS_EPS)), reads=[st_ss], writes=[st_sd])
            cx.op("act", lambda: A.activation(out=st_rs.ap[:], in_=st_sd.ap[:], func=AF.Exp, scale=-0.5), reads=[st_sd], writes=[st_rs])
            cx.op("dve", lambda: V.scalar_tensor_tensor(out=mn.ap[:, :], in0=row.ap[:, :], scalar=st_rs.ap[:, 0:1], in1=wbc.ap[:, :],
                                                        op0=ALU.mult, op1=ALU.mult), reads=[row, st_rs, wbc], writes=[mn])

            def do_T(t, c0):
                bk = auxget(1)[0]
                pb = bk.ap[:, :].bitcast(BF16)
                for c in range(6):
                    cx.op("pe", lambda: T.transpose(pb[:, c * 128:(c + 1) * 128], mn.ap[:, c * 128:(c + 1) * 128], identb.ap[:]),
                          reads=[mn, identb], writes=[bk], sig=(c == 5))
                m3 = mixT.ap[:, :].rearrange("p (c k) -> p c k", c=12)
                cx.op("act", lambda: A.copy(out=m3[:, c0:c0 + 6, t * 128:(t + 1) * 128],
                                            in_=pb[:, 0:768].rearrange("p (c k) -> p c k", c=6)), reads=[bk], writes=[mixT])
            return do_T

        def chk(name):
            if STOP == name:
                raise _StopNow()

        def mixer_phase(u, LAST_HT=None):
            q = u % UPS
            sq = u // UPS
            tok0 = u * TU
            m3 = mixT.ap[:, :].rearrange("p (c k) -> p c k", c=12)
            ws = [0]

            def next_seg(col0, ncols):
                s = ws[0] % 2
                ws[0] += 1
                load_win(s, col0, ncols)
                return s

            if q == 0:
                cx.op("pool", lambda: G.memset(hstate.ap[:], 0.0), writes=[hstate])
                cx.op("pool", lambda: G.memset(prevb.ap[:], 0.0), writes=[prevb])
                cx.op("pool", lambda: G.memset(tail.ap[:], 0.0), writes=[tail])

            def proj_tok(slot, ncols, t, bk):
                w3 = win_slot[slot].ap[:, :].rearrange("p (c n) -> p c n", c=8)
                for c in range(8):
                    cx.op("pe", lambda: T.matmul(bk.ap[:, 0:ncols], lhsT=hT.ap[:, c, t * 128:(t + 1) * 128], rhs=w3[:, c, 0:ncols],
                                                 start=(c == 0), stop=(c == 7)),
                          reads=[hT_t[t], win_slot[slot]], writes=[bk], sig=(c == 7))

            z3 = zs.ap[:, :].rearrange("p (t n) -> p t n", t=NT)
            dt3 = dtv.ap[:, :].rearrange("p (t n) -> p t n", t=NT)
            dA3 = dAb.ap[:, :].rearrange("p (t n) -> p t n", t=NT)
            hi3 = dAhi.ap[:, :].rearrange("p (t n) -> p t n", t=NT)
            lo3 = dAlo.ap[:, :].rearrange("p (t n) -> p t n", t=NT)
            wdt3 = wdt_b.ap[:, :].rearrange("p (c n) -> p c n", c=8)
            xs4 = xs_tok.ap[:, :].rearrange("p (t n) -> p t n", t=NT)
            Bt4 = B_tok.ap[:, :].rearrange("p (t n) -> p t n", t=NT)
            BT3 = BTb.ap[:, :].rearrange("p (g k) -> p g k", g=NG)
            CT3 = CTb.ap[:, :].rearrange("p (g k) -> p g k", g=NG)

            def z_step(s_, ncols, t, zoff):
                def f():
                    bk = psget(1)[0]
                    proj_tok(s_(), ncols, t, bk)
                    cx.op("act", lambda: A.activation(out=z3[:, t, zoff:zoff + ncols], in_=bk.ap[:, 0:ncols], func=AF.Silu),
                          reads=[bk], writes=[zs])
                return f

            def dt_step(t):
                def f():
                    if t == 0:
                        cx.dma("sp", wdt_b.ap[:, :].rearrange("p (c n) -> p c n", c=8),
                               sin_s[:, COL_DT:COL_DT + H].rearrange("(c p) n -> p c n", p=128), wdt_b.chan, writes=[wdt_b])
                    bk = psget(1)[0]
                    for c in range(8):
                        cx.op("pe", lambda: T.matmul(bk.ap[:, 0:H], lhsT=hT.ap[:, c, t * 128:(t + 1) * 128], rhs=wdt3[:, c, :],
                                                     start=(c == 0), stop=(c == 7)), reads=[hT_t[t], wdt_b], writes=[bk], sig=(c == 7))
                    cx.op("dve", lambda: V.tensor_tensor(out=draw.ap[:, :], in0=bk.ap[:, 0:H], in1=dtb.ap[:, :], op=ALU.add),
                          reads=[bk, dtb], writes=[draw])
                    cx.op("act", lambda: A.activation(out=draw.ap[:, :], in_=draw.ap[:, :], func=AF.Exp), reads=[draw], writes=[draw])
                    cx.op("act", lambda: A.activation(out=dt3[:, t, :], in_=draw.ap[:, :], func=AF.Ln, bias=1.0), reads=[draw], writes=[dtv])
                    cx.op("dve", lambda: V.tensor_tensor(out=dA3[:, t, :], in0=dt3[:, t, :], in1=aneg.ap[:, :], op=ALU.mult),
                          reads=[dtv, aneg], writes=[dAb])
                    if t == NT - 1:
                        cx.op("dve", lambda: V.tensor_copy(out=dAhi.ap[:, :], in_=dAb.ap[:, :]), reads=[dAb], writes=[dAhi])
                        cx.op("dve", lambda: V.tensor_tensor(out=dAtmp.ap[:, :], in0=dAb.ap[:, :], in1=dAhi.ap[:, :], op=ALU.subtract),
                              reads=[dAb, dAhi], writes=[dAtmp])
                        cx.op("dve", lambda: V.tensor_copy(out=dAlo.ap[:, :], in_=dAtmp.ap[:, :]), reads=[dAtmp], writes=[dAlo])
                return f

            zslot = {}
            extra = []
            zo = 0
            for zi, (c0, ncols) in enumerate(SEG_Z):
                for t in range(NT):
                    extra.append(("z", zi, c0, ncols, t, zo))
                zo += ncols
            for t in range(NT):
                extra.append(("dt", t))

            def run_extra():
                if not extra:
                    return
                e = extra.pop(0)
                if e[0] == "z":
                    _, zi, c0, ncols, t, zo_ = e
                    if zi not in zslot:
                        zslot[zi] = next_seg(c0, ncols)
                    z_step(lambda: zslot[zi], ncols, t, zo_)()
                else:
                    dt_step(e[1])()

            for _ in range(NT):
                e_ = [x for x in extra if x[0] == "dt"][0]
                extra.remove(e_)
                dt_step(e_[1])()
            early = [x for x in extra if x[0] == "z" and x[1] == 0 and x[4] < NT - 1]
            for e_ in early:
                extra.remove(e_)
                extra.insert(0, e_)
            for _ in range(len(early)):
                run_extra()
            ch = 0
            deferred = []

            def conv_T(ch, dst_b, dst):
                def f():
                    bt = psget(1)[0]
                    pb = bt.ap[:, :].bitcast(BF16)
                    for t in range(NT):
                        cx.op("pe", lambda: T.transpose(pb[:, t * 128:(t + 1) * 128], dst[:, t * 128:(t + 1) * 128], identb.ap[:]),
                              reads=[dst_b, identb], writes=[bt], sig=(t == NT - 1))
                    if ch < 6:
                        cx.op("act", lambda: A.copy(out=xs4[:, :, ch * 128:(ch + 1) * 128],
                                                    in_=pb[:, 0:512].rearrange("p (t k) -> p t k", t=NT)),
                              reads=[bt], writes=[xs_tok])
                    else:
                        g = ch - 6
                        cx.op("act", lambda: A.copy(out=Bt4[:, :, g * 128:(g + 1) * 128],
                                                    in_=pb[:, 0:512].rearrange("p (t k) -> p t k", t=NT)),
                              reads=[bt], writes=[B_tok])
                return f

            for (c0, ncols) in SEG_X:
                s = next_seg(c0, ncols)
                w3 = win_slot[s].ap[:, :].rearrange("p (c n) -> p c n", c=8)
                for lc in range(ncols // 128):
                    bk = psget(1)[0]
                    for c in range(8):
                        cx.op("pe", lambda: T.matmul(bk.ap[:, :], lhsT=w3[:, c, lc * 128:(lc + 1) * 128], rhs=hT.ap[:, c, :],
                                                     start=(c == 0), stop=(c == 7)), reads=[win_slot[s], hT], writes=[bk], sig=(c == 7))
                    if len(deferred) >= 2:
                        deferred.pop(0)()
                    p_ = pc[ch % 2]
                    ca = cacc[ch % 2]
                    cx.op("pool", lambda: G.tensor_copy(out=p_.ap[:, 0:3], in_=tail.ap[:, ch, :]), reads=[tail], writes=[p_])
                    cx.op("act", lambda: A.copy(out=p_.ap[:, 3:515], in_=bk.ap[:, :]), reads=[bk], writes=[p_])
                    cx.op("pool", lambda: G.tensor_copy(out=tail.ap[:, ch, :], in_=p_.ap[:, 512:515]), reads=[p_], writes=[tail])
                    cx.op("act", lambda: A.activation(out=ca.ap[:, :], in_=p_.ap[:, 0:512], func=AF.Copy, scale=convw.ap[:, ch, 0:1]),
                          reads=[p_, convw], writes=[ca])
                    for kk in range(1, 4):
                        cx.op("dve", lambda: V.scalar_tensor_tensor(out=ca.ap[:, :], in0=p_.ap[:, kk:kk + 512],
                                                                    scalar=convw.ap[:, ch, kk:kk + 1], in1=ca.ap[:, :],
                                                                    op0=ALU.mult, op1=ALU.add), reads=[p_, convw, ca], writes=[ca])
                    if ch < 6:
                        dst_b, dst = cout[ch % 2], cout[ch % 2].ap[:, :]
                    elif ch < 10:
                        dst_b, dst = BTb, BT3[:, ch - 6, :]
                    else:
                        dst_b, dst = CTb, CT3[:, ch - 10, :]
                    cx.op("act", lambda: A.activation(out=dst, in_=ca.ap[:, :], func=AF.Silu, bias=convb.ap[:, ch:ch + 1]),
                          reads=[ca, convb], writes=[dst_b])
                    if ch < 10:
                        deferred.append(conv_T(ch, dst_b, dst))
                    ch += 1
                    if extra and extra[0][0] == "z":
                        run_extra()
            while deferred:
                deferred.pop(0)()
            while extra:
                run_extra()
            while deferred:
                deferred.pop(0)()
            assert ch == 14
            chk("mix_conv")
            dump("zs", zs, [128, NT * 768], BF16)
            dump("xs_tok", xs_tok, [128, NT * 768], BF16)
            dump("B_tok", B_tok, [128, NT * 512], BF16)
            dump("CTb", CTb, [128, NG * 512], BF16)
            dump("dtv", dtv, [128, NT * H])
            pre_seg = [next_seg(*SEG_QKV[0]), next_seg(*SEG_QKV[1])]
            cx.dma("sp", wout_b.ap[:, :].rearrange("p (c n) -> p c n", c=12),
                   sout_s[:, :].rearrange("(c p) n -> p c n", p=128), wout_b.chan, writes=[wout_b])
            load_ln(1, False)
            q3 = qT.ap[:, :].rearrange("p (c k) -> p c k", c=6)
            kbase = q * TU
            PQ = []
            seg_slot = {0: pre_seg[0], 1: pre_seg[1]}
            qkv_steps = [(si, c0, ncols, t) for si, (c0, ncols) in enumerate(SEG_QKV) for t in range(NT)]

            def run_qkv():
                if not qkv_steps:
                    return
                si, c0, ncols, t = qkv_steps.pop(0)
                if si not in seg_slot:
                    seg_slot[si] = next_seg(c0, ncols)
                if t == 0 and si + 1 < len(SEG_QKV) and (si + 1) not in seg_slot:
                    seg_slot[si + 1] = next_seg(*SEG_QKV[si + 1])
                s = seg_slot[si]
                gt = q * NT + t
                bk = psget(1)[0]
                proj_tok(s, ncols, t, bk)
                if len(PQ) >= 2:
                    PQ.pop(0)()
                if si < 3:
                    nh = ncols // 64
                    st = qkst[(si * NT + t) % 2]
                    b3 = bk.ap[:, 0:ncols].rearrange("p (h d) -> p h d", h=nh)
                    s3 = st.ap[:, 0:ncols].rearrange("p (h d) -> p h d", h=nh)
                    cb = cost.ap[:, gt, :].unsqueeze(1).to_broadcast([128, nh, 8])
                    sn = sint.ap[:, gt, :].unsqueeze(1).to_broadcast([128, nh, 8])
                    r = [rt.ap[:, :].rearrange("p (h d) -> p h d", h=8)[:, 0:nh, :] for rt in rtmp]
                    cx.op("act", lambda: A.copy(out=st.ap[:, 0:ncols], in_=bk.ap[:, 0:ncols]), reads=[bk], writes=[st])
                    cx.op("dve", lambda: V.tensor_tensor(out=r[0], in0=b3[:, :, 0:8], in1=cb, op=ALU.mult), reads=[bk, cost], writes=[rtmp[0]])
                    cx.op("dve", lambda: V.tensor_tensor(out=r[1], in0=b3[:, :, 8:16], in1=sn, op=ALU.mult), reads=[bk, sint], writes=[rtmp[1]])
                    cx.op("dve", lambda: V.tensor_tensor(out=r[2], in0=b3[:, :, 8:16], in1=cb, op=ALU.mult), reads=[bk, cost], writes=[rtmp[2]])
                    cx.op("dve", lambda: V.tensor_tensor(out=r[3], in0=b3[:, :, 0:8], in1=sn, op=ALU.mult), reads=[bk, sint], writes=[rtmp[3]])
                    cx.op("dve", lambda: V.tensor_tensor(out=s3[:, :, 0:8], in0=r[0], in1=r[1], op=ALU.subtract),
                          reads=[rtmp[0], rtmp[1]], writes=[st])
                    cx.op("dve", lambda: V.tensor_tensor(out=s3[:, :, 8:16], in0=r[2], in1=r[3], op=ALU.add),
                          reads=[rtmp[2], rtmp[3]], writes=[st])

                    def qk_T(st=st, ncols=ncols, c0=c0, t=t):
                        bt = psget(1)[0]
                        pb = bt.ap[:, :].bitcast(BF16)
                        npair = ncols // 128
                        for pi in range(npair):
                            cx.op("pe", lambda: T.transpose(pb[:, pi * 128:(pi + 1) * 128], st.ap[:, pi * 128:(pi + 1) * 128], identb.ap[:]),
                                  reads=[st, identb], writes=[bt], sig=(pi == npair - 1))
                        gp0 = c0 // 128
                        nq = max(0, min(6 - gp0, npair))
                        if nq > 0:
                            cx.op("act", lambda: A.copy(out=q3[:, gp0:gp0 + nq, t * 128:(t + 1) * 128],
                                                        in_=pb[:, 0:nq * 128].rearrange("p (c k) -> p c k", c=nq)),
                                  reads=[bt], writes=[qT])
                        if npair - nq > 0:
                            nk = npair - nq
                            kp0 = gp0 + nq - 6
                            cx.op("act", lambda: A.copy(out=kT.ap[:, kp0:kp0 + nk, kbase + t * 128:kbase + (t + 1) * 128],
                                                        in_=pb[:, nq * 128:npair * 128].rearrange("p (c k) -> p c k", c=nk)),
                                  reads=[bt], writes=[kT])
                    PQ.append(qk_T)
                else:
                    nh = ncols // 64
                    h0 = (c0 - 1536) // 64
                    vv = Vp.ap[:, gt, :].rearrange("p (h e) -> p h e", e=65)
                    cx.op("act", lambda: A.copy(out=vv[:, h0:h0 + nh, 0:64], in_=bk.ap[:, 0:ncols].rearrange("p (h d) -> p h d", h=nh)),
                          reads=[bk], writes=[Vp])

            Pool.lo, Pool.hi, Pool.nxt = 0, 6, 0
            E3 = Eb.ap[:, :]
            bS = [ps[6], ps[7]]
            h3 = hstate.ap[:, :].rearrange("p (h d) -> p h d", h=H)
            pend_T = None
            bE = psget(1)[0]
            for t in range(NT):
                for i, cm in enumerate((cU, cSL, cones)):
                    cx.op("pe", lambda: T.matmul(bE.ap[:, t * 36 + i * H:t * 36 + (i + 1) * H], lhsT=cm.ap[:, :], rhs=dA3[:, t, :],
                                                 start=True, stop=True), reads=[cm, dAb], writes=[bE], sig=(i == 2 and t == NT - 1))
            cx.op("act", lambda: A.activation(out=Eb.ap[:, :], in_=bE.ap[:, 0:NT * 36], func=AF.Exp), reads=[bE], writes=[Eb])
            cx.op("dve", lambda: V.tensor_scalar(out=nac.ap[:, :].rearrange("p (t h) -> p t h", t=NT),
                                                 in0=bE.ap[:, 0:NT * 36].rearrange("p (t k) -> p t k", t=NT)[:, :, 0:H],
                                                 scalar1=-1.0, scalar2=None, op0=ALU.mult), reads=[bE], writes=[nac])
            for t in range(NT):
                tc = slice(t * 128, (t + 1) * 128)
                E3 = Eb.ap[:, t * 36:(t + 1) * 36]
                nact = nac.ap[:, t * H:(t + 1) * H]
                cx.op("dve", lambda: V.tensor_tensor(out=ddb.ap[:, :], in0=dt3[:, t, :], in1=E3[:, H:2 * H], op=ALU.mult),
                      reads=[dtv, Eb], writes=[ddb])
                xs_h = xs4[:, t, :].rearrange("p (h d) -> p h d", h=H)
                cx.op("pool", lambda: G.tensor_tensor(out=xdt.ap[:, :].rearrange("p (h d) -> p h d", h=H), in0=xs_h,
                                                      in1=dt3[:, t, :].unsqueeze(2).to_broadcast([128, H, HD]), op=ALU.mult),
                      reads=[xs_tok, dtv], writes=[xdt])
                cx.op("pool", lambda: G.tensor_tensor(out=xdtd.ap[:, :].rearrange("p (h d) -> p h d", h=H), in0=xs_h,
                                                      in1=ddb.ap[:, :].unsqueeze(2).to_broadcast([128, H, HD]), op=ALU.mult),
                      reads=[xs_tok, ddb], writes=[xdtd])
                yr = yrow[t % 2]
                banks = {}

                def stage1(g):
                    bX = psget(1)[0]
                    banks[g] = bX
                    lt = LT[g]
                    mt = MT[g]
                    lt3 = lt.ap[:, :].rearrange("p (j k) -> p j k", j=3)
                    mt3 = mt.ap[:, :].rearrange("p (j k) -> p j k", j=3)
                    cx.op("pe", lambda: T.matmul(bX.ap[:, 384:512], lhsT=BT3[:, g, tc], rhs=CT3[:, g, tc], start=True, stop=True),
                          reads=[BTb, CTb], writes=[bX], sig=False)
                    for j in range(3):
                        hh = 3 * g + j
                        cx.op("pe", lambda: T.matmul(bX.ap[:, j * 128:(j + 1) * 128], lhsT=hi3[:, t, hh:hh + 1].to_broadcast([128, 128]),
                                                     rhs=cUb.ap[:, :], start=True, stop=False), reads=[dAhi, cUb], writes=[bX], sig=False)
                        cx.op("pe", lambda: T.matmul(bX.ap[:, j * 128:(j + 1) * 128], lhsT=lo3[:, t, hh:hh + 1].to_broadcast([128, 128]),
                                                     rhs=cUb.ap[:, :], start=False, stop=False), reads=[dAlo, cUb], writes=[bX], sig=False)
                        cx.op("pe", lambda: T.matmul(bX.ap[:, j * 128:(j + 1) * 128], lhsT=identb.ap[:, :], rhs=cnegb.ap[:, :],
                                                     start=False, stop=True), reads=[identb, cnegb], writes=[bX], sig=(j == 2))
                    for j in range(3):
                        hh = 3 * g + j
                        cx.op("act", lambda: A.activation(out=lt3[:, j, :], in_=bX.ap[:, j * 128:(j + 1) * 128], func=AF.Exp,
                                                          bias=nact[:, hh:hh + 1]), reads=[bX, nac], writes=[lt])
                    cx.op("dve", lambda: V.tensor_tensor(out=mt3, in0=lt3, in1=bX.ap[:, 384:512].unsqueeze(1).to_broadcast([128, 3, 128]),
                                                         op=ALU.mult), reads=[lt, bX], writes=[mt])

                def stage2(g):
                    bY = psget(1)[0]
                    mt = MT[g]
                    yt_ = ytmp[g % 2]
                    mt3 = mt.ap[:, :].rearrange("p (j k) -> p j k", j=3)
                    for j in range(3):
                        hh = 3 * g + j
                        cx.op("pe", lambda: T.matmul(bY.ap[:, j * 64:(j + 1) * 64], lhsT=mt3[:, j, :],
                                                     rhs=xdt.ap[:, hh * 64:(hh + 1) * 64], start=True, stop=True),
                              reads=[mt, xdt], writes=[bY], sig=False)
                    cx.op("pe", lambda: T.matmul(bY.ap[:, 192:384], lhsT=CT3[:, g, tc], rhs=prevb.ap[:, g * 192:(g + 1) * 192],
                                                 start=True, stop=True), reads=[CTb, prevb], writes=[bY])
                    bSg = bS[g // 2]
                    cx.op("pe", lambda: T.matmul(bSg.ap[:, (g % 2) * 192:(g % 2) * 192 + 192], lhsT=Bt4[:, t, g * 128:(g + 1) * 128],
                                                 rhs=xdtd.ap[:, g * 192:(g + 1) * 192], start=True, stop=True),
                          reads=[B_tok, xdtd], writes=[bSg])
                    yg = yr.ap[:, g * 192:(g + 1) * 192].rearrange("p (j d) -> p j d", j=3)
                    xg = xs4[:, t, g * 192:(g + 1) * 192].rearrange("p (j d) -> p j d", j=3)
                    cx.op("pool", lambda: G.tensor_tensor(out=yg, in0=xg, in1=dsk.ap[:, 3 * g:3 * g + 3].unsqueeze(2).to_broadcast([128, 3, HD]),
                                                          op=ALU.mult), reads=[xs_tok, dsk], writes=[yr])
                    cx.op("dve", lambda: V.tensor_tensor(out=yt_.ap[:, :].rearrange("p (j d) -> p j d", j=3),
                                                         in0=bY.ap[:, 192:384].rearrange("p (j d) -> p j d", j=3),
                                                         in1=E3[:, 3 * g:3 * g + 3].unsqueeze(2).to_broadcast([128, 3, HD]), op=ALU.mult),
                          reads=[bY, Eb], writes=[yt_])
                    cx.op("dve", lambda: V.tensor_tensor(out=yr.ap[:, g * 192:(g + 1) * 192], in0=yr.ap[:, g * 192:(g + 1) * 192],
                                                         in1=yt_.ap[:, :], op=ALU.add), reads=[yr, yt_], writes=[yr])
                    cx.op("dve", lambda: V.tensor_tensor(out=yr.ap[:, g * 192:(g + 1) * 192], in0=yr.ap[:, g * 192:(g + 1) * 192],
                                                         in1=bY.ap[:, 0:192], op=ALU.add), reads=[yr, bY], writes=[yr])

                for g in range(NG):
                    stage1(g)
                run_qkv()
                run_qkv()
                if pend_T is not None:
                    pend_T()
                    pend_T = None
                for g in range(NG):
                    stage2(g)
                run_qkv()
                run_qkv()
                run_qkv()
                if "ypre" in dbg and t == 0:
                    dump("ypre", yr, [128, 768])
                cx.op("dve", lambda: V.tensor_tensor(out=yr.ap[:, :], in0=yr.ap[:, :], in1=z3[:, t, :], op=ALU.mult),
                      reads=[yr, zs], writes=[yr])
                dT = rms_norm(yr, snw, t % 2)
                pend_T = (lambda f, tt: (lambda: f(tt, 6)))(dT, t)
                cx.op("pool", lambda: G.tensor_tensor(out=h3, in0=h3, in1=E3[:, 2 * H:3 * H].unsqueeze(2).to_broadcast([128, H, HD]),
                                                      op=ALU.mult), reads=[hstate, Eb], writes=[hstate])
                for i in range(2):
                    cx.op("dve", lambda: V.tensor_tensor(out=hstate.ap[:, i * 384:(i + 1) * 384], in0=hstate.ap[:, i * 384:(i + 1) * 384],
                                                         in1=bS[i].ap[:, 0:384], op=ALU.add), reads=[hstate, bS[i]], writes=[hstate])
                cx.op("act", lambda: A.copy(out=prevb.ap[:, :], in_=hstate.ap[:, :]), reads=[hstate], writes=[prevb])
            Pool.lo, Pool.hi, Pool.nxt = 0, 8, 0
            chk("mix_ssd")
            cx.dma("sp", anw.ap[:, :], anw_d[0:1, :].partition_broadcast(128), anw.chan, writes=[anw])
            while qkv_steps:
                run_qkv()
            if pend_T is not None:
                pend_T()
                pend_T = None
            while PQ:
                PQ.pop(0)()
            chk("mix_qkv")
            dump("qT", qT, [128, 6 * 512], BF16)

            Pool.lo, Pool.hi, Pool.nxt = 0, 4, 0
            Pool.split = True
            accA, accB = ps[6], ps[7]
            mk3 = masks.ap
            wo3 = wout_b.ap[:, :].rearrange("p (c n) -> p c n", c=12)
            NPT = len(PT)
            LAG = NPT // 2 - 2

            def stage_a(t):
                gt = q * NT + t
                ngrp = gt // 4 + 1
                items = []
                for pair in range(6):
                    for g in range(ngrp):
                        bmax = min(4 * g + 3, gt)
                        nb = bmax - 4 * g + 1
                        js = [gt - b for b in range(bmax, 4 * g - 1, -1)]
                        items.append((pair, g, nb, js))
                stash = {}

                def emit_qk(i):
                    pair, g, nb, js = items[i]
                    bks = [psget(1)[0], psget(1)[0]]
                    for idx, j in enumerate(js):
                        for hb in range(2):
                            bp = hb * 64
                            bk = bks[hb]
                            cx.op("pe", lambda: T.matmul(bk.ap[:, idx * 128:(idx + 1) * 128], lhsT=kT.ap[bp:bp + 64, pair, j * 128:(j + 1) * 128],
                                                         rhs=q3[bp:bp + 64, pair, t * 128:(t + 1) * 128], start=True, stop=True),
                                  reads=[kT, qT], writes=[bk], sig=(idx == nb - 1))
                    ms = MSTART[min(g, 2)]
                    pts = []
                    for hb in range(2):
                        pt = PT[(2 * i + hb) % NPT]
                        bk = bks[hb]
                        cx.op("act", lambda: A.activation(out=pt.ap[:, 0:nb * 128], in_=bk.ap[:, 0:nb * 128], func=AF.Exp, scale=0.125),
                              reads=[bk], writes=[pt])
                        cx.op("dve", lambda: V.tensor_tensor(out=pt.ap[:, 0:nb * 128], in0=pt.ap[:, 0:nb * 128],
                                                             in1=mk3[:, (ms + 4 - nb) * 128:(ms + 4) * 128], op=ALU.mult),
                              reads=[pt, masks], writes=[pt])
                        pts.append(pt)
                    stash[i] = pts

                def emit_pv(i):
                    pair, g, nb, js = items[i]
                    pts = stash.pop(i)
                    for hb in range(2):
                        hh = 2 * pair + hb
                        pt = pts[hb]
                        acc = accA if hb == 0 else accB
                        col = pair * 65
                        for idx, j in enumerate(js):
                            first = (g == 0 and idx == 0)
                            last = (g == ngrp - 1 and idx == nb - 1)
                            cx.op("pe", lambda: T.matmul(acc.ap[:, col:col + 65], lhsT=pt.ap[:, idx * 128:(idx + 1) * 128],
                                                         rhs=Vp.ap[:, j, hh * 65:(hh + 1) * 65], start=first, stop=last),
                                  reads=[pt, Vp], writes=[acc], sig=(idx == nb - 1))

                n = len(items)
                for i in range(n + LAG):
                    if i < n:
                        emit_qk(i)
                    if i - LAG >= 0:
                        emit_pv(i - LAG)
                ar = arow[t % 2]
                for i, acc in enumerate((accA, accB)):
                    a3_ = acc.ap[:, 0:390].rearrange("p (h e) -> p h e", e=65)
                    cx.op("dve", lambda: V.reciprocal(out=st_rd.ap[:, i * 6:(i + 1) * 6], in_=a3_[:, :, 64]), reads=[acc], writes=[st_rd])
                    cx.op("dve", lambda: V.tensor_tensor(out=ar.ap[:, :].rearrange("p (c b d) -> p c b d", c=6, b=2)[:, :, i, :],
                                                         in0=a3_[:, :, 0:64],
                                                         in1=st_rd.ap[:, i * 6:(i + 1) * 6].unsqueeze(2).to_broadcast([128, 6, HD]),
                                                         op=ALU.mult), reads=[acc, st_rd], writes=[ar])
                if "arow" in dbg and t == 0:
                    dump("arow", ar, [128, 768])
                return rms_norm(ar, anw, t % 2)

            def stage_b(t, dT):
                dT(t, 0)
                b0, b1 = auxget(1)[0], auxget(1)[0]
                for half, bk in ((0, b0), (1, b1)):
                    for c in range(12):
                        cx.op("pe", lambda: T.matmul(bk.ap[:, :], lhsT=m3[:, c, t * 128:(t + 1) * 128], rhs=wo3[:, c, half * 512:(half + 1) * 512],
                                                     start=(c == 0), stop=(c == 11)), reads=[mixT, wout_b], writes=[bk], sig=(c == 11))
                h = hA[t]
                for half, bk in ((0, b0), (1, b1)):
                    cx.op("dve", lambda: V.tensor_tensor(out=h.ap[:, half * 512:(half + 1) * 512], in0=h.ap[:, half * 512:(half + 1) * 512],
                                                         in1=bk.ap[:, :], op=ALU.add), reads=[h, bk], writes=[h])
                if "mixres" in dbg and t == 0:
                    dump("mixres", h, [128, D])
                layer_norm(t, False, lnexp=True)

            dTs = {}
            for step in range(NT + 2):
                if step < NT:
                    dTs[step] = stage_a(step)
                    chk("mix_attn")
                if 0 <= step - 2 < NT:
                    if step - 2 == NT - 1 and LAST_HT is not None:
                        LAST_HT.append(lambda: make_hT(NT - 1))
                    else:
                        make_hT(step - 2)
                if 0 <= step - 1 < NT:
                    stage_b(step - 1, dTs.pop(step - 1))
            Pool.lo, Pool.hi, Pool.nxt = 0, 8, 0
            Pool.split = False

        nunits = nseq * UPS if nunits_override is None else nunits_override
        if STOP == "prologue" or (STOP and (STOP == "const" or STOP.startswith("cast"))):
            nunits = 0
        NUNITS[0] = nunits
        for u in range(nunits):
            tok0 = u * TU
            if u % UPS == 0:
                rotary_tables(u // UPS)
                if u == 0:
                    dump("cost", cost, [128, 16, 8])
                    dump("sint", sint, [128, 16, 8])
            if u == 0:
                for t in range(NT):
                    load_x(0, t)
                    x_to_hT(t)
            for t in range(NT):
                x_to_hA(t)
            if STOP == "x":
                break
            ffn_phase(0, u, False)
            if u == 0:
                dump("h1", (hA_t[:, :, :], [hh_ for hh_ in hA]), [128, NT, D])
            if STOP == "ffn1":
                break
            last_ht = []
            try:
                mixer_phase(u, LAST_HT=last_ht)
            except _StopNow:
                Pool.lo, Pool.hi, Pool.nxt = 0, 8, 0
                Pool.split = False
                break
            if u == 0:
                dump("h2", (hA_t[:, :, :], [hh_ for hh_ in hA]), [128, NT, D])
            if STOP == "mixer":
                for f_ in last_ht:
                    f_()
                break
            ffn_phase(1, u, True, pre_final=(last_ht[0] if last_ht else None))

        cx.wait_tokens("act", [(h.chan2.sem, 16 * h.chan2.n) for h in hA])
        cx.wait_tokens("sp", [(c.sem, 16 * c.n) for c in cx.chans if c.n > 0])
    return nc, sorted(dbg_out)


def _consts():
    s = np.arange(128)[:, None]
    t = np.arange(128)[None, :]

    def cb(b):
        delta = 128 * b + t - s
        c = ((delta <= 128).astype(np.float32) + ((delta % 4 == 0) & (delta <= 512)).astype(np.float32)
             + (delta % 16 == 0).astype(np.float32))
        return c * (delta >= 0)

    masks = np.zeros((128, 9 * 128), np.float32)
    for i, b in enumerate((8, 7, 6, 5, 4, 3, 2, 1, 0)):
        masks[:, i * 128:(i + 1) * 128] = cb(b)
    k_ = np.arange(128)[:, None]
    l_ = np.arange(128)[None, :]
    U = (k_ <= l_).astype(np.float32)
    SL = (k_ > l_).astype(np.float32)
    neg = np.where(l_ >= k_, 0.0, NEG).astype(np.float32)
    invf = (np.float32(500000.0) ** (-np.arange(0, 16, 2, dtype=np.float32) / np.float32(16))).astype(np.float32)
    return {
        "c_masks": masks.astype(ml_dtypes.bfloat16),
        "c_identb": np.eye(128, dtype=np.float32).astype(ml_dtypes.bfloat16),
        "c_identf": np.eye(128, dtype=np.float32),
        "c_U": U, "c_SL": SL, "c_ones": np.ones((128, 128), np.float32), "c_negb": neg.astype(ml_dtypes.bfloat16),
        "c_invf": invf.reshape(1, 8),
    }


def make_in_maps(inputs, ncore, nseq):
    f = lambda a: np.ascontiguousarray(np.asarray(a, dtype=np.float32))
    shared = {
        "wg1": f(inputs["ffn1_gate"][0]), "wu1": f(inputs["ffn1_up"][0]), "wd1": f(inputs["ffn1_down"][0]),
        "wg2": f(inputs["ffn2_gate"][0]), "wu2": f(inputs["ffn2_up"][0]), "wd2": f(inputs["ffn2_down"][0]),
        "w_in": f(inputs["w_in"][0]), "w_out": f(inputs["w_out"][0]),
        "ln1_g": f(inputs["ln1_g"]), "ln1_b": f(inputs["ln1_b"]),
        "ln2_g": f(inputs["ln2_g"]), "ln2_b": f(inputs["ln2_b"]),
        "ln3_g": f(inputs["ln3_g"]), "ln3_b": f(inputs["ln3_b"]),
        "conv_w_l": f(np.asarray(inputs["conv_w"][0]).T.reshape(14, 128, 4).transpose(1, 0, 2)),
        "conv_b_l": f(np.asarray(inputs["conv_b"][0]).reshape(14, 128).T),
        "dt_bias": f(inputs["dt_bias"]), "a_log": f(inputs["a_log"]), "d_skip": f(inputs["d_skip"]),
        "attn_norm_w": f(inputs["attn_norm_w"]), "ssd_norm_w": f(inputs["ssd_norm_w"]),
    }
    shared.update(_consts())
    x = np.asarray(inputs["x"], dtype=np.float32)
    pos = np.asarray(inputs["positions"]).astype(np.int32)
    maps = []
    for c in range(ncore):
        m = dict(shared)
        m["x"] = np.ascontiguousarray(x[c * nseq:(c + 1) * nseq].reshape(nseq * SEQ, D))
        m["pos"] = np.ascontiguousarray(pos[c * nseq:(c + 1) * nseq].reshape(nseq * 16, 128))
        maps.append(m)
    return maps


_CACHE = {}


def kernel(**inputs):
    x = np.asarray(inputs["x"])
    B = x.shape[0]
    nseq = B // NCORE
    if nseq not in _CACHE:
        _CACHE[nseq] = build(nseq)[0]
    nc = _CACHE[nseq]
    maps = make_in_maps(inputs, NCORE, nseq)
    res = run_bass_kernel_spmd(nc, maps, core_ids=list(range(NCORE)))
    outs = [np.asarray(r["out"], dtype=np.float32).reshape(nseq, SEQ, D) for r in res.results]
    return np.concatenate(outs, axis=0)
```

```python
import contextlib
import numpy as np
import ml_dtypes
import concourse.bass as bass
import concourse.mybir as mybir
from concourse.bass_utils import run_bass_kernel_spmd

F32 = mybir.dt.float32
BF16 = mybir.dt.bfloat16
I32 = mybir.dt.int32
AF = mybir.ActivationFunctionType
ALU = mybir.AluOpType

D = 1024
DFF = 2816
NFC = 22
SEQ = 2048
H = 12
HD = 64
NG = 4
DIN = 4876
DMIX = 1536
TU = 512
NT = 4
UPS = SEQ // TU
NCORE = 8
ALPHA = float(2.0 ** 0.25)
LN_EPS = 1e-5
RMS_EPS = 1e-6
NEG = -30000.0
MSTART = {0: 5, 1: 1, 2: 0}

SEG_QKV = [(0, 512), (512, 512), (1024, 512), (1536, 512), (2048, 256)]
SEG_Z = [(2304, 512), (2816, 256)]
SEG_X = [(3072, 512), (3584, 512), (4096, 512), (4608, 256)]
COL_DT = 4864

ARENA = 110592
BLK = 256


class _StopNow(Exception):
    pass


class Trk:
    __slots__ = ("w", "r", "excl")

    def __init__(self, excl=False):
        self.w = {}
        self.r = {}
        self.excl = excl


class Chan:
    __slots__ = ("sem", "n")

    def __init__(self, sem):
        self.sem = sem
        self.n = 0


class Buf:
    __slots__ = ("ap", "trks", "chan", "chan2")

    def __init__(self, ap, trks, chan=None, chan2=None):
        self.ap = ap
        self.trks = trks
        self.chan = chan
        self.chan2 = chan2


def _trks(lst):
    out = []
    for b in lst:
        if isinstance(b, Trk):
            out.append(b)
        else:
            out.extend(b.trks)
    return out


class Ctx:
    def __init__(self, nc, es):
        self.nc = nc
        self.es = es
        self.eng = {}
        for name, h in (("pe", nc.tensor), ("act", nc.scalar), ("dve", nc.vector),
                        ("pool", nc.gpsimd), ("sp", nc.sync)):
            sem = es.enter_context(nc.semaphore("sem_" + name))
            self.eng[name] = {"h": h, "sem": sem, "cnt": 0, "known": {}}
        self.nchan = 0
        self.chans = []

    def chan(self):
        self.nchan += 1
        c = Chan(self.es.enter_context(self.nc.semaphore("dch%d" % self.nchan)))
        self.chans.append(c)
        return c

    def _need(self, e, reads, writes):
        need = {}

        def add(d, skip_self):
            for k, (sem, thr) in d.items():
                if skip_self and sem is e["sem"]:
                    continue
                if need.get(k, (None, 0))[1] < thr:
                    need[k] = (sem, thr)

        for t in reads:
            add(t.w, False)
            if t.excl:
                add(t.r, True)
        for t in writes:
            add(t.r, True)
            add(t.w, True)
        return need

    def _emit_waits(self, e, need):
        for k, (sem, thr) in need.items():
            if e["known"].get(k, 0) < thr:
                e["h"].wait_ge(sem, thr)
                e["known"][k] = thr

    def _commit(self, tok, reads, writes):
        k = id(tok[0])
        for t in reads:
            if t.r.get(k, (None, 0))[1] < tok[1]:
                t.r[k] = tok
        for t in writes:
            t.w = {k: tok}
            t.r = {}

    def op(self, en, fn, reads=(), writes=(), sig=True):
        e = self.eng[en]
        reads = _trks(reads)
        writes = _trks(writes)
        self._emit_waits(e, self._need(e, reads, writes))
        inst = fn()
        if sig:
            e["cnt"] += 1
            inst.then_inc(e["sem"], 1)
            tok = (e["sem"], e["cnt"])
        else:
            assert en == "pe"
            tok = (e["sem"], e["cnt"] + 1)
        self._commit(tok, reads, writes)
        return tok

    def dma(self, qn, out_ap, in_ap, chan, reads=(), writes=(), **kw):
        e = self.eng[qn]
        reads = _trks(reads)
        writes = _trks(writes)
        self._emit_waits(e, self._need(e, reads, writes))
        chan.n += 1
        e["h"].dma_start(out=out_ap, in_=in_ap, **kw).then_inc(chan.sem, 16)
        tok = (chan.sem, 16 * chan.n)
        self._commit(tok, reads, writes)
        return tok

    def wait_tokens(self, en, toks):
        e = self.eng[en]
        need = {}
        for sem, thr in toks:
            k = id(sem)
            if need.get(k, (None, 0))[1] < thr:
                need[k] = (sem, thr)
        self._emit_waits(e, need)


def build(nseq=4, dbg=(), stop=None, nunits_override=None):
    nc = bass.Bass("TRN2", target_bir_lowering=False)
    ntok = nseq * SEQ
    dbg = set(dbg)
    dbg_out = {}

    def din(name, shape, dt=F32):
        return nc.dram_tensor(name, list(shape), dt, kind="ExternalInput").ap()

    x_d = din("x", [ntok, D])
    pos_d = din("pos", [nseq * 16, 128], I32)
    wg_d = [din("wg1", [D, DFF]), din("wg2", [D, DFF])]
    wu_d = [din("wu1", [D, DFF]), din("wu2", [D, DFF])]
    wd_d = [din("wd1", [DFF, D]), din("wd2", [DFF, D])]
    win_d = din("w_in", [D, DIN])
    wout_d = din("w_out", [DMIX, D])
    lng_d = [din("ln%d_g" % i, [1, D]) for i in (1, 2, 3)]
    lnb_d = [din("ln%d_b" % i, [1, D]) for i in (1, 2, 3)]
    convw_d = din("conv_w_l", [128, 14, 4])
    convb_d = din("conv_b_l", [128, 14])
    dtb_d = din("dt_bias", [1, H])
    alog_d = din("a_log", [1, H])
    dsk_d = din("d_skip", [1, H])
    anw_d = din("attn_norm_w", [1, 768])
    snw_d = din("ssd_norm_w", [1, 768])
    masks_d = din("c_masks", [128, 9 * 128], BF16)
    identb_d = din("c_identb", [128, 128], BF16)
    identf_d = din("c_identf", [128, 128])
    cU_d = din("c_U", [128, 128])
    cSL_d = din("c_SL", [128, 128])
    cones_d = din("c_ones", [128, 128])
    cneg_d = din("c_negb", [128, 128], BF16)
    invf_d = din("c_invf", [1, 8])
    out_d = nc.dram_tensor("out", [ntok, D], F32, kind="ExternalOutput").ap()

    def dscr(name, shape):
        return nc.dram_tensor(name, list(shape), BF16, kind="Internal").ap()

    sg_s = [dscr("s_wg1", [D, DFF]), dscr("s_wg2", [D, DFF])]
    su_s = [dscr("s_wu1", [D, DFF]), dscr("s_wu2", [D, DFF])]
    sd_s = [dscr("s_wd1", [DFF, D]), dscr("s_wd2", [DFF, D])]
    sin_s = dscr("s_win", [D, DIN])
    sout_s = dscr("s_wout", [DMIX, D])

    with contextlib.ExitStack() as es:
        cx = Ctx(nc, es)

        def sb(name, shape, dt):
            return es.enter_context(nc.sbuf_tensor(name, list(shape), dt))

        def mk(name, shape, dt, chan=False):
            t = sb(name, shape, dt)
            return Buf(t, [Trk()], cx.chan() if chan else None)

        kT = mk("kT", [128, 6, SEQ], BF16)
        Vp = mk("Vp", [128, 16, H * 65], BF16)
        hstate = mk("hstate", [128, 768], F32)
        prevb = mk("prevb", [128, 768], BF16)
        tail = mk("tail", [128, 14, 3], F32)
        masks = mk("masks", [128, 9 * 128], BF16, True)
        identb = mk("identb", [128, 128], BF16, True)
        identf = mk("identf", [128, 128], F32, True)
        cU = mk("cU", [128, 128], F32, True)
        cUb = mk("cUb", [128, 128], BF16)
        cSL = mk("cSL", [128, 128], F32, True)
        cones = mk("cones", [128, 128], F32, True)
        cnegb = mk("cnegb", [128, 128], BF16, True)
        invf = mk("invf", [128, 8], F32, True)
        cost = mk("cost", [128, 16, 8], F32)
        sint = mk("sint", [128, 16, 8], F32)
        lng = mk("lng", [128, D], F32, True)
        lnb = mk("lnb", [128, D], F32, True)
        snw = mk("snw", [128, 768], F32, True)
        convw = mk("convw", [128, 14, 4], F32, True)
        convb = mk("convb", [128, 14], F32, True)
        dtb = mk("dtb", [128, H], F32, True)
        aneg = mk("aneg", [128, H], F32, True)
        dsk = mk("dsk", [128, H], F32, True)
        hA_t = sb("hA", [128, NT, D], F32)
        hA = [Buf(hA_t[:, t, :], [Trk()], cx.chan(), cx.chan()) for t in range(NT)]
        hT_h = sb("hT", [128, 8, TU], BF16)
        hT_trk = [Trk() for _ in range(NT)]
        hT = Buf(hT_h, hT_trk)
        hT_t = [Buf(hT_h, [hT_trk[t]]) for t in range(NT)]
        sgt1 = mk("sgt1", [128, 512], F32)
        st_bn = mk("st_bn", [128, 2, 6], F32)
        st_mv = mk("st_mv", [128, 2], F32)
        st_sd = mk("st_sd", [128, 1], F32)
        st_rs = mk("st_rs", [128, 1], F32)
        st_ss = mk("st_ss", [128, 1], F32)
        st_rd = mk("st_rd", [128, H], F32)

        arena_t = sb("arena", [128, ARENA // 4], F32)
        blocks = [Trk() for _ in range(ARENA // BLK)]

        def av(off, nbytes, dt, chan=False, **rr):
            assert off % 4 == 0 and nbytes % 4 == 0 and off + nbytes <= ARENA
            ap = arena_t[:, off // 4:(off + nbytes) // 4]
            if dt is not F32:
                ap = ap.bitcast(dt)
            trks = blocks[off // BLK:(off + nbytes - 1) // BLK + 1]
            return Buf(ap, trks, cx.chan() if chan else None)

        win_slot = [av(0, 8192, BF16, True), av(8192, 8192, BF16, True)]
        A_AT = 16384
        aT = av(A_AT, 22528, BF16)
        sgt = [av(A_AT + 22528, 2048, F32), sgt1]
        wout_b = av(A_AT, 24576, BF16, True)
        A_WGU = 40960
        wg_slot = [av(A_WGU + s * 8192, 4096, BF16, True) for s in range(3)]
        wu_slot = [av(A_WGU + s * 8192 + 4096, 4096, BF16, True) for s in range(3)]
        A_WD = 65536
        wd_grp = [av(A_WD + g * 4096, 4096, BF16, True) for g in range(11)]
        o = A_WGU
        zs = av(o, 6144, BF16); o += 6144
        xs_tok = av(o, 6144, BF16); o += 6144
        B_tok = av(o, 4096, BF16); o += 4096
        BTb = av(o, 4096, BF16); o += 4096
        CTb = av(o, 4096, BF16); o += 4096
        assert o == A_WD
        pc = [av(o + i * 2304, 2060, F32) for i in range(2)]; o += 4608
        cacc = [av(o + i * 2048, 2048, F32) for i in range(2)]; o += 4096
        cout = [av(o + i * 1024, 1024, BF16) for i in range(2)]; o += 2048
        xdt = av(o, 1536, BF16); o += 1536
        xdtd = av(o, 1536, BF16); o += 1536
        LT = [av(o + i * 1536, 1536, F32) for i in range(4)]; o += 6144
        MT = [av(o + i * 768, 768, BF16) for i in range(4)]; o += 3072
        yrow = [av(o, 3072, F32)] * 2; o += 3072
        ytmp = [av(o, 768, F32)] * 2; o += 768
        wdt_b = av(o, 192, BF16, True); o += 256
        dtv = av(o, 192, F32); o += 256
        dAb = av(o, 192, F32); o += 256
        Eb = av(o, 576, F32); o += 768
        nac = av(o, 192, F32); o += 192
        ddb = av(o, 48, F32); o += 64
        draw = av(o, 48, F32); o += 128
        dAhi = av(o, 96, BF16); o += 256
        dAlo = av(o, 96, BF16); o += 256
        dAtmp = av(o, 192, F32); o += 256
        assert o <= 95232
        o = A_WD
        qT = av(o, 6144, BF16); o += 6144
        qkst = [av(o + i * 1024, 1024, BF16) for i in range(2)]; o += 2048
        rtmp = [av(o + i * 256, 256, F32) for i in range(4)]; o += 1024
        PT = [av(o + i * 1024, 1024, BF16) for i in range(4)] + [av(88064 + i * 1024, 1024, BF16) for i in range(6)]; o += 4096
        arow = [av(o + i * 3072, 3072, F32) for i in range(2)]; o += 6144
        anw = av(84992, 3072, F32, True)
        assert o <= 84992
        mixT = av(95232, 12288, BF16)
        mixn = [av(107520 + i * 1536, 1536, BF16) for i in range(2)]
        stg_f = [av(i * 19712, 19504, F32, True) for i in range(3)]
        stg_b = [av(59392 + i * 9984, 9752, BF16, True) for i in range(3)]

        ps_t = [es.enter_context(nc.psum_tensor("ps%d" % i, [128, 512], F32)) for i in range(8)]
        ps = [Buf(ps_t[i], [Trk(excl=True)]) for i in range(8)]

        class Pool:
            lo, hi, nxt = 0, 8, 0
            split = False

        class Aux:
            lo, hi, nxt = 4, 6, 4

        def auxget(n=1):
            if not Pool.split:
                return psget(n)
            p = Aux.nxt
            if p + n > Aux.hi:
                p = Aux.lo
            Aux.nxt = p + n
            if Aux.nxt >= Aux.hi:
                Aux.nxt = Aux.lo
            return [ps[p + i] for i in range(n)]

        def psget(n=1):
            p = Pool.nxt
            if p % n:
                p += n - p % n
            if p + n > Pool.hi:
                p = Pool.lo
            Pool.nxt = p + n
            if Pool.nxt >= Pool.hi:
                Pool.nxt = Pool.lo
            return [ps[p + i] for i in range(n)]

        V = nc.vector
        A = nc.scalar
        G = nc.gpsimd
        T = nc.tensor

        def dump(name, buf, shape, dt=F32):
            if name not in dbg:
                return
            d = nc.dram_tensor("dbg_" + name, list(shape), dt, kind="ExternalOutput").ap()
            ch = cx.chan()
            tok = cx.dma("sp", d, buf.ap[:] if isinstance(buf, Buf) else buf[0], ch,
                         reads=[buf] if isinstance(buf, Buf) else buf[1])
            cx.wait_tokens("sp", [tok])
            dbg_out[name] = None

        def cload(buf, src):
            cx.dma("sp", buf.ap[:], src, buf.chan, writes=[buf])

        cload(masks, masks_d)
        cload(identb, identb_d)
        cload(identf, identf_d)
        cload(cU, cU_d)
        cload(cSL, cSL_d)
        cload(cones, cones_d)
        cload(cnegb, cneg_d)
        cload(invf, invf_d[0:1, :].partition_broadcast(128))
        cload(snw, snw_d[0:1, :].partition_broadcast(128))
        cload(convw, convw_d)
        cload(convb, convb_d)
        cload(dtb, dtb_d[0:1, :].partition_broadcast(128))
        cload(aneg, alog_d[0:1, :].partition_broadcast(128))
        cload(dsk, dsk_d[0:1, :].partition_broadcast(128))
        cx.op("act", lambda: A.activation(out=aneg.ap[:], in_=aneg.ap[:], func=AF.Exp), reads=[aneg], writes=[aneg])
        cx.op("dve", lambda: V.tensor_scalar(out=aneg.ap[:], in0=aneg.ap[:], scalar1=-1.0, scalar2=None, op0=ALU.mult),
              reads=[aneg], writes=[aneg])
        cx.op("dve", lambda: V.tensor_copy(out=cUb.ap[:], in_=cU.ap[:]), reads=[cU], writes=[cUb])
        cx.op("pool", lambda: G.memset(Vp.ap[:], 1.0), writes=[Vp])

        RB = 95232
        nti = 16
        posi = av(RB, 512, I32, True)
        posf = av(RB + 512, 512, F32)
        post = av(RB + 1024, 64, F32)
        ang = av(RB + 1088, 512, F32)
        kint = av(RB + 1600, 512, I32)
        kf = av(RB + 2112, 512, F32)
        tgt = av(RB + 2624, 512, F32)
        TWO_PI = 2.0 * np.pi
        C1 = 6.28125
        C2 = float(TWO_PI - 6.28125)

        def sin_table(dst, shift):
            cx.op("dve", lambda: V.tensor_scalar(out=kint.ap[:, :], in0=ang.ap[:, :], scalar1=float(shift),
                                                 scalar2=float(1.0 / TWO_PI), op0=ALU.add, op1=ALU.mult),
                  reads=[ang], writes=[kint])
            cx.op("dve", lambda: V.tensor_copy(out=kf.ap[:, :], in_=kint.ap[:, :]), reads=[kint], writes=[kf])
            cx.op("dve", lambda: V.scalar_tensor_tensor(out=tgt.ap[:, :], in0=kf.ap[:, :], scalar=-C1, in1=ang.ap[:, :],
                                                        op0=ALU.mult, op1=ALU.add), reads=[kf, ang], writes=[tgt])
            cx.op("dve", lambda: V.scalar_tensor_tensor(out=tgt.ap[:, :], in0=kf.ap[:, :], scalar=-C2, in1=tgt.ap[:, :],
                                                        op0=ALU.mult, op1=ALU.add), reads=[kf, tgt], writes=[tgt])
            if shift:
                cx.op("dve", lambda: V.tensor_scalar(out=tgt.ap[:, :], in0=tgt.ap[:, :], scalar1=float(shift), scalar2=None,
                                                     op0=ALU.add), reads=[tgt], writes=[tgt])
            cx.op("dve", lambda: V.tensor_scalar(out=kf.ap[:, :], in0=tgt.ap[:, :], scalar1=float(np.pi), scalar2=float(-TWO_PI),
                                                 op0=ALU.is_gt, op1=ALU.mult), reads=[tgt], writes=[kf])
            cx.op("dve", lambda: V.tensor_tensor(out=tgt.ap[:, :], in0=tgt.ap[:, :], in1=kf.ap[:, :], op=ALU.add),
                  reads=[tgt, kf], writes=[tgt])
            cx.op("dve", lambda: V.tensor_scalar(out=tgt.ap[:, :], in0=tgt.ap[:, :], scalar1=float(-np.pi), scalar2=float(np.pi),
                                                 op0=ALU.max, op1=ALU.min), reads=[tgt], writes=[tgt])
            cx.op("act", lambda: A.activation(out=dst.ap[:, :, :].rearrange("p t f -> p (t f)"), in_=tgt.ap[:, :], func=AF.Sin),
                  reads=[tgt], writes=[dst])

        def rotary_tables(sq):
            cx.dma("sp", posi.ap[0:nti, :], pos_d[sq * 16:(sq + 1) * 16, :], posi.chan, writes=[posi])
            cx.op("dve", lambda: V.tensor_copy(out=posf.ap[0:nti, :], in_=posi.ap[0:nti, :]), reads=[posi], writes=[posf])
            bk = psget(1)[0]
            cx.op("pe", lambda: T.transpose(bk.ap[:, 0:nti], posf.ap[0:nti, :], identf.ap[0:nti, 0:nti]),
                  reads=[posf, identf], writes=[bk])
            cx.op("dve", lambda: V.tensor_copy(out=post.ap[:, :], in_=bk.ap[:, 0:nti]), reads=[bk], writes=[post])
            cx.op("dve", lambda: V.tensor_tensor(out=ang.ap[:, :].rearrange("p (t f) -> p t f", f=8),
                                                 in0=post.ap[:, :].unsqueeze(2).to_broadcast([128, nti, 8]),
                                                 in1=invf.ap[:, :].unsqueeze(1).to_broadcast([128, nti, 8]), op=ALU.mult),
                  reads=[post, invf], writes=[ang])
            sin_table(sint, 0.0)
            sin_table(cost, np.pi / 2)

        store_toks = []
        rr = [0]
        cast_eng = ["dve", "act"]

        NSTG = 9
        PCW = 2048
        stg_f = [av(i * 12288, 8192, F32, True) for i in range(NSTG)]
        stg_b = [av(i * 12288 + 8192, 4096, BF16, True) for i in range(NSTG)]

        def cast_matrix(src, dst, rows, cols):
            for rc in range(rows // 128):
                for c0 in range(0, cols, PCW):
                    n = min(PCW, cols - c0)
                    i = rr[0] % NSTG
                    e = cast_eng[rr[0] % len(cast_eng)]
                    rr[0] += 1
                    f, b = stg_f[i], stg_b[i]
                    cx.dma("sp", f.ap[:, 0:n], src[rc * 128:(rc + 1) * 128, c0:c0 + n], f.chan, writes=[f])
                    if e == "dve":
                        cx.op("dve", lambda: V.tensor_copy(out=b.ap[:, 0:n], in_=f.ap[:, 0:n]), reads=[f], writes=[b])
                    elif e == "act":
                        cx.op("act", lambda: A.copy(out=b.ap[:, 0:n], in_=f.ap[:, 0:n]), reads=[f], writes=[b])
                    else:
                        cx.op("pool", lambda: G.tensor_copy(out=b.ap[:, 0:n], in_=f.ap[:, 0:n]), reads=[f], writes=[b])
                    store_toks.append(cx.dma("act", dst[rc * 128:(rc + 1) * 128, c0:c0 + n], b.ap[:, 0:n], b.chan, reads=[b]))

        jobs = [(wg_d[0], sg_s[0], D, DFF), (wu_d[0], su_s[0], D, DFF), (wd_d[0], sd_s[0], DFF, D), (win_d, sin_s, D, DIN),
                (wout_d, sout_s, DMIX, D), (wg_d[1], sg_s[1], D, DFF), (wu_d[1], su_s[1], D, DFF), (wd_d[1], sd_s[1], DFF, D)]
        if stop == "const":
            jobs = []
        if stop and stop.startswith("cast:"):
            cast_eng = stop[5:].split(",")
            jobs = jobs[:1]
        if stop and stop.startswith("castn:"):
            sel = [int(v) for v in stop[6:].split(",")]
            jobs = [jobs[i] for i in sel]
        for jb in jobs:
            cast_matrix(*jb)
        cx.wait_tokens("sp", store_toks)
        STOP = stop

        def load_ln(k, final):
            cx.dma("sp", lng.ap[:], lng_d[k][0:1, :].partition_broadcast(128), lng.chan, writes=[lng])
            cx.dma("sp", lnb.ap[:], lnb_d[k][0:1, :].partition_broadcast(128), lnb.chan, writes=[lnb])
            if not final:
                cx.op("act", lambda: A.mul(out=lnb.ap[:], in_=lnb.ap[:], mul=ALPHA), reads=[lnb], writes=[lnb])

        def make_hT(t, src=None, scale=1.0 / ALPHA):
            h = hA[t] if src is None else src
            for half in range(2):
                bk = auxget(1)[0]
                for j in range(4):
                    c = half * 4 + j
                    cx.op("pe", lambda: T.transpose(bk.ap[:, j * 128:(j + 1) * 128], h.ap[:, c * 128:(c + 1) * 128], identf.ap[:]),
                          reads=[h, identf], writes=[bk], sig=(j == 3))
                cx.op("act", lambda: A.activation(out=hT.ap[:, half * 4:half * 4 + 4, t * 128:(t + 1) * 128],
                                                  in_=bk.ap[:, :].rearrange("p (c k) -> p c k", c=4),
                                                  func=AF.Copy, scale=float(scale)),
                      reads=[bk], writes=[hT_t[t]])

        def layer_norm(t, final, lnexp=False):
            h = hA[t]
            for i in range(2):
                cx.op("dve", lambda: V.bn_stats(out=st_bn.ap[:, i, :], in_=h.ap[:, i * 512:(i + 1) * 512]),
                      reads=[h], writes=[st_bn])
            cx.op("dve", lambda: V.bn_aggr(out=st_mv.ap[:], in_=st_bn.ap[:, :, :].rearrange("p a b -> p (a b)")),
                  reads=[st_bn], writes=[st_mv])
            sc = 1.0 if final else 1.0 / (ALPHA * ALPHA)
            if lnexp:
                cx.op("act", lambda: A.activation(out=st_sd.ap[:], in_=st_mv.ap[:, 1:2], func=AF.Ln, scale=float(sc),
                                                  bias=float(LN_EPS * sc)), reads=[st_mv], writes=[st_sd])
                cx.op("act", lambda: A.activation(out=st_rs.ap[:], in_=st_sd.ap[:], func=AF.Exp, scale=-0.5), reads=[st_sd], writes=[st_rs])
            else:
                cx.op("act", lambda: A.activation(out=st_sd.ap[:], in_=st_mv.ap[:, 1:2], func=AF.Sqrt, scale=float(sc),
                                                  bias=float(LN_EPS * sc)), reads=[st_mv], writes=[st_sd])
                cx.op("dve", lambda: V.reciprocal(out=st_rs.ap[:], in_=st_sd.ap[:]), reads=[st_sd], writes=[st_rs])
            cx.op("dve", lambda: V.scalar_tensor_tensor(out=h.ap[:, :], in0=h.ap[:, :], scalar=st_mv.ap[:, 0:1], in1=lng.ap[:, :],
                                                        op0=ALU.subtract, op1=ALU.mult), reads=[h, st_mv, lng], writes=[h])
            cx.op("dve", lambda: V.scalar_tensor_tensor(out=h.ap[:, :], in0=h.ap[:, :], scalar=st_rs.ap[:, 0:1], in1=lnb.ap[:, :],
                                                        op0=ALU.mult, op1=ALU.add), reads=[h, st_rs, lnb], writes=[h])

        slot_rr = [0]
        NUNITS = [0]

        xst = [av(t * 4096, 4096, F32, True) for t in range(NT)]

        def load_x(u, t):
            tk = u * TU
            cx.dma("sp", xst[t].ap[:, :], x_d[tk + t * 128:tk + (t + 1) * 128, :], xst[t].chan, writes=[xst[t]])

        def x_to_hT(t):
            make_hT(t, src=xst[t], scale=1.0)

        def x_to_hA(t):
            h = hA[t]
            cx.op("act", lambda: A.mul(out=h.ap[:, :], in_=xst[t].ap[:, :], mul=ALPHA), reads=[xst[t]], writes=[h])

        def ffn_phase(k, u, final, pre_final=None):
            tok0 = u * TU
            sg_, su_, sd_ = sg_s[k], su_s[k], sd_s[k]
            state = {"nld": 0}

            def load_gu(fcg):
                s = slot_rr[0] % 3
                slot_rr[0] += 1
                c0 = fcg * 256
                cx.dma("sp", wg_slot[s].ap[:, :].rearrange("p (c n) -> p c n", c=8),
                       sg_[:, c0:c0 + 256].rearrange("(c p) n -> p c n", p=128), wg_slot[s].chan, writes=[wg_slot[s]])
                cx.dma("sp", wu_slot[s].ap[:, :].rearrange("p (c n) -> p c n", c=8),
                       su_[:, c0:c0 + 256].rearrange("(c p) n -> p c n", p=128), wu_slot[s].chan, writes=[wu_slot[s]])
                return s

            def load_d(g):
                cx.dma("sp", wd_grp[g].ap[:, :].rearrange("p (j n) -> p j n", j=2),
                       sd_[g * 256:(g + 1) * 256, :].rearrange("(j p) n -> p j n", p=128), wd_grp[g].chan, writes=[wd_grp[g]])

            slots = {}
            for g in range(3):
                slots[g] = load_gu(g)
            load_ln(0 if k == 0 else 2, final)
            load_d(0)
            early = {}
            EC = (NT - 1) * 128
            if pre_final is not None:
                for fc in range(3):
                    fcg_, j_ = divmod(fc, 2)
                    s_ = slots[fcg_]
                    bg, bu = psget(2)
                    early[fc] = (bg, bu)
                    for bank, wsl in ((bg, wg_slot[s_]), (bu, wu_slot[s_])):
                        w3_ = wsl.ap[:, :].rearrange("p (c n) -> p c n", c=8)
                        for c in range(8):
                            cx.op("pe", lambda: T.matmul(bank.ap[:, 0:EC], lhsT=w3_[:, c, j_ * 128:(j_ + 1) * 128], rhs=hT.ap[:, c, 0:EC],
                                                         start=(c == 0), stop=(c == 7)),
                                  reads=[wsl] + hT_t[0:NT - 1], writes=[bank], sig=(c == 7))
                pre_final()
            for fcg in range(11):
                s = slots[fcg]
                wg3 = wg_slot[s].ap[:, :].rearrange("p (c n) -> p c n", c=8)
                wu3 = wu_slot[s].ap[:, :].rearrange("p (c n) -> p c n", c=8)
                for j in range(2):
                    fc = fcg * 2 + j
                    if fc in early:
                        bg, bu = early[fc]
                        c_lo = EC
                        rd = [hT_t[NT - 1]]
                    else:
                        bg, bu = psget(2)
                        c_lo = 0
                        rd = [hT]
                    for c in range(8):
                        cx.op("pe", lambda: T.matmul(bg.ap[:, c_lo:TU], lhsT=wg3[:, c, j * 128:(j + 1) * 128], rhs=hT.ap[:, c, c_lo:TU],
                                                     start=(c == 0), stop=(c == 7)),
                              reads=[wg_slot[s]] + rd, writes=[bg], sig=(c == 7))
                    for c in range(8):
                        cx.op("pe", lambda: T.matmul(bu.ap[:, c_lo:TU], lhsT=wu3[:, c, j * 128:(j + 1) * 128], rhs=hT.ap[:, c, c_lo:TU],
                                                     start=(c == 0), stop=(c == 7)),
                              reads=[wu_slot[s]] + rd, writes=[bu], sig=(c == 7))
                    sgb = sgt[fc % 2]
                    cx.op("act", lambda: A.activation(out=sgb.ap[:, :], in_=bg.ap[:, :], func=AF.Silu), reads=[bg], writes=[sgb])
                    cx.op("dve", lambda: V.tensor_tensor(out=aT.ap[:, fc * 512:(fc + 1) * 512], in0=sgb.ap[:, :], in1=bu.ap[:, :],
                                                         op=ALU.mult), reads=[sgb, bu], writes=[aT])
                if fcg + 3 < 11:
                    slots[fcg + 3] = load_gu(fcg + 3)
                if fcg + 1 < 11:
                    load_d(fcg + 1)
            pending = None
            if final and u + 1 < NUNITS[0]:
                for t in range(NT):
                    load_x(u + 1, t)
            for t in range(NT):
                b0, b1 = psget(2)
                for half, bk in ((0, b0), (1, b1)):
                    for fc in range(NFC):
                        g = fc // 2
                        wd3 = wd_grp[g].ap[:, :].rearrange("p (j n) -> p j n", j=2)
                        cx.op("pe", lambda: T.matmul(bk.ap[:, :], lhsT=aT.ap[:, fc * 512 + t * 128:fc * 512 + (t + 1) * 128],
                                                     rhs=wd3[:, fc % 2, half * 512:(half + 1) * 512],
                                                     start=(fc == 0), stop=(fc == NFC - 1)),
                              reads=[aT, wd_grp[g]], writes=[bk], sig=(fc == NFC - 1))
                if pending is not None:
                    pending()
                    pending = None
                h = hA[t]
                for half, bk in ((0, b0), (1, b1)):
                    cx.op("dve", lambda: V.scalar_tensor_tensor(out=h.ap[:, half * 512:(half + 1) * 512], in0=bk.ap[:, :], scalar=0.5,
                                                                in1=h.ap[:, half * 512:(half + 1) * 512], op0=ALU.mult, op1=ALU.add),
                          reads=[bk, h], writes=[h])
                layer_norm(t, final)
                if final:
                    cx.dma("act", out_d[tok0 + t * 128:tok0 + (t + 1) * 128, :], h.ap[:, :], h.chan2, reads=[h])
                    if u + 1 < NUNITS[0]:
                        pending = (lambda tt: (lambda: x_to_hT(tt)))(t)
                else:
                    pending = (lambda tt: (lambda: make_hT(tt)))(t)
            if pending is not None:
                pending()

        def load_win(slot, col0, ncols):
            b = win_slot[slot]
            cx.dma("sp", b.ap[:, :].rearrange("p (c n) -> p c n", c=8)[:, :, 0:ncols],
                   sin_s[:, col0:col0 + ncols].rearrange("(c p) n -> p c n", p=128), b.chan, writes=[b])

        def rms_norm(row, wbc, mslot):
            mn = mixn[mslot]
            cx.op("act", lambda: A.activation(out=mn.ap[:, :], in_=row.ap[:, :], func=AF.Square, accum_out=st_ss.ap[:, 0:1]),
                  reads=[row], writes=[mn, st_ss])
            cx.op("act", lambda: A.activation(out=st_sd.ap[:], in_=st_ss.ap[:], func=AF.Ln, scale=float(1.0 / 768.0),
                                              bias=float(RMS_EPS)), reads=[st_ss], writes=[st_sd])
            cx.op("act", lambda: A.activation(out=st_rs.ap[:], in_=st_sd.ap[:], func=AF.Exp, scale=-0.5), reads=[st_sd], writes=[st_rs])
            cx.op("dve", lambda: V.scalar_tensor_tensor(out=mn.ap[:, :], in0=row.ap[:, :], scalar=st_rs.ap[:, 0:1], in1=wbc.ap[:, :],
                                                        op0=ALU.mult, op1=ALU.mult), reads=[row, st_rs, wbc], writes=[mn])

            def do_T(t, c0):
                bk = auxget(1)[0]
                pb = bk.ap[:, :].bitcast(BF16)
                for c in range(6):
                    cx.op("pe", lambda: T.transpose(pb[:, c * 128:(c + 1) * 128], mn.ap[:, c * 128:(c + 1) * 128], identb.ap[:]),
                          reads=[mn, identb], writes=[bk], sig=(c == 5))
                m3 = mixT.ap[:, :].rearrange("p (c k) -> p c k", c=12)
                cx.op("act", lambda: A.copy(out=m3[:, c0:c0 + 6, t * 128:(t + 1) * 128],
                                            in_=pb[:, 0:768].rearrange("p (c k) -> p c k", c=6)), reads=[bk], writes=[mixT])
            return do_T

        def chk(name):
            if STOP == name:
                raise _StopNow()

        def mixer_phase(u, LAST_HT=None):
            q = u % UPS
            sq = u // UPS
            tok0 = u * TU
            m3 = mixT.ap[:, :].rearrange("p (c k) -> p c k", c=12)
            ws = [0]

            def next_seg(col0, ncols):
                s = ws[0] % 2
                ws[0] += 1
                load_win(s, col0, ncols)
                return s

            if q == 0:
                cx.op("pool", lambda: G.memset(hstate.ap[:], 0.0), writes=[hstate])
                cx.op("pool", lambda: G.memset(prevb.ap[:], 0.0), writes=[prevb])
                cx.op("pool", lambda: G.memset(tail.ap[:], 0.0), writes=[tail])

            def proj_tok(slot, ncols, t, bk):
                w3 = win_slot[slot].ap[:, :].rearrange("p (c n) -> p c n", c=8)
                for c in range(8):
                    cx.op("pe", lambda: T.matmul(bk.ap[:, 0:ncols], lhsT=hT.ap[:, c, t * 128:(t + 1) * 128], rhs=w3[:, c, 0:ncols],
                                                 start=(c == 0), stop=(c == 7)),
                          reads=[hT_t[t], win_slot[slot]], writes=[bk], sig=(c == 7))

            z3 = zs.ap[:, :].rearrange("p (t n) -> p t n", t=NT)
            dt3 = dtv.ap[:, :].rearrange("p (t n) -> p t n", t=NT)
            dA3 = dAb.ap[:, :].rearrange("p (t n) -> p t n", t=NT)
            hi3 = dAhi.ap[:, :].rearrange("p (t n) -> p t n", t=NT)
            lo3 = dAlo.ap[:, :].rearrange("p (t n) -> p t n", t=NT)
            wdt3 = wdt_b.ap[:, :].rearrange("p (c n) -> p c n", c=8)
            xs4 = xs_tok.ap[:, :].rearrange("p (t n) -> p t n", t=NT)
            Bt4 = B_tok.ap[:, :].rearrange("p (t n) -> p t n", t=NT)
            BT3 = BTb.ap[:, :].rearrange("p (g k) -> p g k", g=NG)
            CT3 = CTb.ap[:, :].rearrange("p (g k) -> p g k", g=NG)

            def z_step(s_, ncols, t, zoff):
                def f():
                    bk = psget(1)[0]
                    proj_tok(s_(), ncols, t, bk)
                    cx.op("act", lambda: A.activation(out=z3[:, t, zoff:zoff + ncols], in_=bk.ap[:, 0:ncols], func=AF.Silu),
                          reads=[bk], writes=[zs])
                return f

            def dt_step(t):
                def f():
                    if t == 0:
                        cx.dma("sp", wdt_b.ap[:, :].rearrange("p (c n) -> p c n", c=8),
                               sin_s[:, COL_DT:COL_DT + H].rearrange("(c p) n -> p c n", p=128), wdt_b.chan, writes=[wdt_b])
                    bk = psget(1)[0]
                    for c in range(8):
                        cx.op("pe", lambda: T.matmul(bk.ap[:, 0:H], lhsT=hT.ap[:, c, t * 128:(t + 1) * 128], rhs=wdt3[:, c, :],
                                                     start=(c == 0), stop=(c == 7)), reads=[hT_t[t], wdt_b], writes=[bk], sig=(c == 7))
                    cx.op("dve", lambda: V.tensor_tensor(out=draw.ap[:, :], in0=bk.ap[:, 0:H], in1=dtb.ap[:, :], op=ALU.add),
                          reads=[bk, dtb], writes=[draw])
                    cx.op("act", lambda: A.activation(out=draw.ap[:, :], in_=draw.ap[:, :], func=AF.Exp), reads=[draw], writes=[draw])
                    cx.op("act", lambda: A.activation(out=dt3[:, t, :], in_=draw.ap[:, :], func=AF.Ln, bias=1.0), reads=[draw], writes=[dtv])
                    cx.op("dve", lambda: V.tensor_tensor(out=dA3[:, t, :], in0=dt3[:, t, :], in1=aneg.ap[:, :], op=ALU.mult),
                          reads=[dtv, aneg], writes=[dAb])
                    if t == NT - 1:
                        cx.op("dve", lambda: V.tensor_copy(out=dAhi.ap[:, :], in_=dAb.ap[:, :]), reads=[dAb], writes=[dAhi])
                        cx.op("dve", lambda: V.tensor_tensor(out=dAtmp.ap[:, :], in0=dAb.ap[:, :], in1=dAhi.ap[:, :], op=ALU.subtract),
                              reads=[dAb, dAhi], writes=[dAtmp])
                        cx.op("dve", lambda: V.tensor_copy(out=dAlo.ap[:, :], in_=dAtmp.ap[:, :]), reads=[dAtmp], writes=[dAlo])
                return f

            zslot = {}
            extra = []
            zo = 0
            for zi, (c0, ncols) in enumerate(SEG_Z):
                for t in range(NT):
                    extra.append(("z", zi, c0, ncols, t, zo))
                zo += ncols
            for t in range(NT):
                extra.append(("dt", t))

            def run_extra():
                if not extra:
                    return
                e = extra.pop(0)
                if e[0] == "z":
                    _, zi, c0, ncols, t, zo_ = e
                    if zi not in zslot:
                        zslot[zi] = next_seg(c0, ncols)
                    z_step(lambda: zslot[zi], ncols, t, zo_)()
                else:
                    dt_step(e[1])()

            for _ in range(NT):
                e_ = [x for x in extra if x[0] == "dt"][0]
                extra.remove(e_)
                dt_step(e_[1])()
            early = [x for x in extra if x[0] == "z" and x[1] == 0 and x[4] < NT - 1]
            for e_ in early:
                extra.remove(e_)
                extra.insert(0, e_)
            for _ in range(len(early)):
                run_extra()
            ch = 0
            deferred = []
            PEND_SILU = []

            def conv_T(ch, dst_b, dst):
                def f():
                    bt = psget(1)[0]
                    pb = bt.ap[:, :].bitcast(BF16)
                    for t in range(NT):
                        cx.op("pe", lambda: T.transpose(pb[:, t * 128:(t + 1) * 128], dst[:, t * 128:(t + 1) * 128], identb.ap[:]),
                              reads=[dst_b, identb], writes=[bt], sig=(t == NT - 1))
                    if ch < 6:
                        cx.op("act", lambda: A.copy(out=xs4[:, :, ch * 128:(ch + 1) * 128],
                                                    in_=pb[:, 0:512].rearrange("p (t k) -> p t k", t=NT)),
                              reads=[bt], writes=[xs_tok])
                    else:
                        g = ch - 6
                        cx.op("act", lambda: A.copy(out=Bt4[:, :, g * 128:(g + 1) * 128],
                                                    in_=pb[:, 0:512].rearrange("p (t k) -> p t k", t=NT)),
                              reads=[bt], writes=[B_tok])
                return f

            for (c0, ncols) in SEG_X:
                s = next_seg(c0, ncols)
                w3 = win_slot[s].ap[:, :].rearrange("p (c n) -> p c n", c=8)
                for lc in range(ncols // 128):
                    bk = psget(1)[0]
                    for c in range(8):
                        cx.op("pe", lambda: T.matmul(bk.ap[:, :], lhsT=w3[:, c, lc * 128:(lc + 1) * 128], rhs=hT.ap[:, c, :],
                                                     start=(c == 0), stop=(c == 7)), reads=[win_slot[s], hT], writes=[bk], sig=(c == 7))
                    if len(deferred) >= 2:
                        deferred.pop(0)()
                    p_ = pc[ch % 2]
                    ca = cacc[ch % 2]
                    cx.op("pool", lambda: G.tensor_copy(out=p_.ap[:, 0:3], in_=tail.ap[:, ch, :]), reads=[tail], writes=[p_])
                    cx.op("act", lambda: A.copy(out=p_.ap[:, 3:515], in_=bk.ap[:, :]), reads=[bk], writes=[p_])
                    cx.op("pool", lambda: G.tensor_copy(out=tail.ap[:, ch, :], in_=p_.ap[:, 512:515]), reads=[p_], writes=[tail])
                    cx.op("act", lambda: A.activation(out=ca.ap[:, :], in_=p_.ap[:, 0:512], func=AF.Copy, scale=convw.ap[:, ch, 0:1]),
                          reads=[p_, convw], writes=[ca])
                    if PEND_SILU:
                        PEND_SILU.pop(0)()
                    for kk in range(1, 4):
                        cx.op("dve", lambda: V.scalar_tensor_tensor(out=ca.ap[:, :], in0=p_.ap[:, kk:kk + 512],
                                                                    scalar=convw.ap[:, ch, kk:kk + 1], in1=ca.ap[:, :],
                                                                    op0=ALU.mult, op1=ALU.add), reads=[p_, convw, ca], writes=[ca])
                    if ch < 6:
                        dst_b, dst = cout[ch % 2], cout[ch % 2].ap[:, :]
                    elif ch < 10:
                        dst_b, dst = BTb, BT3[:, ch - 6, :]
                    else:
                        dst_b, dst = CTb, CT3[:, ch - 10, :]

                    def fin(ch=ch, ca=ca, dst_b=dst_b, dst=dst):
                        cx.op("act", lambda: A.activation(out=dst, in_=ca.ap[:, :], func=AF.Silu, bias=convb.ap[:, ch:ch + 1]),
                              reads=[ca, convb], writes=[dst_b])
                        if ch < 10:
                            deferred.append(conv_T(ch, dst_b, dst))
                    PEND_SILU.append(fin)
                    ch += 1
                    if extra and extra[0][0] == "z":
                        run_extra()
            while PEND_SILU:
                PEND_SILU.pop(0)()
            while deferred:
                deferred.pop(0)()
            while extra:
                run_extra()
            while deferred:
                deferred.pop(0)()
            assert ch == 14
            chk("mix_conv")
            dump("zs", zs, [128, NT * 768], BF16)
            dump("xs_tok", xs_tok, [128, NT * 768], BF16)
            dump("B_tok", B_tok, [128, NT * 512], BF16)
            dump("CTb", CTb, [128, NG * 512], BF16)
            dump("dtv", dtv, [128, NT * H])
            pre_seg = [next_seg(*SEG_QKV[0]), next_seg(*SEG_QKV[1])]
            cx.dma("sp", wout_b.ap[:, :].rearrange("p (c n) -> p c n", c=12),
                   sout_s[:, :].rearrange("(c p) n -> p c n", p=128), wout_b.chan, writes=[wout_b])
            load_ln(1, False)
            q3 = qT.ap[:, :].rearrange("p (c k) -> p c k", c=6)
            kbase = q * TU
            PQ = []
            seg_slot = {0: pre_seg[0], 1: pre_seg[1]}
            qkv_steps = [(si, c0, ncols, t) for si, (c0, ncols) in enumerate(SEG_QKV) for t in range(NT)]

            def run_qkv():
                if not qkv_steps:
                    return
                si, c0, ncols, t = qkv_steps.pop(0)
                if si not in seg_slot:
                    seg_slot[si] = next_seg(c0, ncols)
                if t == 0 and si + 1 < len(SEG_QKV) and (si + 1) not in seg_slot:
                    seg_slot[si + 1] = next_seg(*SEG_QKV[si + 1])
                s = seg_slot[si]
                gt = q * NT + t
                bk = psget(1)[0]
                proj_tok(s, ncols, t, bk)
                if len(PQ) >= 2:
                    PQ.pop(0)()
                if si < 3:
                    nh = ncols // 64
                    st = qkst[(si * NT + t) % 2]
                    b3 = bk.ap[:, 0:ncols].rearrange("p (h d) -> p h d", h=nh)
                    s3 = st.ap[:, 0:ncols].rearrange("p (h d) -> p h d", h=nh)
                    cb = cost.ap[:, gt, :].unsqueeze(1).to_broadcast([128, nh, 8])
                    sn = sint.ap[:, gt, :].unsqueeze(1).to_broadcast([128, nh, 8])
                    r = [rt.ap[:, :].rearrange("p (h d) -> p h d", h=8)[:, 0:nh, :] for rt in rtmp]
                    cx.op("act", lambda: A.copy(out=st.ap[:, 0:ncols], in_=bk.ap[:, 0:ncols]), reads=[bk], writes=[st])
                    cx.op("dve", lambda: V.tensor_tensor(out=r[0], in0=b3[:, :, 0:8], in1=cb, op=ALU.mult), reads=[bk, cost], writes=[rtmp[0]])
                    cx.op("dve", lambda: V.tensor_tensor(out=r[1], in0=b3[:, :, 8:16], in1=sn, op=ALU.mult), reads=[bk, sint], writes=[rtmp[1]])
                    cx.op("dve", lambda: V.tensor_tensor(out=r[2], in0=b3[:, :, 8:16], in1=cb, op=ALU.mult), reads=[bk, cost], writes=[rtmp[2]])
                    cx.op("dve", lambda: V.tensor_tensor(out=r[3], in0=b3[:, :, 0:8], in1=sn, op=ALU.mult), reads=[bk, sint], writes=[rtmp[3]])
                    cx.op("dve", lambda: V.tensor_tensor(out=s3[:, :, 0:8], in0=r[0], in1=r[1], op=ALU.subtract),
                          reads=[rtmp[0], rtmp[1]], writes=[st])
                    cx.op("dve", lambda: V.tensor_tensor(out=s3[:, :, 8:16], in0=r[2], in1=r[3], op=ALU.add),
                          reads=[rtmp[2], rtmp[3]], writes=[st])

                    def qk_T(st=st, ncols=ncols, c0=c0, t=t):
                        bt = psget(1)[0]
                        pb = bt.ap[:, :].bitcast(BF16)
                        npair = ncols // 128
                        for pi in range(npair):
                            cx.op("pe", lambda: T.transpose(pb[:, pi * 128:(pi + 1) * 128], st.ap[:, pi * 128:(pi + 1) * 128], identb.ap[:]),
                                  reads=[st, identb], writes=[bt], sig=(pi == npair - 1))
                        gp0 = c0 // 128
                        nq = max(0, min(6 - gp0, npair))
                        if nq > 0:
                            cx.op("act", lambda: A.copy(out=q3[:, gp0:gp0 + nq, t * 128:(t + 1) * 128],
                                                        in_=pb[:, 0:nq * 128].rearrange("p (c k) -> p c k", c=nq)),
                                  reads=[bt], writes=[qT])
                        if npair - nq > 0:
                            nk = npair - nq
                            kp0 = gp0 + nq - 6
                            cx.op("act", lambda: A.copy(out=kT.ap[:, kp0:kp0 + nk, kbase + t * 128:kbase + (t + 1) * 128],
                                                        in_=pb[:, nq * 128:npair * 128].rearrange("p (c k) -> p c k", c=nk)),
                                  reads=[bt], writes=[kT])
                    PQ.append(qk_T)
                else:
                    nh = ncols // 64
                    h0 = (c0 - 1536) // 64
                    vv = Vp.ap[:, gt, :].rearrange("p (h e) -> p h e", e=65)
                    cx.op("act", lambda: A.copy(out=vv[:, h0:h0 + nh, 0:64], in_=bk.ap[:, 0:ncols].rearrange("p (h d) -> p h d", h=nh)),
                          reads=[bk], writes=[Vp])

            Pool.lo, Pool.hi, Pool.nxt = 0, 6, 0
            E3 = Eb.ap[:, :]
            bS = [ps[6], ps[7]]
            h3 = hstate.ap[:, :].rearrange("p (h d) -> p h d", h=H)
            pend_T = None
            bE = psget(1)[0]
            for t in range(NT):
                for i, cm in enumerate((cU, cSL, cones)):
                    cx.op("pe", lambda: T.matmul(bE.ap[:, t * 36 + i * H:t * 36 + (i + 1) * H], lhsT=cm.ap[:, :], rhs=dA3[:, t, :],
                                                 start=True, stop=True), reads=[cm, dAb], writes=[bE], sig=(i == 2 and t == NT - 1))
            cx.op("act", lambda: A.activation(out=Eb.ap[:, :], in_=bE.ap[:, 0:NT * 36], func=AF.Exp), reads=[bE], writes=[Eb])
            cx.op("dve", lambda: V.tensor_scalar(out=nac.ap[:, :].rearrange("p (t h) -> p t h", t=NT),
                                                 in0=bE.ap[:, 0:NT * 36].rearrange("p (t k) -> p t k", t=NT)[:, :, 0:H],
                                                 scalar1=-1.0, scalar2=None, op0=ALU.mult), reads=[bE], writes=[nac])
            for t in range(NT):
                tc = slice(t * 128, (t + 1) * 128)
                E3 = Eb.ap[:, t * 36:(t + 1) * 36]
                nact = nac.ap[:, t * H:(t + 1) * H]
                cx.op("dve", lambda: V.tensor_tensor(out=ddb.ap[:, :], in0=dt3[:, t, :], in1=E3[:, H:2 * H], op=ALU.mult),
                      reads=[dtv, Eb], writes=[ddb])
                xs_h = xs4[:, t, :].rearrange("p (h d) -> p h d", h=H)
                cx.op("pool", lambda: G.tensor_tensor(out=xdt.ap[:, :].rearrange("p (h d) -> p h d", h=H), in0=xs_h,
                                                      in1=dt3[:, t, :].unsqueeze(2).to_broadcast([128, H, HD]), op=ALU.mult),
                      reads=[xs_tok, dtv], writes=[xdt])
                cx.op("pool", lambda: G.tensor_tensor(out=xdtd.ap[:, :].rearrange("p (h d) -> p h d", h=H), in0=xs_h,
                                                      in1=ddb.ap[:, :].unsqueeze(2).to_broadcast([128, H, HD]), op=ALU.mult),
                      reads=[xs_tok, ddb], writes=[xdtd])
                yr = yrow[t % 2]
                banks = {}

                def stage1(g):
                    bX = psget(1)[0]
                    banks[g] = bX
                    lt = LT[g]
                    mt = MT[g]
                    lt3 = lt.ap[:, :].rearrange("p (j k) -> p j k", j=3)
                    mt3 = mt.ap[:, :].rearrange("p (j k) -> p j k", j=3)
                    cx.op("pe", lambda: T.matmul(bX.ap[:, 384:512], lhsT=BT3[:, g, tc], rhs=CT3[:, g, tc], start=True, stop=True),
                          reads=[BTb, CTb], writes=[bX], sig=False)
                    for j in range(3):
                        hh = 3 * g + j
                        cx.op("pe", lambda: T.matmul(bX.ap[:, j * 128:(j + 1) * 128], lhsT=hi3[:, t, hh:hh + 1].to_broadcast([128, 128]),
                                                     rhs=cUb.ap[:, :], start=True, stop=False), reads=[dAhi, cUb], writes=[bX], sig=False)
                        cx.op("pe", lambda: T.matmul(bX.ap[:, j * 128:(j + 1) * 128], lhsT=lo3[:, t, hh:hh + 1].to_broadcast([128, 128]),
                                                     rhs=cUb.ap[:, :], start=False, stop=False), reads=[dAlo, cUb], writes=[bX], sig=False)
                        cx.op("pe", lambda: T.matmul(bX.ap[:, j * 128:(j + 1) * 128], lhsT=identb.ap[:, :], rhs=cnegb.ap[:, :],
                                                     start=False, stop=True), reads=[identb, cnegb], writes=[bX], sig=(j == 2))
                    for j in range(3):
                        hh = 3 * g + j
                        cx.op("act", lambda: A.activation(out=lt3[:, j, :], in_=bX.ap[:, j * 128:(j + 1) * 128], func=AF.Exp,
                                                          bias=nact[:, hh:hh + 1]), reads=[bX, nac], writes=[lt])
                    cx.op("dve", lambda: V.tensor_tensor(out=mt3, in0=lt3, in1=bX.ap[:, 384:512].unsqueeze(1).to_broadcast([128, 3, 128]),
                                                         op=ALU.mult), reads=[lt, bX], writes=[mt])

                def stage2(g):
                    bY = psget(1)[0]
                    mt = MT[g]
                    yt_ = ytmp[g % 2]
                    mt3 = mt.ap[:, :].rearrange("p (j k) -> p j k", j=3)
                    for j in range(3):
                        hh = 3 * g + j
                        cx.op("pe", lambda: T.matmul(bY.ap[:, j * 64:(j + 1) * 64], lhsT=mt3[:, j, :],
                                                     rhs=xdt.ap[:, hh * 64:(hh + 1) * 64], start=True, stop=True),
                              reads=[mt, xdt], writes=[bY], sig=False)
                    cx.op("pe", lambda: T.matmul(bY.ap[:, 192:384], lhsT=CT3[:, g, tc], rhs=prevb.ap[:, g * 192:(g + 1) * 192],
                                                 start=True, stop=True), reads=[CTb, prevb], writes=[bY])
                    bSg = bS[g // 2]
                    cx.op("pe", lambda: T.matmul(bSg.ap[:, (g % 2) * 192:(g % 2) * 192 + 192], lhsT=Bt4[:, t, g * 128:(g + 1) * 128],
                                                 rhs=xdtd.ap[:, g * 192:(g + 1) * 192], start=True, stop=True),
                          reads=[B_tok, xdtd], writes=[bSg])
                    yg = yr.ap[:, g * 192:(g + 1) * 192].rearrange("p (j d) -> p j d", j=3)
                    xg = xs4[:, t, g * 192:(g + 1) * 192].rearrange("p (j d) -> p j d", j=3)
                    cx.op("pool", lambda: G.tensor_tensor(out=yg, in0=xg, in1=dsk.ap[:, 3 * g:3 * g + 3].unsqueeze(2).to_broadcast([128, 3, HD]),
                                                          op=ALU.mult), reads=[xs_tok, dsk], writes=[yr])
                    cx.op("dve", lambda: V.tensor_tensor(out=yt_.ap[:, :].rearrange("p (j d) -> p j d", j=3),
                                                         in0=bY.ap[:, 192:384].rearrange("p (j d) -> p j d", j=3),
                                                         in1=E3[:, 3 * g:3 * g + 3].unsqueeze(2).to_broadcast([128, 3, HD]), op=ALU.mult),
                          reads=[bY, Eb], writes=[yt_])
                    cx.op("dve", lambda: V.tensor_tensor(out=yr.ap[:, g * 192:(g + 1) * 192], in0=yr.ap[:, g * 192:(g + 1) * 192],
                                                         in1=yt_.ap[:, :], op=ALU.add), reads=[yr, yt_], writes=[yr])
                    cx.op("dve", lambda: V.tensor_tensor(out=yr.ap[:, g * 192:(g + 1) * 192], in0=yr.ap[:, g * 192:(g + 1) * 192],
                                                         in1=bY.ap[:, 0:192], op=ALU.add), reads=[yr, bY], writes=[yr])

                for g in range(NG):
                    stage1(g)
                run_qkv()
                run_qkv()
                if pend_T is not None:
                    pend_T()
                    pend_T = None
                for g in range(NG):
                    stage2(g)
                run_qkv()
                run_qkv()
                run_qkv()
                if "ypre" in dbg and t == 0:
                    dump("ypre", yr, [128, 768])
                cx.op("dve", lambda: V.tensor_tensor(out=yr.ap[:, :], in0=yr.ap[:, :], in1=z3[:, t, :], op=ALU.mult),
                      reads=[yr, zs], writes=[yr])
                dT = rms_norm(yr, snw, t % 2)
                pend_T = (lambda f, tt: (lambda: f(tt, 6)))(dT, t)
                cx.op("pool", lambda: G.tensor_tensor(out=h3, in0=h3, in1=E3[:, 2 * H:3 * H].unsqueeze(2).to_broadcast([128, H, HD]),
                                                      op=ALU.mult), reads=[hstate, Eb], writes=[hstate])
                for i in range(2):
                    cx.op("dve", lambda: V.tensor_tensor(out=hstate.ap[:, i * 384:(i + 1) * 384], in0=hstate.ap[:, i * 384:(i + 1) * 384],
                                                         in1=bS[i].ap[:, 0:384], op=ALU.add), reads=[hstate, bS[i]], writes=[hstate])
                cx.op("act", lambda: A.copy(out=prevb.ap[:, :], in_=hstate.ap[:, :]), reads=[hstate], writes=[prevb])
            Pool.lo, Pool.hi, Pool.nxt = 0, 8, 0
            chk("mix_ssd")
            cx.dma("sp", anw.ap[:, :], anw_d[0:1, :].partition_broadcast(128), anw.chan, writes=[anw])
            while qkv_steps:
                run_qkv()
            if pend_T is not None:
                pend_T()
                pend_T = None
            while PQ:
                PQ.pop(0)()
            chk("mix_qkv")
            dump("qT", qT, [128, 6 * 512], BF16)

            Pool.lo, Pool.hi, Pool.nxt = 0, 4, 0
            Pool.split = True
            accA, accB = ps[6], ps[7]
            mk3 = masks.ap
            wo3 = wout_b.ap[:, :].rearrange("p (c n) -> p c n", c=12)
            NPT = len(PT)
            LAG = NPT // 2 - 2

            def stage_a(t):
                gt = q * NT + t
                ngrp = gt // 4 + 1
                items = []
                for pair in range(6):
                    for g in range(ngrp):
                        bmax = min(4 * g + 3, gt)
                        nb = bmax - 4 * g + 1
                        js = [gt - b for b in range(bmax, 4 * g - 1, -1)]
                        items.append((pair, g, nb, js))
                stash = {}

                def emit_qk(i):
                    pair, g, nb, js = items[i]
                    bks = [psget(1)[0], psget(1)[0]]
                    for idx, j in enumerate(js):
                        for hb in range(2):
                            bp = hb * 64
                            bk = bks[hb]
                            cx.op("pe", lambda: T.matmul(bk.ap[:, idx * 128:(idx + 1) * 128], lhsT=kT.ap[bp:bp + 64, pair, j * 128:(j + 1) * 128],
                                                         rhs=q3[bp:bp + 64, pair, t * 128:(t + 1) * 128], start=True, stop=True),
                                  reads=[kT, qT], writes=[bk], sig=(idx == nb - 1))
                    ms = MSTART[min(g, 2)]
                    pts = []
                    for hb in range(2):
                        pt = PT[(2 * i + hb) % NPT]
                        bk = bks[hb]
                        cx.op("act", lambda: A.activation(out=pt.ap[:, 0:nb * 128], in_=bk.ap[:, 0:nb * 128], func=AF.Exp, scale=0.125),
                              reads=[bk], writes=[pt])
                        cx.op("dve", lambda: V.tensor_tensor(out=pt.ap[:, 0:nb * 128], in0=pt.ap[:, 0:nb * 128],
                                                             in1=mk3[:, (ms + 4 - nb) * 128:(ms + 4) * 128], op=ALU.mult),
                              reads=[pt, masks], writes=[pt])
                        pts.append(pt)
                    stash[i] = pts

                def emit_pv(i):
                    pair, g, nb, js = items[i]
                    pts = stash.pop(i)
                    for hb in range(2):
                        hh = 2 * pair + hb
                        pt = pts[hb]
                        acc = accA if hb == 0 else accB
                        col = pair * 65
                        for idx, j in enumerate(js):
                            first = (g == 0 and idx == 0)
                            last = (g == ngrp - 1 and idx == nb - 1)
                            cx.op("pe", lambda: T.matmul(acc.ap[:, col:col + 65], lhsT=pt.ap[:, idx * 128:(idx + 1) * 128],
                                                         rhs=Vp.ap[:, j, hh * 65:(hh + 1) * 65], start=first, stop=last),
                                  reads=[pt, Vp], writes=[acc], sig=(idx == nb - 1))

                n = len(items)
                for i in range(n + LAG):
                    if i < n:
                        emit_qk(i)
                    if i - LAG >= 0:
                        emit_pv(i - LAG)
                ar = arow[t % 2]
                for i, acc in enumerate((accA, accB)):
                    a3_ = acc.ap[:, 0:390].rearrange("p (h e) -> p h e", e=65)
                    cx.op("dve", lambda: V.reciprocal(out=st_rd.ap[:, i * 6:(i + 1) * 6], in_=a3_[:, :, 64]), reads=[acc], writes=[st_rd])
                    cx.op("dve", lambda: V.tensor_tensor(out=ar.ap[:, :].rearrange("p (c b d) -> p c b d", c=6, b=2)[:, :, i, :],
                                                         in0=a3_[:, :, 0:64],
                                                         in1=st_rd.ap[:, i * 6:(i + 1) * 6].unsqueeze(2).to_broadcast([128, 6, HD]),
                                                         op=ALU.mult), reads=[acc, st_rd], writes=[ar])
                if "arow" in dbg and t == 0:
                    dump("arow", ar, [128, 768])
                return rms_norm(ar, anw, t % 2)

            def stage_b(t, dT):
                dT(t, 0)
                b0, b1 = auxget(1)[0], auxget(1)[0]
                for half, bk in ((0, b0), (1, b1)):
                    for c in range(12):
                        cx.op("pe", lambda: T.matmul(bk.ap[:, :], lhsT=m3[:, c, t * 128:(t + 1) * 128], rhs=wo3[:, c, half * 512:(half + 1) * 512],
                                                     start=(c == 0), stop=(c == 11)), reads=[mixT, wout_b], writes=[bk], sig=(c == 11))
                h = hA[t]
                for half, bk in ((0, b0), (1, b1)):
                    cx.op("dve", lambda: V.tensor_tensor(out=h.ap[:, half * 512:(half + 1) * 512], in0=h.ap[:, half * 512:(half + 1) * 512],
                                                         in1=bk.ap[:, :], op=ALU.add), reads=[h, bk], writes=[h])
                if "mixres" in dbg and t == 0:
                    dump("mixres", h, [128, D])
                layer_norm(t, False, lnexp=True)

            dTs = {}
            for step in range(NT + 2):
                if step < NT:
                    dTs[step] = stage_a(step)
                    chk("mix_attn")
                if 0 <= step - 2 < NT:
                    if step - 2 == NT - 1 and LAST_HT is not None:
                        LAST_HT.append(lambda: make_hT(NT - 1))
                    else:
                        make_hT(step - 2)
                if 0 <= step - 1 < NT:
                    stage_b(step - 1, dTs.pop(step - 1))
            Pool.lo, Pool.hi, Pool.nxt = 0, 8, 0
            Pool.split = False

        nunits = nseq * UPS if nunits_override is None else nunits_override
        if STOP == "prologue" or (STOP and (STOP == "const" or STOP.startswith("cast"))):
            nunits = 0
        NUNITS[0] = nunits
        for u in range(nunits):
            tok0 = u * TU
            if u % UPS == 0:
                rotary_tables(u // UPS)
                if u == 0:
                    dump("cost", cost, [128, 16, 8])
                    dump("sint", sint, [128, 16, 8])
            if u == 0:
                for t in range(NT):
                    load_x(0, t)
                    x_to_hT(t)
            for t in range(NT):
                x_to_hA(t)
            if STOP == "x":
                break
            ffn_phase(0, u, False)
            if u == 0:
                dump("h1", (hA_t[:, :, :], [hh_ for hh_ in hA]), [128, NT, D])
            if STOP == "ffn1":
                break
            last_ht = []
            try:
                mixer_phase(u, LAST_HT=last_ht)
            except _StopNow:
                Pool.lo, Pool.hi, Pool.nxt = 0, 8, 0
                Pool.split = False
                break
            if u == 0:
                dump("h2", (hA_t[:, :, :], [hh_ for hh_ in hA]), [128, NT, D])
            if STOP == "mixer":
                for f_ in last_ht:
                    f_()
                break
            ffn_phase(1, u, True, pre_final=(last_ht[0] if last_ht else None))

        cx.wait_tokens("act", [(h.chan2.sem, 16 * h.chan2.n) for h in hA])
        cx.wait_tokens("sp", [(c.sem, 16 * c.n) for c in cx.chans if c.n > 0])
    return nc, sorted(dbg_out)


def _consts():
    s = np.arange(128)[:, None]
    t = np.arange(128)[None, :]

    def cb(b):
        delta = 128 * b + t - s
        c = ((delta <= 128).astype(np.float32) + ((delta % 4 == 0) & (delta <= 512)).astype(np.float32)
             + (delta % 16 == 0).astype(np.float32))
        return c * (delta >= 0)

    masks = np.zeros((128, 9 * 128), np.float32)
    for i, b in enumerate((8, 7, 6, 5, 4, 3, 2, 1, 0)):
        masks[:, i * 128:(i + 1) * 128] = cb(b)
    k_ = np.arange(128)[:, None]
    l_ = np.arange(128)[None, :]
    U = (k_ <= l_).astype(np.float32)
    SL = (k_ > l_).astype(np.float32)
    neg = np.where(l_ >= k_, 0.0, NEG).astype(np.float32)
    invf = (np.float32(500000.0) ** (-np.arange(0, 16, 2, dtype=np.float32) / np.float32(16))).astype(np.float32)
    return {
        "c_masks": masks.astype(ml_dtypes.bfloat16),
        "c_identb": np.eye(128, dtype=np.float32).astype(ml_dtypes.bfloat16),
        "c_identf": np.eye(128, dtype=np.float32),
        "c_U": U, "c_SL": SL, "c_ones": np.ones((128, 128), np.float32), "c_negb": neg.astype(ml_dtypes.bfloat16),
        "c_invf": invf.reshape(1, 8),
    }


def make_in_maps(inputs, ncore, nseq):
    f = lambda a: np.ascontiguousarray(np.asarray(a, dtype=np.float32))
    shared = {
        "wg1": f(inputs["ffn1_gate"][0]), "wu1": f(inputs["ffn1_up"][0]), "wd1": f(inputs["ffn1_down"][0]),
        "wg2": f(inputs["ffn2_gate"][0]), "wu2": f(inputs["ffn2_up"][0]), "wd2": f(inputs["ffn2_down"][0]),
        "w_in": f(inputs["w_in"][0]), "w_out": f(inputs["w_out"][0]),
        "ln1_g": f(inputs["ln1_g"]), "ln1_b": f(inputs["ln1_b"]),
        "ln2_g": f(inputs["ln2_g"]), "ln2_b": f(inputs["ln2_b"]),
        "ln3_g": f(inputs["ln3_g"]), "ln3_b": f(inputs["ln3_b"]),
        "conv_w_l": f(np.asarray(inputs["conv_w"][0]).T.reshape(14, 128, 4).transpose(1, 0, 2)),
        "conv_b_l": f(np.asarray(inputs["conv_b"][0]).reshape(14, 128).T),
        "dt_bias": f(inputs["dt_bias"]), "a_log": f(inputs["a_log"]), "d_skip": f(inputs["d_skip"]),
        "attn_norm_w": f(inputs["attn_norm_w"]), "ssd_norm_w": f(inputs["ssd_norm_w"]),
    }
    shared.update(_consts())
    x = np.asarray(inputs["x"], dtype=np.float32)
    pos = np.asarray(inputs["positions"]).astype(np.int32)
    maps = []
    for c in range(ncore):
        m = dict(shared)
        m["x"] = np.ascontiguousarray(x[c * nseq:(c + 1) * nseq].reshape(nseq * SEQ, D))
        m["pos"] = np.ascontiguousarray(pos[c * nseq:(c + 1) * nseq].reshape(nseq * 16, 128))
        maps.append(m)
    return maps


_CACHE = {}


def kernel(**inputs):
    x = np.asarray(inputs["x"])
    B = x.shape[0]
    nseq = B // NCORE
    if nseq not in _CACHE:
        _CACHE[nseq] = build(nseq)[0]
    nc = _CACHE[nseq]
    maps = make_in_maps(inputs, NCORE, nseq)
    res = run_bass_kernel_spmd(nc, maps, core_ids=list(range(NCORE)))
    outs = [np.asarray(r["out"], dtype=np.float32).reshape(nseq, SEQ, D) for r in res.results]
    return np.concatenate(outs, axis=0)
```
